# Optimizing a Trainium2 kernel written in Bass

```python
import jax, jax.numpy as jnp
from jax import lax
import numpy as np

D_MODEL = 1024
BATCH = 8
SEQ = 2048
DEPTH = 2

CTX_LEN = 256
GRID_W = 64
ROPE_BASE = 10000.0
NORM_EPS = 1e-6
NEG_INF = -1e30

A_HEADS = 8
A_KV_HEADS = 2
A_HEAD_DIM = 64
A_WIDTH = A_HEADS * A_HEAD_DIM
A_KV_WIDTH = A_KV_HEADS * A_HEAD_DIM
WINDOW = 128
A_BLOCK = 128
B_GROUPS = 8
B_WIDTH = D_MODEL // 2
B_GROUP_DIM = B_WIDTH // B_GROUPS
CHUNK = 128
EVEN_MIX = A_WIDTH + B_WIDTH
EVEN_SPLITS = [A_WIDTH, A_WIDTH + A_KV_WIDTH, A_WIDTH + 2 * A_KV_WIDTH,
               A_WIDTH + 2 * A_KV_WIDTH + B_WIDTH, A_WIDTH + 2 * A_KV_WIDTH + 2 * B_WIDTH]
EVEN_IN = A_WIDTH + 2 * A_KV_WIDTH + 2 * B_WIDTH + EVEN_MIX

C_HEADS = 16
C_NOPE = 64
C_ROPE = 32
C_V = 64
Q_RANK = 256
KV_RANK = 128
C_WIDTH = C_HEADS * C_V
ODD_SPLITS = [Q_RANK, Q_RANK + KV_RANK, Q_RANK + KV_RANK + C_ROPE]
ODD_IN = Q_RANK + KV_RANK + C_ROPE + C_WIDTH
Q_BLOCK = 128

N_EVEN = (DEPTH + 1) // 2
N_ODD = DEPTH // 2

kernel_name = 'hybrid_dit_window_gmlp_mla'


def rms_norm(x, g):
    xf = x.astype(jnp.float32)
    y = xf * lax.rsqrt(jnp.mean(xf * xf, axis=-1, keepdims=True) + NORM_EPS)
    return (y * g.astype(jnp.float32)).astype(x.dtype)


def layer_norm(x, g, b):
    xf = x.astype(jnp.float32)
    mu = jnp.mean(xf, axis=-1, keepdims=True)
    xc = xf - mu
    y = xc * lax.rsqrt(jnp.mean(xc * xc, axis=-1, keepdims=True) + NORM_EPS)
    return (y * g.astype(jnp.float32) + b.astype(jnp.float32)).astype(x.dtype)


def grid_positions(n):
    rows = n // GRID_W
    row = jnp.broadcast_to(jnp.arange(rows, dtype=jnp.float32)[:, None], (rows, GRID_W)).reshape(-1)
    col = jnp.broadcast_to(jnp.arange(GRID_W, dtype=jnp.float32)[None, :], (rows, GRID_W)).reshape(-1)
    return row, col


def rope_1d(x, pos):
    half = x.shape[-1] // 2
    inv = ROPE_BASE ** (-jnp.arange(half, dtype=jnp.float32) / half)
    ang = pos[:, None] * inv[None, :]
    cos = jnp.cos(ang)[None, :, None, :]
    sin = jnp.sin(ang)[None, :, None, :]
    x1, x2 = x[..., :half], x[..., half:]
    return jnp.concatenate([x1 * cos - x2 * sin, x1 * sin + x2 * cos], axis=-1)


def rope_2d(x, row, col):
    d = x.shape[-1] // 2
    xf = x.astype(jnp.float32)
    out = jnp.concatenate([rope_1d(xf[..., :d], row), rope_1d(xf[..., d:], col)], axis=-1)
    return out.astype(x.dtype)


def windowed_sink_gqa(q, k, v, kc, vc, sink):
    B_, S, Hq, Dh = q.shape
    Hkv = k.shape[2]
    G = Hq // Hkv
    blk = A_BLOCK
    nb = S // blk
    C = kc.shape[1]
    scale = Dh ** -0.5
    qb = q.reshape(B_, nb, blk, Hkv, G, Dh)
    pad = ((0, 0), (blk, blk), (0, 0), (0, 0))
    kp = jnp.pad(k, pad).reshape(B_, nb + 2, blk, Hkv, Dh)
    vp = jnp.pad(v, pad).reshape(B_, nb + 2, blk, Hkv, Dh)
    kb = jnp.concatenate([kp[:, :-2], kp[:, 1:-1], kp[:, 2:]], axis=2)
    vb = jnp.concatenate([vp[:, :-2], vp[:, 1:-1], vp[:, 2:]], axis=2)
    qi = jnp.arange(nb)[:, None, None] * blk + jnp.arange(blk)[None, :, None]
    kj = jnp.arange(nb)[:, None, None] * blk - blk + jnp.arange(3 * blk)[None, None, :]
    valid = (jnp.abs(qi - kj) <= WINDOW) & (kj >= 0) & (kj < S)
    s_loc = jnp.einsum('bnqhgd,bnkhd->bhgnqk', qb, kb).astype(jnp.float32) * scale
    s_loc = jnp.where(valid, s_loc, NEG_INF)
    s_ctx = jnp.einsum('bnqhgd,bchd->bhgnqc', qb, kc).astype(jnp.float32) * scale
    s_sink = jnp.broadcast_to(sink.astype(jnp.float32).reshape(1, Hkv, G, 1, 1, 1), s_loc.shape[:-1] + (1,))
    p = jax.nn.softmax(jnp.concatenate([s_loc, s_ctx, s_sink], axis=-1), axis=-1).astype(v.dtype)
    nk = 3 * blk
    o = (jnp.einsum('bhgnqk,bnkhd->bnqhgd', p[..., :nk], vb)
         + jnp.einsum('bhgnqc,bchd->bnqhgd', p[..., nk:nk + C], vc))
    return o.reshape(B_, S, Hq * Dh)


def context_sink_gqa(qc, kc, vc, sink):
    B_, C, Hq, Dh = qc.shape
    Hkv = kc.shape[2]
    G = Hq // Hkv
    scale = Dh ** -0.5
    qg = qc.reshape(B_, C, Hkv, G, Dh)
    s = jnp.einsum('bqhgd,bkhd->bhgqk', qg, kc).astype(jnp.float32) * scale
    s_sink = jnp.broadcast_to(sink.astype(jnp.float32).reshape(1, Hkv, G, 1, 1), s.shape[:-1] + (1,))
    p = jax.nn.softmax(jnp.concatenate([s, s_sink], axis=-1), axis=-1)[..., :C].astype(vc.dtype)
    return jnp.einsum('bhgqk,bkhd->bqhgd', p, vc).reshape(B_, C, Hq * Dh)


def chunk_gmlp(u, v, ln_g, ln_b, w_s, b_s):
    B_, L, _ = v.shape
    n = L // CHUNK
    vn = layer_norm(v, ln_g, ln_b).reshape(B_, n, CHUNK, B_GROUPS, B_GROUP_DIM)
    s = jnp.einsum('gpq,bnqgc->bnpgc', w_s, vn) + jnp.transpose(b_s)[None, None, :, :, None]
    return u * s.reshape(B_, L, B_WIDTH)


def even_layer(hl, hc, w_in, sink, ln_g, ln_b, w_s, b_s, w_out, row, col, need_ctx):
    B_, S, _ = hl.shape
    C = hc.shape[1]
    gelu = lambda t: jax.nn.gelu(t, approximate=False)
    q, k, v, u, vv, gate = jnp.split(hl @ w_in, EVEN_SPLITS, axis=-1)
    q = rope_2d(q.reshape(B_, S, A_HEADS, A_HEAD_DIM), row, col)
    k = rope_2d(k.reshape(B_, S, A_KV_HEADS, A_HEAD_DIM), row, col)
    v = v.reshape(B_, S, A_KV_HEADS, A_HEAD_DIM)
    if need_ctx:
        qc, kc, vc, uc, vvc, gate_c = jnp.split(hc @ w_in, EVEN_SPLITS, axis=-1)
    else:
        kc, vc = jnp.split(hc @ w_in[:, A_WIDTH:A_WIDTH + 2 * A_KV_WIDTH], 2, axis=-1)
    kc = kc.reshape(B_, C, A_KV_HEADS, A_HEAD_DIM)
    vc = vc.reshape(B_, C, A_KV_HEADS, A_HEAD_DIM)
    a = windowed_sink_gqa(q, k, v, kc, vc, sink)
    b = chunk_gmlp(gelu(u), gelu(vv), ln_g, ln_b, w_s, b_s)
    y = (jnp.concatenate([a, b], axis=-1) * jax.nn.silu(gate)) @ w_out
    if need_ctx:
        ac = context_sink_gqa(qc.reshape(B_, C, A_HEADS, A_HEAD_DIM), kc, vc, sink)
        bc = chunk_gmlp(gelu(uc), gelu(vvc), ln_g, ln_b, w_s, b_s)
        yc = (jnp.concatenate([ac, bc], axis=-1) * jax.nn.silu(gate_c)) @ w_out
    else:
        yc = None
    return y, yc


def mla_queries(q_a, q_norm, w_qb, row, col):
    B_, L, _ = q_a.shape
    q = (rms_norm(q_a, q_norm) @ w_qb).reshape(B_, L, C_HEADS, C_NOPE + C_ROPE)
    q_nope, q_pe = q[..., :C_NOPE], q[..., C_NOPE:]
    if row is not None:
        q_pe = rope_2d(q_pe, row, col)
    return jnp.concatenate([q_nope, q_pe], axis=-1)


def mla_keys_values(kv_a, k_pe, kv_norm, w_kvb, row, col):
    B_, L, _ = kv_a.shape
    kv = (rms_norm(kv_a, kv_norm) @ w_kvb).reshape(B_, L, C_HEADS, C_NOPE + C_V)
    k_nope, v = kv[..., :C_NOPE], kv[..., C_NOPE:]
    k_pe = k_pe[:, :, None, :]
    if row is not None:
        k_pe = rope_2d(k_pe, row, col)
    k = jnp.concatenate([k_nope, jnp.broadcast_to(k_pe, (B_, L, C_HEADS, C_ROPE))], axis=-1)
    return k, v


def block_attention(q, k, v):
    B_, Lq, H, dk = q.shape
    nb = Lq // Q_BLOCK
    scale = dk ** -0.5
    qb = jnp.moveaxis(q.reshape(B_, nb, Q_BLOCK, H, dk), 1, 0)

    def one(qblk):
        s = jnp.einsum('bqhd,bkhd->bhqk', qblk, k).astype(jnp.float32) * scale
        p = jax.nn.softmax(s, axis=-1).astype(v.dtype)
        return jnp.einsum('bhqk,bkhd->bqhd', p, v)

    o = lax.map(one, qb)
    return jnp.moveaxis(o, 0, 1).reshape(B_, Lq, H * v.shape[-1])


def odd_layer(hl, hc, w_in, q_norm, w_qb, kv_norm, w_kvb, w_out, row, col, need_ctx):
    q_a, kv_a, k_pe, gate = jnp.split(hl @ w_in, ODD_SPLITS, axis=-1)
    q = mla_queries(q_a, q_norm, w_qb, row, col)
    k, v = mla_keys_values(kv_a, k_pe, kv_norm, w_kvb, row, col)
    if need_ctx:
        q_ac, kv_ac, k_pec, gate_c = jnp.split(hc @ w_in, ODD_SPLITS, axis=-1)
    else:
        kv_ac, k_pec = jnp.split(hc @ w_in[:, Q_RANK:Q_RANK + KV_RANK + C_ROPE], [KV_RANK], axis=-1)
    kc, vc = mla_keys_values(kv_ac, k_pec, kv_norm, w_kvb, None, None)
    o = block_attention(q, jnp.concatenate([k, kc], axis=1), jnp.concatenate([v, vc], axis=1))
    y = (o * jax.nn.silu(gate)) @ w_out
    if need_ctx:
        qc = mla_queries(q_ac, q_norm, w_qb, None, None)
        oc = block_attention(qc, kc, vc)
        yc = (oc * jax.nn.silu(gate_c)) @ w_out
    else:
        yc = None
    return y, yc


def setup_inputs(seed: int = 0) -> dict:
    key = jax.random.key(seed)
    ks = jax.random.split(key, 24)
    f32 = jnp.float32

    def nrm(k, shape, s=1.0):
        return jax.random.normal(k, shape, f32) * s

    def w(k, shape, fan_in, g=1.0):
        return nrm(k, shape, g * fan_in ** -0.5)

    def gain(k, shape):
        return 1.0 + nrm(k, shape, 0.05)

    return {
        'x': nrm(ks[0], (BATCH, SEQ, D_MODEL)),
        'c': nrm(ks[1], (BATCH, D_MODEL)),
        'ctx': nrm(ks[2], (BATCH, CTX_LEN, D_MODEL)),
        'c_ctx': nrm(ks[3], (D_MODEL,)),
        'norm_g': gain(ks[4], (DEPTH, D_MODEL)),
        'w_ada': w(ks[5], (DEPTH, D_MODEL, 3 * D_MODEL), D_MODEL, 0.5),
        'b_ada': nrm(ks[6], (DEPTH, 3 * D_MODEL), 0.02),
        'w_in0': w(ks[7], (N_EVEN, D_MODEL, EVEN_IN), D_MODEL),
        'sink0': nrm(ks[8], (N_EVEN, A_HEADS), 0.5),
        'gm_ln_g': gain(ks[9], (N_EVEN, B_WIDTH)),
        'gm_ln_b': nrm(ks[10], (N_EVEN, B_WIDTH), 0.02),
        'gm_ws': w(ks[11], (N_EVEN, B_GROUPS, CHUNK, CHUNK), CHUNK),
        'gm_bs': 1.0 + nrm(ks[12], (N_EVEN, B_GROUPS, CHUNK), 0.02),
        'w_out0': w(ks[13], (N_EVEN, EVEN_MIX, D_MODEL), EVEN_MIX),
        'w_in1': w(ks[14], (N_ODD, D_MODEL, ODD_IN), D_MODEL),
        'q_norm': gain(ks[15], (N_ODD, Q_RANK)),
        'w_qb': w(ks[16], (N_ODD, Q_RANK, C_HEADS * (C_NOPE + C_ROPE)), Q_RANK),
        'kv_norm': gain(ks[17], (N_ODD, KV_RANK)),
        'w_kvb': w(ks[18], (N_ODD, KV_RANK, C_HEADS * (C_NOPE + C_V)), KV_RANK),
        'w_out1': w(ks[19], (N_ODD, C_WIDTH, D_MODEL), C_WIDTH),
        'final_g': gain(ks[20], (D_MODEL,)),
    }


def reference(x, c, ctx, c_ctx, norm_g, w_ada, b_ada, w_in0, sink0, gm_ln_g, gm_ln_b, gm_ws, gm_bs,
              w_out0, w_in1, q_norm, w_qb, kv_norm, w_kvb, w_out1, final_g):
    S = x.shape[1]
    row, col = grid_positions(S)
    cond_l = jax.nn.silu(c)[:, None, :]
    cond_c = jax.nn.silu(c_ctx)[None, None, :]
    xc = ctx
    for layer in range(DEPTH):
        need_ctx = layer < DEPTH - 1
        shift_l, scale_l, gate_l = jnp.split(cond_l @ w_ada[layer] + b_ada[layer], 3, axis=-1)
        shift_c, scale_c, gate_c = jnp.split(cond_c @ w_ada[layer] + b_ada[layer], 3, axis=-1)
        hl = rms_norm(x, norm_g[layer]) * (1.0 + scale_l) + shift_l
        hc = rms_norm(xc, norm_g[layer]) * (1.0 + scale_c) + shift_c
        i = layer // 2
        if layer % 2 == 0:
            yl, yc = even_layer(hl, hc, w_in0[i], sink0[i], gm_ln_g[i], gm_ln_b[i], gm_ws[i], gm_bs[i],
                                w_out0[i], row, col, need_ctx)
        else:
            yl, yc = odd_layer(hl, hc, w_in1[i], q_norm[i], w_qb[i], kv_norm[i], w_kvb[i], w_out1[i],
                               row, col, need_ctx)
        x = x + gate_l * yl
        if need_ctx:
            xc = xc + gate_c * yc
    return rms_norm(x, final_g)
```

```python
import numpy as np
import concourse.bass as bass
import concourse.mybir as mybir
from concourse.bass_utils import run_bass_kernel_spmd

F32 = mybir.dt.float32
BF16 = mybir.dt.bfloat16
AF = mybir.ActivationFunctionType
ALU = mybir.AluOpType
AX = mybir.AxisListType

ENGS = ("pe", "act", "dve", "pool", "sp")
SAME_ENGINE_SYNC = True
N_DMA_SEMS = 6
SEM_K = 240
DMA_USES = 12
NOSYNC = ("pe",)


class Op:
    __slots__ = ("eng", "fn", "deps", "idx", "signaled", "is_dma", "dsem", "dval", "count", "prev_same_sem")

    def __init__(self, eng, fn, is_dma=False):
        self.eng = eng
        self.fn = fn
        self.deps = []
        self.idx = -1
        self.signaled = False
        self.is_dma = is_dma
        self.dsem = None
        self.dval = 0
        self.count = 0
        self.prev_same_sem = None


class Sched:
    def __init__(self, nc):
        self.nc = nc
        self.q = {e: [] for e in ENGS}
        self.last_w = {}
        self.readers = {}
        self.dma_count = {e: 0 for e in ENGS}
        self.dma_last = {}

    def _track(self, op, reads, writes):
        deps = []
        for r in reads:
            w = self.last_w.get(r)
            if w is not None:
                deps.append(w)
        for wkey in writes:
            w = self.last_w.get(wkey)
            if w is not None:
                deps.append(w)
            deps.extend(self.readers.get(wkey, ()))
        for r in reads:
            lst = self.readers.setdefault(r, [])
            if not op.is_dma:
                lst[:] = [o for o in lst if o.is_dma or o.eng != op.eng]
            lst.append(op)
        for wkey in writes:
            self.last_w[wkey] = op
            self.readers[wkey] = []
        best = {}
        keep = []
        seen = set()
        for d in deps:
            if d is op or id(d) in seen:
                continue
            seen.add(id(d))
            if d.is_dma:
                keep.append(d)
            else:
                b = best.get(d.eng)
                if b is None or d.idx > b.idx:
                    best[d.eng] = d
        op.deps.extend(keep)
        op.deps.extend(best.values())

    def add(self, eng, fn, reads=(), writes=(), extra=()):
        op = Op(eng, fn)
        op.idx = len(self.q[eng])
        self.q[eng].append(op)
        self._track(op, reads, writes)
        for d in extra:
            if d is not None and d not in op.deps:
                op.deps.append(d)
        return op

    def dma(self, eng, out, in_, reads=(), writes=(), extra=(), **kw):
        op = Op(eng, lambda e: e.dma_start(out=out, in_=in_, **kw), is_dma=True)
        op.idx = len(self.q[eng])
        self.q[eng].append(op)
        self._track(op, reads, writes)
        for d in extra:
            if d is not None and d not in op.deps:
                op.deps.append(d)
        j = self.dma_count[eng]
        self.dma_count[eng] += 1
        slot = j % N_DMA_SEMS
        ep = j // (N_DMA_SEMS * DMA_USES)
        op.dsem = (eng, ep, slot)
        op.dval = 16 * ((j % (N_DMA_SEMS * DMA_USES)) // N_DMA_SEMS + 1)
        op.prev_same_sem = self.dma_last.get((eng, slot))
        self.dma_last[(eng, slot)] = op
        return op

    def barrier(self):
        lasts = [self.q[e][-1] for e in ENGS if self.q[e]] + list(self.dma_last.values())
        for e in ENGS:
            self.add(e, None, extra=lasts)
        self.last_w = {}
        self.readers = {}

    def emit(self, block, stack):
        nc = self.nc
        csem = {}
        dsem = {}
        for e in ENGS:
            neps = (self.dma_count[e] + N_DMA_SEMS * DMA_USES - 1) // (N_DMA_SEMS * DMA_USES)
            for ep in range(neps):
                for s in range(N_DMA_SEMS):
                    dsem[(e, ep, s)] = stack.enter_context(nc.semaphore(f"d_{e}{ep}_{s}"))
        for e in ENGS:
            for op in self.q[e]:
                for d in op.deps:
                    if not d.is_dma:
                        if d.eng == op.eng and (not SAME_ENGINE_SYNC or d.eng in NOSYNC):
                            continue
                        d.signaled = True
        for e in ENGS:
            c = 0
            for op in self.q[e]:
                if op.signaled and not op.is_dma:
                    c += 1
                op.count = c
            for ep in range((c + SEM_K - 1) // SEM_K):
                csem[(e, ep)] = stack.enter_context(nc.semaphore(f"c_{e}{ep}"))
        self.nsems = len(csem) + len(dsem)
        stats = {e: [0, 0] for e in ENGS}

        def body_for(e):
            def body(eng):
                waited = {}
                for op in self.q[e]:
                    waits = {}
                    for d in op.deps:
                        if d.is_dma:
                            key = ("d",) + d.dsem
                            sem = dsem[d.dsem]
                            val = d.dval
                        else:
                            if d.eng == e and (not SAME_ENGINE_SYNC or e in NOSYNC):
                                continue
                            if d.count == 0:
                                continue
                            dep_ = (d.count - 1) // SEM_K
                            key = ("c", d.eng, dep_)
                            sem = csem[(d.eng, dep_)]
                            val = (d.count - 1) % SEM_K + 1
                        if waits.get(key, (None, 0))[1] < val:
                            waits[key] = (sem, val)
                    if op.is_dma and op.prev_same_sem is not None:
                        p = op.prev_same_sem
                        key = ("d",) + p.dsem
                        if waits.get(key, (None, 0))[1] < p.dval:
                            waits[key] = (dsem[p.dsem], p.dval)
                    for key, (sem, val) in waits.items():
                        if waited.get(key, 0) >= val:
                            continue
                        waited[key] = val
                        eng.wait_ge(sem, val)
                        stats[e][1] += 1
                    mysem = csem[(e, (op.count - 1) // SEM_K)] if (op.signaled and not op.is_dma) else None
                    if op.fn is None:
                        if op.signaled:
                            eng.nop(nofuse=True).then_inc(mysem, 1)
                        continue
                    ins = op.fn(eng)
                    stats[e][0] += 1
                    if op.is_dma:
                        ins.then_inc(dsem[op.dsem], 16)
                    elif op.signaled:
                        ins.then_inc(mysem, 1)
            return body

        block.tensor(body_for("pe"))
        block.scalar(body_for("act"))
        block.vector(body_for("dve"))
        block.gpsimd(body_for("pool"))
        block.sync(body_for("sp"))
        self.stats = stats

import contextlib

EPS = 1e-6
NT_L = 16
NT = 18
W0C = 3456
W1C = 1600
NDUMMY = 0


def build(nlayers=2):
    nc = bass.Bass("TRN2", target_bir_lowering=False)

    def din(name, shape):
        return nc.dram_tensor(name, list(shape), F32, kind="ExternalInput").ap()

    x_d = din("x", [2048, 1024]); ctx_d = din("ctx", [256, 1024]); cc_d = din("cc", [128, 8, 2])
    wada_d = din("w_ada", [2, 1024, 3072]); bada_d = din("b_ada", [2, 3072]); badac_d = din("b_ada_col", [128, 2, 16])
    ng_d = din("norm_g_col", [128, 2, 8]); fg_d = din("final_g", [1, 1024])
    w0_d = din("W0", [1024, W0C]); wout0_d = din("wout0", [1024, 1024])
    sink_d = din("sink_rep", [1, 2, 512]); lng_d = din("ln_g", [1, 512]); lnb_d = din("ln_b", [1, 512])
    wst_d = din("WsT", [128, 8, 128]); bs_d = din("bs", [8, 128]); bones_d = din("blockones", [8, 512])
    mask_d = din("masks", [128, 2, 512]); cos0_d = din("cos0", [128, 2048]); sin0_d = din("sin0", [128, 2048])
    w1_d = din("W1", [1024, W1C]); qn_d = din("qn_col", [128, 2]); kvn_d = din("kvn_col", [128, 1])
    wqb_d = din("Wqb", [256, 3072]); wkvb_d = din("Wkvb", [128, 2048]); wout1_d = din("wout1", [1024, 1024])
    cos1_d = din("cos1", [128, 2048]); sin1_d = din("sin1", [128, 2048])
    if nlayers == 2:
        out_d = nc.dram_tensor("out", [2048, 1024], F32, kind="ExternalOutput").ap()
        xs_d = nc.dram_tensor("xs", [2048, 1024], F32, kind="Internal").ap()
        xcs_d = nc.dram_tensor("xcs", [256, 1024], F32, kind="Internal").ap()
    else:
        xs_d = nc.dram_tensor("xs", [2048, 1024], F32, kind="ExternalOutput").ap()
        xcs_d = nc.dram_tensor("xcs", [256, 1024], F32, kind="ExternalOutput").ap()

    S = Sched(nc)
    with contextlib.ExitStack() as st:
        def sb(name, shape, dt):
            return st.enter_context(nc.sbuf_tensor("s_" + name, list(shape), dt))

        def ps(name, shape, dt):
            return st.enter_context(nc.psum_tensor("p_" + name, list(shape), dt))

        Tb = [ps("T0", [128, 1024], BF16), ps("T1", [128, 1024], BF16)]
        PP = ps("PP", [128, 1024], F32)
        Pb = [PP[:, 0:512], PP[:, 512:1024]]
        Ab = ps("A", [128, 1024], F32)
        Ob = ps("O", [128, 1024], F32)
        pctr = [0]

        def pnext():
            i = pctr[0] % 2
            pctr[0] += 1
            return Pb[i], ("P", i)


        def MM(out, lhsT, rhs, start, stop, reads, writes):
            return S.add("pe", lambda e: e.matmul(out, lhsT=lhsT, rhs=rhs, start=start, stop=stop), reads, writes)

        def TR(out, in_, reads, writes):
            return S.add("pe", lambda e: e.transpose(out, in_, ident[:]), list(reads) + ["ident"], writes)

        def ACT(out, in_, func, reads, writes, **kw):
            return S.add("act", lambda e: e.activation(out=out, in_=in_, func=func, **kw), reads, writes)

        def TT(eng, out, in0, in1, op, reads, writes):
            return S.add(eng, lambda e: e.tensor_tensor(out=out, in0=in0, in1=in1, op=op), reads, writes)

        def TS(eng, out, in0, s1, s2, op0, op1, reads, writes):
            if s2 is None:
                return S.add(eng, lambda e: e.tensor_scalar(out=out, in0=in0, scalar1=s1, scalar2=None, op0=op0), reads, writes)
            return S.add(eng, lambda e: e.tensor_scalar(out=out, in0=in0, scalar1=s1, scalar2=s2, op0=op0, op1=op1), reads, writes)

        def STT(eng, out, in0, scalar, in1, op0, op1, reads, writes):
            return S.add(eng, lambda e: e.scalar_tensor_tensor(out=out, in0=in0, scalar=scalar, in1=in1, op0=op0, op1=op1), reads, writes)

        def CP(eng, out, in_, reads, writes):
            return S.add(eng, lambda e: e.tensor_copy(out=out, in_=in_), reads, writes)

        def RCP(out, in_, reads, writes):
            return S.add("dve", lambda e: e.reciprocal(out=out, in_=in_), reads, writes)

        def MSET(ap, val, reads, writes):
            return S.add("pool", lambda e: e.memset(ap, val), reads, writes)

        ident = sb("ident", [128, 128], BF16)
        ones_bf = sb("ones_bf", [128, 128], BF16)
        negh = sb("negh", [128, 1], F32)
        cc = sb("cc", [128, 8, 2], F32)
        cond = sb("cond", [128, 8, 2], BF16)
        cond_bc = sb("cond_bc", [128, 2, 8, 128], BF16)
        badac = sb("badac", [128, 2, 16], F32)
        ngc = sb("ngc", [128, 2, 8], F32)
        modc = sb("modc", [128, 16, 2], F32)
        gmodL = [sb("gmod0", [128, 8, 2], F32), sb("gmod1", [128, 8, 2], F32)]
        shiftL = [sb("shiftc0", [128, 8, 2], F32), sb("shiftc1", [128, 8, 2], F32)]
        gate_bc = sb("gate_bc", [128, 2, 1024], F32)
        bg_bc = sb("bg_bc", [128, 1024], F32)
        wada = [sb("wada0", [128, 8, 512], BF16)] * 2
        NXR = 3
        xring = [sb(f"xr{i}", [128, 1024], F32) for i in range(NXR)]
        xctr = [0]
        junk = sb("junk", [128, 1024], BF16)
        ss = sb("ss", [128, 4], F32)
        std = sb("std", [128, 4], F32)
        rstd = sb("rstd", [128, 4], F32)
        xn = sb("xn", [128, 2, 1024], BF16)
        hT = [sb("hT0", [128, 8, 256], BF16)] * 2
        tmpA = sb("tmpA", [128, 1024], F32)
        tmpB = sb("tmpB", [128, 1024], F32)
        xnew = [sb("xnew0", [128, 1024], F32)] * 2
        recb = sb("recb", [128, 2, 512], F32)
        rec2 = recb
        pTw = [sb(f"pTw{i}", [128, 1024], BF16) for i in range(2)]
        pT = [pTw[0][:, 0:512], pTw[0][:, 512:1024], pTw[1][:, 0:512], pTw[1][:, 512:1024]]
        pTc = [0]
        cosT = sb("cosT", [128, 2048], BF16)
        sinT = sb("sinT", [128, 2048], BF16)

        MSET(ident[:], 1.0, [], ["ident"])
        S.add("pool", lambda e: e.affine_select(out=ident[:], in_=ident[:], pattern=[[-1, 128]],
                                                compare_op=ALU.is_equal, fill=0.0, base=0, channel_multiplier=1),
              reads=["ident"], writes=["ident"])
        MSET(ones_bf[:], 1.0, [], ["ones_bf"])
        MSET(negh[:], -0.5, [], ["negh"])
        S.dma("sp", cc[:], cc_d, writes=["cc"])
        S.dma("sp", badac[:], badac_d, writes=["badac"])
        S.dma("sp", ngc[:], ng_d, writes=["ngc"])
        ACT(cond[:], cc[:], AF.Silu, ["cc"], ["cond"])
        for j in range(2):
            for kc in range(8):
                CP("dve", cond_bc[:, j, kc, :], cond[:, kc, j:j + 1].to_broadcast([128, 128]), ["cond"], [("cond_bc", j)])

        def adaln_pieces(L, need_gate_c, gate_slot):
            wv = wada_d[L].rearrange("(kc p) n -> p kc n", p=128)
            wt = wada[0]
            wk = ("wada", 0)
            gm, sh = gmodL[L], shiftL[L]
            pieces = []
            for blk in range(6):
                def dma_piece(blk=blk):
                    S.dma("pool", wt[:], wv[:, :, blk * 512:(blk + 1) * 512], writes=[wk])
                    if blk == 4:
                        S.dma("sp", bg_bc[:], bada_d[L:L + 1, 2048:3072].partition_broadcast(128), writes=["bg_bc"])

                def mm_piece(blk=blk):
                    if blk < 4:
                        pt, pk = pnext()
                        for mm_ in range(4):
                            for kc in range(8):
                                MM(pt[:, mm_ * 2:mm_ * 2 + 2], wt[:, kc, mm_ * 128:(mm_ + 1) * 128], cond[:, kc, :], kc == 0, kc == 7,
                                   [wk, "cond"], [pk])
                        pv = pt[:, 0:8].rearrange("p (m j) -> p m j", j=2)
                        for j in range(2):
                            TT("dve", modc[:, blk * 4:blk * 4 + 4, j], pv[:, :, j], badac[:, L, blk * 4:blk * 4 + 4], ALU.add,
                               [pk, "badac"], [("modc", blk, j)])
                        if blk == 3:
                            for j in range(2):
                                mk = [("modc", b_, j) for b_ in range(4)]
                                TS("dve", gm[:, :, j], modc[:, 8:16, j], 1.0, None, ALU.add, None, mk, [("gmod", L, j)])
                                TT("dve", gm[:, :, j], gm[:, :, j], ngc[:, L, :], ALU.mult, [("gmod", L, j), "ngc"], [("gmod", L, j)])
                                CP("dve", sh[:, :, j], modc[:, 0:8, j], mk, [("shiftc", L, j)])
                    else:
                        n = blk - 4
                        for j in range(2 if need_gate_c else 1):
                            pt, pk = pnext()
                            for kc in range(8):
                                MM(pt[:, :], cond_bc[:, j, kc, :], wt[:, kc, :], kc == 0, kc == 7, [wk, ("cond_bc", j)], [pk])
                            TT("dve", gate_bc[:, gate_slot[j], n * 512:(n + 1) * 512], pt[:, :], bg_bc[:, n * 512:(n + 1) * 512], ALU.add,
                               [pk, "bg_bc"], [("gate_bc", gate_slot[j], n)])
                pieces.append((dma_piece, mm_piece))
            return pieces

        normed = set()

        preloaded = {}

        def preload_x(tag, srcs, src_keys, queue="act"):
            if tag in preloaded or tag in normed:
                return
            xts = []
            for src, sk in zip(srcs, src_keys):
                xi = xctr[0] % NXR
                xctr[0] += 1
                xt = xring[xi]
                xk = ("xr", xi)
                xts.append((xt, xk))
                S.dma(queue, xt[:], src, reads=[sk], writes=[xk])
            preloaded[tag] = xts

        def ensure_norm(tag, srcs, src_keys):
            if tag in normed:
                return
            normed.add(tag)
            stage1_norm(srcs, src_keys, preloaded.pop(tag, None))

        def stage1(srcs, src_keys, j, hTt, hkey, L=0, tag=None):
            ensure_norm(tag if tag is not None else ("anon", len(normed)), srcs, src_keys)
            stage1_tr(len(srcs), j, hTt, hkey, L)

        def stage1_norm(srcs, src_keys, pre=None):
            nt = len(srcs)
            xts = []
            for i, (src, sk) in enumerate(zip(srcs, src_keys)):
                if pre is not None:
                    xt, xk = pre[i]
                else:
                    xi = xctr[0] % NXR
                    xctr[0] += 1
                    xt = xring[xi]
                    xk = ("xr", xi)
                    S.dma("sp", xt[:], src, reads=[sk], writes=[xk])
                xts.append((xt, xk))
                ACT(junk[:], xt[:], AF.Square, [xk], ["junk", ("ss", i)], accum_out=ss[:, i:i + 1])
            ACT(std[:, 0:nt], ss[:, 0:nt], AF.Sqrt, [("ss", i) for i in range(nt)], ["std"], scale=1.0 / 1024, bias=EPS)
            RCP(rstd[:, 0:nt], std[:, 0:nt], ["std"], ["rstd"])
            for i, (xt, xk) in enumerate(xts):
                TS("dve", xn[:, i, :], xt[:], rstd[:, i:i + 1], None, ALU.mult, None, [xk, "rstd"], [("xn", i)])

        def stage1_tr(nt, j, hTt, hkey, L):
            for pr in range(4):
                tb = Tb[pr % 2]
                tk = ("T", pr % 2)
                for kc in (2 * pr, 2 * pr + 1):
                    off = (kc % 2) * 512
                    for i in range(nt):
                        TR(tb[:, off + i * 128: off + (i + 1) * 128], xn[:, i, kc * 128:(kc + 1) * 128], [("xn", i)], [tk])
                for kc in (2 * pr, 2 * pr + 1):
                    off = (kc % 2) * 512
                    TS("dve", hTt[:, kc, 0:nt * 128], tb[:, off:off + nt * 128], gmodL[L][:, kc, j:j + 1], shiftL[L][:, kc, j:j + 1], ALU.mult, ALU.add,
                       [tk, ("gmod", L, j), ("shiftc", L, j)], [hkey])

        def proj_fm(hTt, hkey, tok0, ntok, wt, wkey, col0, M, nkc=8):
            pt, pk = pnext()
            for kc in range(nkc):
                MM(pt[0:M, 0:ntok], wt[:, kc, col0:col0 + M], hTt[:, kc, tok0:tok0 + ntok], kc == 0, kc == nkc - 1, flat([hkey, wkey]), [pk])
            return pt, pk

        def flat(l):
            o = []
            for a in l:
                if isinstance(a, list):
                    o.extend(a)
                else:
                    o.append(a)
            return o

        octr = [0]

        def outproj_residual(yT, ykey, tok0, wo, wokey, xsrc, xsrc_key, j, dst, dst_key, final=False, alt=None):
            for n in range(2):
                for kc in range(8):
                    MM(Ob[:, n * 512:(n + 1) * 512], yT[:, kc, tok0:tok0 + 128], wo[:, kc, n * 512:(n + 1) * 512], kc == 0, kc == 7,
                       list(ykey) + [wokey], [("O", n)])
            xi = xctr[0] % NXR
            xctr[0] += 1
            xt = xring[xi]
            xk = ("xr", xi)
            S.dma("sp", xt[:], xsrc, reads=[xsrc_key], writes=[xk])
            if alt is None or alt[5] is None:
                tA, tAk = tmpA[:], ["tmpA", ("tmpo", 0), ("tmpo", 1)]
            else:
                tA, tAk = alt[5], alt[6]
            TT("dve", tA, Ob[:], gate_bc[:, j, :], ALU.mult, [("O", 0), ("O", 1), ("gate_bc", j, 0), ("gate_bc", j, 1)], tAk)
            xo = xnew[octr[0] % 2]
            xok = ("xnew", 0)
            octr[0] += 1
            if not final:
                TT("dve", xo[:, 0:512], tA[:, 0:512], xt[:, 0:512], ALU.add, tAk + [xk], [(xok, "a")])
                TT("pool", xo[:, 512:1024], tA[:, 512:1024], xt[:, 512:1024], ALU.add, tAk + [xk], [(xok, "b")])
                return S.dma("sp", dst, xo[:], reads=[(xok, "a"), (xok, "b")], writes=[dst_key])
            if alt is None:
                tB, tBk, xo2, xo2k, c = tmpB[:], ["tmpB", "r1", "r2"], xo[:], [xok], 0
            else:
                tB, tBk, xo2, xo2k, c = alt[0:5]
            TT("pool", tB, tA, xt[:], ALU.add, tAk + [xk], tBk)
            ACT(junk[:], tB, AF.Square, tBk, ["junk", ("ss", c)], accum_out=ss[:, c:c + 1])
            ACT(std[:, c:c + 1], ss[:, c:c + 1], AF.Sqrt, [("ss", c)], [("std", c)], scale=1.0 / 1024, bias=EPS)
            RCP(rstd[:, c:c + 1], std[:, c:c + 1], [("std", c)], [("rstd", c)])
            STT("dve", xo2, tB, rstd[:, c:c + 1], bg_bc[:], ALU.mult, ALU.mult, tBk + [("rstd", c), "bg_bc"], xo2k)
            return S.dma("sp", dst, xo2, reads=xo2k, writes=[dst_key])

        actr = [0]
        pend_e2 = []

        def flush_e2():
            while pend_e2:
                pend_e2.pop(0)()

        def attn_core(q_ap, qkey, kts, A_idx_unused, ones_first, sink_mm, scale, o_ap, g_ap, t_view, gkey, ykey):
            A_idx = actr[0] % 2
            actr[0] += 1
            Aacc = Ab[:, A_idx * 512:(A_idx + 1) * 512]
            ak = ("A", A_idx)
            n = len(kts)

            def score(idx):
                kap, kkey, mask = kts[idx][0], kts[idx][1], kts[idx][4]
                pt, pk = pnext()
                MM(pt[:, :], kap, q_ap, True, mask is None, flat([kkey, qkey]), [pk])
                if mask is not None:
                    MM(pt[:, :], ident[:, :], mask, False, True, ["ident", "masks"], [pk])
                return pt, pk
            nxt = score(0)
            for idx, (kap, kkey, vap, vkey, mask) in enumerate(kts):
                pt, pk = nxt
                if idx + 1 < n:
                    nxt = score(idx + 1)
                pi = pTc[0] % 4
                pTc[0] += 1
                pt_sb = pT[pi]
                ptk = ("pT", pi)
                ACT(pt_sb, pt[:, :], AF.Exp, [pk], [ptk], scale=scale)
                last = (idx == n - 1) and sink_mm is None
                MM(Aacc, vap, pt_sb, idx == 0, last, flat([vkey, ptk]), [ak])
            if sink_mm is not None:
                MM(Aacc, sink_mm[0], sink_mm[1], False, True, ["sel", "esink"], [ak])
            oh = slice(64, 128) if ones_first else slice(0, 64)
            dh = slice(0, 64) if ones_first else slice(64, 128)
            flush_e2()
            RCP(recb[dh, A_idx, :], Aacc[dh, :], [ak], [("recb", A_idx)])
            S.dma("sp", rec2[oh, A_idx, :], recb[dh, A_idx, :], reads=[("recb", A_idx)], writes=[("rec2", A_idx)])

            def e2():
                TT("dve", tmpA[oh, A_idx * 512:(A_idx + 1) * 512], Aacc[oh, :], rec2[oh, A_idx, :], ALU.mult, [ak, ("rec2", A_idx)], [("tmpo", A_idx)])
                tv = tmpA[oh, A_idx * 512:(A_idx + 1) * 512]
                if t_view is not None:
                    tv = tv.rearrange("p (c t) -> p c t", c=4)
                TT("pool", o_ap, tv, g_ap, ALU.mult, flat([("tmpo", A_idx), gkey]), [ykey])
            pend_e2.append(e2)

        pair_banks = [(PP, [("P", 0), ("P", 1)]), (Ob, [("O", 0), ("O", 1)])]

        def attn_dual(chains, scale, acc=None):
            n = len(chains[0]["kts"])
            st_ = []
            for c, ch in enumerate(chains):
                A_idx = c
                a_ap, a_key = (Ab[:, A_idx * 512:(A_idx + 1) * 512], ("A", A_idx)) if acc is None else acc[c]
                st_.append({"A": a_ap, "ak": a_key, "A_idx": A_idx, "banks": pair_banks[c], "nxt": None})

            def score(c, idx):
                ch = chains[c]
                kap, kkey, mask = ch["kts"][idx][0], ch["kts"][idx][1], ch["kts"][idx][4]
                bt, bk = st_[c]["banks"]
                h_ = idx % 2
                pt, pk = bt[:, h_ * 512:(h_ + 1) * 512], bk[h_]
                MM(pt, kap, ch["q_ap"], True, mask is None, flat([kkey, ch["qkey"]]), [pk])
                if mask is not None:
                    MM(pt, ident[:, :], mask, False, True, ["ident", "masks"], [pk])
                return pt, pk
            for c in range(2):
                st_[c]["nxt"] = score(c, 0)
            for idx in range(n):
                cur = [st_[c]["nxt"] for c in range(2)]
                if idx + 1 < n:
                    for c in range(2):
                        st_[c]["nxt"] = score(c, idx + 1)
                for c in range(2):
                    ch = chains[c]
                    pt, pk = cur[c]
                    vap, vkey = ch["kts"][idx][2], ch["kts"][idx][3]
                    pi = pTc[0] % 4
                    pTc[0] += 1
                    pt_sb = pT[pi]
                    ptk = ("pT", pi)
                    ACT(pt_sb, pt, AF.Exp, [pk], [ptk], scale=scale)
                    MM(st_[c]["A"], vap, pt_sb, idx == 0, False, flat([vkey, ptk]), [st_[c]["ak"]])
            for c in range(2):
                ch = chains[c]
                MM(st_[c]["A"], ch["sink_mm"][0], ch["sink_mm"][1], False, True, ["sel", "esink"], [st_[c]["ak"]])
            flush_e2()
            for c in range(2):
                ch = chains[c]
                A_idx, Aacc, ak = st_[c]["A_idx"], st_[c]["A"], st_[c]["ak"]
                oh = slice(64, 128) if ch["ones_first"] else slice(0, 64)
                dh = slice(0, 64) if ch["ones_first"] else slice(64, 128)
                RCP(recb[dh, A_idx, :], Aacc[dh, :], [ak], [("recb", A_idx)])
                S.dma("sp", rec2[oh, A_idx, :], recb[dh, A_idx, :], reads=[("recb", A_idx)], writes=[("rec2", A_idx)])

                def e2(A_idx=A_idx, Aacc=Aacc, ak=ak, oh=oh, ch=ch):
                    TT("dve", tmpA[oh, A_idx * 512:(A_idx + 1) * 512], Aacc[oh, :], rec2[oh, A_idx, :], ALU.mult, [ak, ("rec2", A_idx)], [("tmpo", A_idx)])
                    tv = tmpA[oh, A_idx * 512:(A_idx + 1) * 512].rearrange("p (c t) -> p c t", c=4)
                    TT("pool", ch["o_ap"], tv, ch["g_ap"], ALU.mult, flat([("tmpo", A_idx), ch["gkey"]]), [ch["ykey"]])
                pend_e2.append(e2)

        def score_pair_of(kts, q_ap, qkey, j):
            bt, bk = pair_banks[pairc[0] % 2]
            pairc[0] += 1
            for t in range(2):
                kap, kkey = kts[2 * j + t][0], kts[2 * j + t][1]
                MM(bt[:, t * 512:(t + 1) * 512], kap, q_ap, True, True, flat([kkey, qkey]), [bk[t]])
            return bt, bk

        def attn_pairs(q_ap, qkey, kts, ones_first, scale, o_ap, g_ap, gkey, ykey, side=(), warm=None, pre=None, next_first=None):
            A_idx = actr[0] % 2
            actr[0] += 1
            Aacc = Ab[:, A_idx * 512:(A_idx + 1) * 512]
            ak = ("A", A_idx)
            n = len(kts)
            assert n % 2 == 0
            npair = n // 2
            def score_pair(j):
                return score_pair_of(kts, q_ap, qkey, j)
            nxt = pre if pre is not None else score_pair(0)
            handed = None
            for j in range(npair):
                bt, bk = nxt
                if j + 1 < npair:
                    nxt = score_pair(j + 1)
                elif next_first is not None:
                    handed = next_first()
                pi = pwc[0] % 2
                pwc[0] += 1
                pw = pTw[pi]
                pwk = [("pT", 2 * pi), ("pT", 2 * pi + 1)]
                ACT(pw[:], bt[:], AF.Exp, bk, pwk, scale=scale)
                for t in range(2):
                    idx = 2 * j + t
                    vap, vkey = kts[idx][2], kts[idx][3]
                    MM(Aacc, vap, pw[:, t * 512:(t + 1) * 512], idx == 0, idx == n - 1, flat([vkey, pwk[t]]), [ak])
                if j == 6:
                    flush_e2()
                ns = len(side)
                per = -(-ns // max(1, npair - 4))
                if j >= 2:
                    for pc_ in side[(j - 2) * per:(j - 1) * per]:
                        pc_()
                if warm is not None:
                    warm()
            oh = slice(64, 128) if ones_first else slice(0, 64)
            dh = slice(0, 64) if ones_first else slice(64, 128)
            RCP(recb[dh, A_idx, :], Aacc[dh, :], [ak], [("recb", A_idx)])
            S.dma("sp", rec2[oh, A_idx, :], recb[dh, A_idx, :], reads=[("recb", A_idx)], writes=[("rec2", A_idx)])

            def e2():
                TT("dve", tmpA[oh, A_idx * 512:(A_idx + 1) * 512], Aacc[oh, :], rec2[oh, A_idx, :], ALU.mult, [ak, ("rec2", A_idx)], [("tmpo", A_idx)])
                TT("pool", o_ap, tmpA[oh, A_idx * 512:(A_idx + 1) * 512], g_ap, ALU.mult, flat([("tmpo", A_idx), gkey]), [ykey])
            pend_e2.append(e2)
            return handed

        pairc = [0]
        pwc = [0]
        outs = []
        with contextlib.ExitStack() as st0:
            def sb0(name, shape, dt):
                return st0.enter_context(nc.sbuf_tensor("s0_" + name, list(shape), dt))
            w0 = sb0("w0", [128, 8, W0C], BF16)
            wo0 = sb0("wo0", [128, 8, 1024], BF16)
            wst = sb0("wst", [128, 8, 128], BF16)
            bs_t = sb0("bs_t", [8, 128], BF16)
            bones = sb0("bones", [8, 512], BF16)
            masks = sb0("masks", [128, 2, 512], BF16)
            lng = sb0("lng", [128, 512], F32)
            lnb = sb0("lnb", [128, 512], F32)
            sink_t = sb0("sink_t", [1, 2, 512], F32)
            esink = sb0("esink", [1, 2, 512], BF16)
            sel = sb0("sel", [1, 2, 128], BF16)
            kT = sb0("kT", [128, NT * 128], BF16)
            vaug = sb0("vaug", [128, NT, 2, 128], BF16)
            qT = [sb0("qT0", [128, 4, 256], BF16), sb0("qT1", [128, 4, 256], BF16)]
            sga = [sb0("sga0", [128, 4, 256], BF16), sb0("sga1", [128, 4, 256], BF16)]
            yT = [sb0("yT0", [128, 8, 256], BF16), sb0("yT1", [128, 8, 256], BF16)]
            gu2 = [sb0("gu", [128, 512], BF16), sb0("gu_b", [128, 512], BF16)]
            sgb2 = [sb0("sgb", [128, 512], BF16), sb0("sgb_b", [128, 512], BF16)]
            ugb2 = [sb0("ugb", [128, 512], BF16), sb0("ugb_b", [128, 512], BF16)]
            gv2 = [sb0("gv", [128, 512], F32), sb0("gv_b", [128, 512], F32)]
            vn2 = [sb0("vn", [128, 512], BF16), sb0("vn_b", [128, 512], BF16)]
            bgt = sb0("bgt", [128, 512], BF16)
            st42 = [sb0("st4", [128, 8], F32), sb0("st4_b", [128, 8], F32)]
            r1 = sb0("r1", [128, 256], F32)
            r2 = sb0("r2", [128, 256], F32)

            w0v = w0_d.rearrange("(kc p) n -> p kc n", p=128)
            for (c0, c1) in [(0, 1280), (1280, 2432), (2432, W0C)]:
                S.dma("pool", w0[:, :, c0:c1], w0v[:, :, c0:c1], writes=[("w0", c0)])
            S.add("pool", None, reads=[("w0", 0), ("w0", 1280), ("w0", 2432)], writes=["w0"])
            for dp_, mp_ in adaln_pieces(0, True, (0, 1)):
                dp_()
                mp_()
            S.dma("pool", wo0[:], wout0_d.rearrange("(kc p) n -> p kc n", p=128), writes=["wo0"])
            S.dma("pool", wst[:], wst_d, writes=["wst"])
            S.dma("pool", bs_t[:], bs_d, writes=["bs_t"])
            S.dma("pool", bones[:], bones_d, writes=["bones"])
            S.dma("pool", masks[:], mask_d, writes=["masks"])
            S.dma("sp", lng[:], lng_d.partition_broadcast(128), writes=["lng"])
            S.dma("sp", lnb[:], lnb_d.partition_broadcast(128), writes=["lnb"])
            S.dma("sp", sink_t[:], sink_d, writes=["sink_t"])
            S.dma("pool", cosT[:], cos0_d, writes=["cosT"])
            S.dma("pool", sinT[:], sin0_d, writes=["sinT"])
            ACT(esink[:], sink_t[:], AF.Exp, ["sink_t"], ["esink"])
            MSET(sel[:], 0.0, [], ["sel"])
            MSET(sel[:, 0, 64:128], 1.0, ["sel"], ["sel"])
            MSET(sel[:, 1, 0:64], 1.0, ["sel"], ["sel"])
            MSET(vaug[:, :, 0, 64:128], 1.0, [], ["vaug_ones"])
            MSET(vaug[:, :, 1, 0:64], 1.0, [], ["vaug_ones"])

            import os
            STOP = int(os.environ.get("KSTOP", "0"))

            def l0_stage(k, tiles, is_ctx, part="ab"):
                par = k % 2
                nt = len(tiles)
                ntok = nt * 128
                hTt = hT[par]
                hkey = ("hT", 0)

                def tmg(i, col0, n):
                    pt, pk = pnext()
                    for kc in range(8):
                        MM(pt[:, 0:n], hTt[:, kc, i * 128:(i + 1) * 128], w0[:, kc, col0:col0 + n], kc == 0, kc == 7, [hkey, "w0"], [pk])
                    return pt, pk
                if "a" in part:
                    l0_stage_a(k, tiles, is_ctx, par, nt, ntok, hTt, hkey, tmg)
                if "b" in part:
                    sub = "".join(ch for ch in part if ch in "12") or "12"
                    l0_stage_b(k, tiles, is_ctx, par, nt, ntok, hTt, hkey, tmg, sub)

            def l0_stage_a(k, tiles, is_ctx, par, nt, ntok, hTt, hkey, tmg):
                if is_ctx:
                    srcs = [ctx_d[i * 128:(i + 1) * 128, :] for i in range(2)]
                else:
                    srcs = [x_d[t * 128:(t + 1) * 128, :] for t in tiles]
                stage1(srcs, [("xin", t) for t in tiles], 1 if is_ctx else 0, hTt, hkey, tag=("L0", k))
                if STOP in (12, 122):
                    return
                tok0 = 0 if is_ctx else tiles[0] * 128
                ktok0 = tiles[0] * 128
                for c in range(5):
                    col = c * 128 if c < 4 else 1024
                    rcol = 512 + c * 128 if c < 4 else 1152
                    dst = qT[par][:, c, 0:ntok] if c < 4 else kT[:, ktok0:ktok0 + ntok]
                    dkey = ("qT", par) if c < 4 else ("kT", k)
                    pt, pk = proj_fm(hTt, hkey, 0, ntok, w0, "w0", col, 128)
                    if is_ctx:
                        ACT(dst, pt[:, 0:ntok], AF.Copy, [pk], [dkey])
                    else:
                        TT("dve", r1[:, 0:ntok], pt[:, 0:ntok], cosT[:, tok0:tok0 + ntok], ALU.mult, [pk, "cosT"], ["r1"])
                        pt2, pk2 = proj_fm(hTt, hkey, 0, ntok, w0, "w0", rcol, 128)
                        TT("dve", r2[:, 0:ntok], pt2[:, 0:ntok], sinT[:, tok0:tok0 + ntok], ALU.mult, [pk2, "sinT"], ["r2"])
                        TT("dve", dst, r1[:, 0:ntok], r2[:, 0:ntok], ALU.add, ["r1", "r2"], [dkey])
                for c in range(4):
                    pt, pk = proj_fm(hTt, hkey, 0, ntok, w0, "w0", 1280 + c * 128, 128)
                    ACT(sga[par][:, c, 0:ntok], pt[:, 0:ntok], AF.Silu, [pk], [("sga", par)])
                for i, t in enumerate(tiles):
                    pt, pk = tmg(i, 1792, 128)
                    ACT(vaug[:, t, 0, 0:64], pt[:, 0:64], AF.Copy, [pk], [("v", t, 0)])
                    ACT(vaug[:, t, 1, 64:128], pt[:, 64:128], AF.Copy, [pk], [("v", t, 1)])

            def l0_stage_b(k, tiles, is_ctx, par, nt, ntok, hTt, hkey, tmg, sub="12"):
                if "1" in sub:
                    l0_stage_b1(k, tiles, is_ctx, par, nt, ntok, hTt, hkey, tmg)
                if "2" in sub:
                    l0_stage_b2(k, tiles, is_ctx, par, nt, ntok, hTt, hkey, tmg)

            def l0_stage_b1(k, tiles, is_ctx, par, nt, ntok, hTt, hkey, tmg):
                for i, t in enumerate(tiles):
                    gu, sgb, ugb, gv, st4 = gu2[i], sgb2[i], ugb2[i], gv2[i], st42[i]
                    pt, pk = tmg(i, 2944, 512)
                    ACT(sgb[:], pt[:, :], AF.Silu, [pk], [("sgb", i)])
                    pt, pk = tmg(i, 1920, 512)
                    ACT(gu[:], pt[:, :], AF.Gelu, [pk], [("gu", i)])
                    TT("pool", ugb[:], gu[:], sgb[:], ALU.mult, [("gu", i), ("sgb", i)], [("ugb", i)])
                    pt, pk = tmg(i, 2432, 512)
                    ACT(gv[:], pt[:, :], AF.Gelu, [pk], [("gv", i), ("st4a", i)], accum_out=st4[:, 0:1])
                for i, t in enumerate(tiles):
                    gu, sgb, ugb, gv, st4 = gu2[i], sgb2[i], ugb2[i], gv2[i], st42[i]
                    ACT(junk[:, 0:512], gv[:], AF.Square, [("gv", i)], ["junk", ("st4b", i)], accum_out=st4[:, 1:2])
                    TS("dve", st4[:, 2:4], st4[:, 0:2], 1.0 / 512, None, ALU.mult, None, [("st4a", i), ("st4b", i)], [("st4c", i)])
                    TT("dve", st4[:, 4:5], st4[:, 2:3], st4[:, 2:3], ALU.mult, [("st4c", i)], [("st4d", i)])
                    TT("dve", st4[:, 5:6], st4[:, 3:4], st4[:, 4:5], ALU.subtract, [("st4c", i), ("st4d", i)], [("st4e", i)])
                    ACT(st4[:, 6:7], st4[:, 5:6], AF.Sqrt, [("st4e", i)], [("st4f", i)], scale=1.0, bias=EPS)
                    RCP(st4[:, 7:8], st4[:, 6:7], [("st4f", i)], [("st4g", i)])
                    STT("dve", gv[:], gv[:], st4[:, 2:3], lng[:], ALU.subtract, ALU.mult, [("gv", i), ("st4c", i), "lng"], [("gv", i)])
                    STT("dve", vn2[i][:], gv[:], st4[:, 7:8], lnb[:], ALU.mult, ALU.add, [("gv", i), ("st4g", i), "lnb"], [("vn", i)])

            def l0_stage_b2(k, tiles, is_ctx, par, nt, ntok, hTt, hkey, tmg):
                for i, t in enumerate(tiles):
                    ugb, vn = ugb2[i], vn2[i]
                    pt, pk = pnext()
                    MM(pt[:, :], bs_t[:, :], bones[:, :], True, False, ["bs_t", "bones"], [pk])
                    for g in range(8):
                        MM(pt[:, g * 64:(g + 1) * 64], wst[:, g, :], vn[:, g * 64:(g + 1) * 64], False, g == 7, ["wst", ("vn", i)], [pk])
                    TT("dve", bgt[:], pt[:, :], ugb[:], ALU.mult, [pk, ("ugb", i)], ["bgt"])
                    tb = Tb[i % 2]
                    tk = ("T", i % 2)
                    for jj in range(4):
                        TR(tb[:, jj * 128:(jj + 1) * 128], bgt[:, jj * 128:(jj + 1) * 128], ["bgt"], [tk])
                    ACT(yT[par][:, 4:8, i * 128:(i + 1) * 128], tb[:, 0:512].rearrange("p (c t) -> p c t", c=4), AF.Copy, [tk], [("yTb", par, i)])

            def l0_attn(k, tiles, is_ctx, part="ao"):
                par = k % 2
                if "a" in part:
                    l0_attn_a(k, tiles, is_ctx, par)
                if "o" in part:
                    l0_attn_o(k, tiles, is_ctx, par)

            def l0_attn_o(k, tiles, is_ctx, par):
                flush_e2()
                for i, t in enumerate(tiles):
                    if is_ctx:
                        src, sk, dst, dk, j = ctx_d[i * 128:(i + 1) * 128, :], ("xin", t), xcs_d[i * 128:(i + 1) * 128, :], ("xs", t), 1
                    else:
                        src, sk, dst, dk, j = x_d[t * 128:(t + 1) * 128, :], ("xin", t), xs_d[t * 128:(t + 1) * 128, :], ("xs", t), 0
                    outs.append(outproj_residual(yT[par], [("yTa", par, i, 0), ("yTa", par, i, 1), ("yTb", par, i)], i * 128, wo0, "wo0", src, sk, j, dst, dk))

            def l0_attn_a(k, tiles, is_ctx, par):
                for i, t in enumerate(tiles):
                    if is_ctx:
                        kt_ids = [(16, None), (17, None)]
                    else:
                        kt_ids = [(16, None), (17, None)]
                        if t > 0:
                            kt_ids.append((t - 1, 0))
                        kt_ids.append((t, None))
                        if t < NT_L - 1:
                            kt_ids.append((t + 1, 1))
                    chains = []
                    for hk in range(2):
                        hs = slice(hk * 64, (hk + 1) * 64)
                        kts = []
                        for (kt, m) in kt_ids:
                            kslab = 0 if kt >= 16 else 1 + kt // 2
                            kts.append((kT[hs, kt * 128:(kt + 1) * 128], ("kT", kslab), vaug[:, kt, hk, :], [("v", kt, hk), "vaug_ones"],
                                        None if m is None else masks[:, m, :]))
                        oh = slice(64, 128) if hk == 1 else slice(0, 64)
                        chains.append(dict(q_ap=qT[par][hs, 0:4, i * 128:(i + 1) * 128], qkey=("qT", par), kts=kts, ones_first=(hk == 1),
                                           sink_mm=(sel[0:1, hk, :], esink[0:1, hk, :]),
                                           o_ap=yT[par][oh, 0:4, i * 128:(i + 1) * 128], g_ap=sga[par][oh, 0:4, i * 128:(i + 1) * 128],
                                           gkey=("sga", par), ykey=("yTa", par, i, hk)))
                    acc = None if i % 2 == 0 else [(Tb[0][:, :].bitcast(F32), ("T", 0)), (Tb[1][:, :].bitcast(F32), ("T", 1))]
                    attn_dual(chains, 0.125, acc)

            seq = [(0, [16, 17], True)] + [(1 + s, [2 * s, 2 * s + 1], False) for s in range(8)]
            def fin():
                S.barrier()
                with nc.Block() as block:
                    S.emit(block, st)
                print("stats", S.stats, flush=True)
                return nc
            if STOP == 1:
                return fin()
            l0_stage(*seq[0])
            if STOP in (2, 12, 13, 122):
                return fin()
            l0_attn(*seq[0])
            if STOP == 3:
                return fin()
            l0_stage(*seq[1])
            ada1 = adaln_pieces(1, False, (1, 1))

            def prefetch_norm0(k):
                if k > 8:
                    return
                tiles_ = seq[k][1]
                ensure_norm(("L0", k), [x_d[t * 128:(t + 1) * 128, :] for t in tiles_], [("xin", t) for t in tiles_])
            prefetch_norm0(2)
            for s in range(1, 9):
                if 2 <= s <= 7:
                    ada1[s - 2][0]()
                if s + 1 <= 8:
                    l0_stage(*seq[s + 1], part="a")
                if s + 2 <= 8:
                    tiles_p = seq[s + 2][1]
                    preload_x(("L0", s + 2), [x_d[t * 128:(t + 1) * 128, :] for t in tiles_p], [("xin", t) for t in tiles_p])
                l0_attn(*seq[s], part="a")
                prefetch_norm0(s + 2)
                flush_e2()
                if s + 1 <= 8:
                    l0_stage(*seq[s + 1], part="b1")
                l0_attn(*seq[s], part="o")
                if s + 1 <= 8:
                    l0_stage(*seq[s + 1], part="b2")
                if 2 <= s <= 7:
                    ada1[s - 2][1]()
            S.barrier()

        if nlayers == 1:
            S.add("sp", None, extra=outs)
            with nc.Block() as block:
                S.emit(block, st)
            print("stats", S.stats, flush=True)
            return nc

        with contextlib.ExitStack() as st1:
            def sb1(name, shape, dt):
                return st1.enter_context(nc.sbuf_tensor("s1_" + name, list(shape), dt))
            w1 = sb1("w1", [128, 8, W1C], BF16)
            wqb = sb1("wqb", [128, 2, 3072], BF16)
            wkvb = sb1("wkvb", [128, 2048], BF16)
            wo1 = sb1("wo1", [128, 8, 1024], BF16)
            qnc = sb1("qnc", [128, 2], F32)
            kvnc = sb1("kvnc", [128, 1], F32)
            sg = sb1("sg", [128, 8, 2048], BF16)
            qan = sb1("qan", [128, 2, 2048], BF16)
            kvn = sb1("kvn", [128, NT * 128], BF16)
            kpe = sb1("kpe", [96, NT * 128], BF16)
            kTh = [sb1("kTh0", [96, NT * 128], BF16), sb1("kTh1", [96, NT * 128], BF16)]
            vah = [sb1("vah0", [128, NT, 128], BF16), sb1("vah1", [128, NT, 128], BF16)]
            qTh = [sb1("qTh0", [96, 512], BF16), sb1("qTh1", [96, 512], BF16)]
            sq = sb1("sq", [128, 2, 256], BF16)
            qa32 = sb1("qa32", [128, 2, 256], F32)
            sdb = sb1("sdb", [128, 256], F32)
            r1 = tmpB[:, 0:512]
            r2 = tmpB[:, 512:1024]

            S.dma("pool", w1[:], w1_d.rearrange("(kc p) n -> p kc n", p=128), writes=["w1"])
            S.dma("pool", wqb[:], wqb_d.rearrange("(kc p) n -> p kc n", p=128), writes=["wqb"])
            S.dma("pool", wkvb[:], wkvb_d, writes=["wkvb"])
            S.dma("pool", wo1[:], wout1_d.rearrange("(kc p) n -> p kc n", p=128), writes=["wo1"])
            S.dma("sp", qnc[:], qn_d, writes=["qnc"])
            S.dma("sp", kvnc[:], kvn_d, writes=["kvnc"])
            S.dma("pool", cosT[:], cos1_d, writes=["cosT"])
            S.dma("pool", sinT[:], sin1_d, writes=["sinT"])
            MSET(vah[0][:, :, 64:128], 1.0, [], ["vah_ones"])
            MSET(vah[1][:, :, 0:64], 1.0, [], ["vah_ones"])

            sqk = sb1("sqk", [128, 1, 256], BF16)
            qa32k = sb1("qa32k", [128, 1, 256], F32)

            def rms_evac(pts, ntok, sqt, q32t, tag):
                for c, (pt, pk) in enumerate(pts):
                    CP("dve", q32t[:, c, 0:ntok], pt[:, 0:ntok], [pk], [("qa32", tag, c)])
                    ACT(sqt[:, c, 0:ntok], q32t[:, c, 0:ntok], AF.Square, [("qa32", tag, c)], [("sq", tag, c)])

            def rms_finish(n, ntok, sqt, q32t, tag, gcol, gkey, dsts, dkey, nfeat):
                pt2, pk2 = pnext()
                for c in range(n):
                    MM(pt2[:, 0:ntok], ones_bf[:, :], sqt[:, c, 0:ntok], c == 0, c == n - 1, ["ones_bf", ("sq", tag, c)], [pk2])
                ACT(sdb[:, 0:ntok], pt2[:, 0:ntok], AF.Sqrt, [pk2], ["sdb"], scale=1.0 / nfeat, bias=EPS)
                RCP(sdb[:, 0:ntok], sdb[:, 0:ntok], ["sdb"], ["sdb"])
                for c in range(n):
                    STT("dve", dsts[c], q32t[:, c, 0:ntok], gcol[:, c:c + 1], sdb[:, 0:ntok], ALU.mult, ALU.mult,
                        [("qa32", tag, c), "sdb", gkey], [(dkey, c)])

            def l1_stage(k, tiles, is_ctx):
                par = k % 2
                nt = len(tiles)
                ntok = nt * 128
                hTt = hT[par]
                hkey = ("hT", 0)
                if is_ctx:
                    srcs = [xcs_d[i * 128:(i + 1) * 128, :] for i in range(2)]
                else:
                    srcs = [xs_d[t * 128:(t + 1) * 128, :] for t in tiles]
                stage1(srcs, [("xs", t) for t in tiles], 1 if is_ctx else 0, hTt, hkey, L=1, tag=("L1", k))
                if k + 1 <= 8:
                    tiles_n = seq[k + 1][1]
                    ensure_norm(("L1", k + 1), [xs_d[t * 128:(t + 1) * 128, :] for t in tiles_n], [("xs", t) for t in tiles_n])
                ktok0 = tiles[0] * 128
                if not is_ctx:
                    ptsq = [proj_fm(hTt, hkey, 0, ntok, w1, "w1", c * 128, 128) for c in range(2)]
                    rms_evac(ptsq, ntok, sq, qa32, "q")
                pkv = [proj_fm(hTt, hkey, 0, ntok, w1, "w1", 256, 128)]
                rms_evac(pkv, ntok, sqk, qa32k, "k")
                pt, pk = proj_fm(hTt, hkey, 0, ntok, w1, "w1", 384, 96)
                if is_ctx:
                    ACT(kpe[64:96, ktok0:ktok0 + ntok], pt[64:96, 0:ntok], AF.Copy, [pk], [("kpe", k)])
                else:
                    TT("dve", r1[64:96, 0:ntok], pt[64:96, 0:ntok], cosT[64:96, ktok0:ktok0 + ntok], ALU.mult, [pk, "cosT"], ["r1"])
                    pt2, pk2 = proj_fm(hTt, hkey, 0, ntok, w1, "w1", 480, 96)
                    TT("dve", r2[64:96, 0:ntok], pt2[64:96, 0:ntok], sinT[64:96, ktok0:ktok0 + ntok], ALU.mult, [pk2, "sinT"], ["r2"])
                    TT("pool", kpe[64:96, ktok0:ktok0 + ntok], r1[64:96, 0:ntok], r2[64:96, 0:ntok], ALU.add, ["r1", "r2"], [("kpe", k)])

                def gates(c0, c1):
                    for c in range(c0, c1):
                        pt, pk = proj_fm(hTt, hkey, 0, ntok, w1, "w1", 576 + c * 128, 128)
                        ACT(sg[:, c, ktok0:ktok0 + ntok], pt[:, 0:ntok], AF.Silu, [pk], [("sg", c, k)])
                if not is_ctx:
                    gates(0, 3)
                    rms_finish(2, ntok, sq, qa32, "q", qnc, "qnc", [qan[:, c, ktok0:ktok0 + ntok] for c in range(2)], ("qan", k), 256)
                    gates(3, 6)
                rms_finish(1, ntok, sqk, qa32k, "k", kvnc, "kvnc", [kvn[:, ktok0:ktok0 + ntok]], ("kvn", k), 128)
                if not is_ctx:
                    gates(6, 8)

            def fin1():
                S.barrier()
                S.add("sp", None, extra=outs)
                with nc.Block() as block:
                    S.emit(block, st)
                print("stats", S.stats, flush=True)
                return nc
            if STOP == 21:
                return fin1()
            seq = [(0, [16, 17], True)] + [(1 + s, [2 * s, 2 * s + 1], False) for s in range(8)]
            for sq_ in seq:
                l1_stage(*sq_)
                if STOP == 22:
                    return fin1()
            if STOP == 23:
                return fin1()
            allk = [(("kvn", k), 0) for k in range(9)]
            allkpe = [("kpe", k) for k in range(9)]
            slabs = [(16 * 128, 256)] + [(s * 512, 512) for s in range(4)]
            L1SCALE = 96 ** -0.5
            Xb = [Tb[0][:, :].bitcast(F32), Tb[1][:, :].bitcast(F32)]
            xpc = [0]

            def xnext():
                i = xpc[0] % 2
                xpc[0] += 1
                return Xb[i], ("T", i)

            def pe_keepwarm():
                for _ in range(NDUMMY):
                    S.add("pe", lambda e: e.matmul(Xb[1][:, :], lhsT=ident[:, :], rhs=cosT[:, 0:512], start=True, stop=True), ["ident", "cosT"], [("T", 1)])

            def expand_pieces(h):
                hp = h % 2
                kt_t = kTh[hp]
                va_t = vah[hp]
                pieces = []
                for (t0, n) in slabs:
                    def pc(t0=t0, n=n):
                        pt, pk = xnext()
                        MM(pt[0:64, 0:n], wkvb[:, h * 64:(h + 1) * 64], kvn[:, t0:t0 + n], True, True, allk + ["wkvb"], [pk])
                        CP("dve", kt_t[0:64, t0:t0 + n], pt[0:64, 0:n], [pk], [("kTh", hp)])
                    pieces.append(pc)
                pieces.append(lambda: S.dma("sp", kt_t[64:96, :], kpe[64:96, :], reads=allkpe, writes=[("kThp", hp)]))
                vs = slice(0, 64) if hp == 0 else slice(64, 128)
                for g0 in range(0, NT, 8):
                    def pv(g0=g0):
                        ng = min(8, NT - g0)
                        pt, pk = xnext()
                        for u in range(ng):
                            MM(pt[:, u * 64:(u + 1) * 64], kvn[:, (g0 + u) * 128:(g0 + u + 1) * 128], wkvb[:, 1024 + h * 64:1024 + (h + 1) * 64],
                               True, True, allk + ["wkvb"], [pk])
                        CP("dve", va_t[:, g0:g0 + ng, vs], pt[:, 0:ng * 64].rearrange("p (u d) -> p u d", d=64), [pk], [("vah", hp)])
                    pieces.append(pv)
                return pieces

            def qproj_pieces(h, s):
                q0 = s * 512
                qt = qTh[s % 2]
                qk = ("qTh", s % 2)
                qank = [(("qan", 1 + 2 * s + d_), c_) for d_ in range(2) for c_ in range(2)]

                def p1():
                    pt, pk = xnext()
                    for kc in range(2):
                        MM(pt[0:96, :], wqb[:, kc, h * 192:h * 192 + 96], qan[:, kc, q0:q0 + 512], kc == 0, kc == 1, qank + ["wqb"], [pk])
                    TT("dve", r1[64:96, :], pt[64:96, :], cosT[64:96, q0:q0 + 512], ALU.mult, [pk, "cosT"], ["r1"])
                    CP("dve", qt[0:64, :], pt[0:64, :], [pk, "r1"], [(qk, "n")])

                def p2():
                    pt2, pk2 = xnext()
                    for kc in range(2):
                        MM(pt2[0:96, :], wqb[:, kc, h * 192 + 96:h * 192 + 192], qan[:, kc, q0:q0 + 512], kc == 0, kc == 1, qank + ["wqb"], [pk2])
                    TT("dve", r2[64:96, :], pt2[64:96, :], sinT[64:96, q0:q0 + 512], ALU.mult, [pk2, "sinT"], ["r2"])
                    TT("pool", qt[64:96, :], r1[64:96, :], r2[64:96, :], ALU.add, ["r1", "r2"], [(qk, "p")])
                return [p1, p2]

            def block_args(h, s):
                hp = h % 2
                qt = qTh[s % 2]
                qk = ("qTh", s % 2)
                kt_t = kTh[hp]
                va_t = vah[hp]
                kts = [(kt_t[0:96, kt * 128:(kt + 1) * 128], [("kTh", hp), ("kThp", hp)], va_t[:, kt, :], [("vah", hp), "vah_ones"], None) for kt in range(NT)]
                return kts, qt[0:96, :], [(qk, "n"), (qk, "p")]

            def attn_block(h, s, side, pre, nxt_hs):
                hp = h % 2
                q0 = s * 512
                kts, q_ap, qkey = block_args(h, s)
                oh = slice(64, 128) if hp == 1 else slice(0, 64)
                gk = [("sg", h // 2, 1 + 2 * s), ("sg", h // 2, 2 + 2 * s)]
                nf = None
                if nxt_hs is not None:
                    nkts, nq_ap, nqkey = block_args(*nxt_hs)
                    nf = lambda: score_pair_of(nkts, nq_ap, nqkey, 0)
                return attn_pairs(q_ap, qkey, kts, hp == 1, L1SCALE,
                                  sg[oh, h // 2, q0:q0 + 512], sg[oh, h // 2, q0:q0 + 512], gk, ("og", h // 2, s, hp), side=side, warm=pe_keepwarm,
                                  pre=pre, next_first=nf)

            for pc_ in expand_pieces(0) + qproj_pieces(0, 0):
                pc_()
            handed_pair = None
            for h in range(16):
                nxt_exp = expand_pieces(h + 1) if h < 15 else []
                cuts = [0, 3, 5, 7, 9]
                for s in range(4):
                    side = list(nxt_exp[cuts[s]:cuts[s + 1]])
                    if s < 3:
                        side = qproj_pieces(h, s + 1) + side
                    elif h < 15:
                        side = side + qproj_pieces(h + 1, 0)
                    nxt_hs = (h, s + 1) if s < 3 else ((h + 1, 0) if h < 15 else None)
                    handed_pair = attn_block(h, s, side, handed_pair, nxt_hs)
            if STOP == 27:
                return fin1()
            flush_e2()
            S.dma("sp", bg_bc[:], fg_d.partition_broadcast(128), writes=["bg_bc"])
            altA = vah[0][:, :, :].rearrange("p a b -> p (a b)").bitcast(F32)[:, 0:1024]
            altB = vah[1][:, :, :].rearrange("p a b -> p (a b)").bitcast(F32)[:, 0:1024]
            altX = qan[:, :, :].rearrange("p a b -> p (a b)").bitcast(F32)[:, 0:1024]
            bufA = [(tmpA[:], ["tmpA", ("tmpo", 0), ("tmpo", 1)]), (altA, ["falA"])]
            bufB = [(tmpB[:], ["tmpB", "r1", "r2"]), (altB, ["falB"])]
            bufX = [(xnew[0][:], [("xnew", 0)]), (altX, ["falX"])]
            xts = {}

            def fin_A(t):
                s_ = t // 4
                ykey = [("og", c, s_, hp_) for c in range(8) for hp_ in range(2)]
                ot, okeys = pair_banks[t % 2]
                for n in range(2):
                    for kc in range(8):
                        MM(ot[:, n * 512:(n + 1) * 512], sg[:, kc, t * 128:(t + 1) * 128], wo1[:, kc, n * 512:(n + 1) * 512], kc == 0, kc == 7,
                           ykey + ["wo1"], [okeys[n]])
                xi = xctr[0] % NXR
                xctr[0] += 1
                xt = xring[xi]
                xk = ("xr", xi)
                xts[t] = (xt, xk)
                S.dma("sp", xt[:], xs_d[t * 128:(t + 1) * 128, :], reads=[("xs", t)], writes=[xk])
                tA, tAk = bufA[t % 2]
                TT("dve", tA, ot[:], gate_bc[:, 1, :], ALU.mult, okeys + [("gate_bc", 1, 0), ("gate_bc", 1, 1)], tAk)

            def fin_B(t):
                tA, tAk = bufA[t % 2]
                tB, tBk = bufB[t % 2]
                xt, xk = xts[t]
                c = t % 2
                TT("pool", tB, tA, xt[:], ALU.add, tAk + [xk], tBk)
                ACT(junk[:], tB, AF.Square, tBk, ["junk", ("ss", c)], accum_out=ss[:, c:c + 1])
                ACT(std[:, c:c + 1], ss[:, c:c + 1], AF.Sqrt, [("ss", c)], [("std", c)], scale=1.0 / 1024, bias=EPS)

            def fin_C(t):
                tB, tBk = bufB[t % 2]
                xo2, xo2k = bufX[t % 2]
                c = t % 2
                RCP(rstd[:, c:c + 1], std[:, c:c + 1], [("std", c)], [("rstd", c)])
                STT("dve", xo2, tB, rstd[:, c:c + 1], bg_bc[:], ALU.mult, ALU.mult, tBk + [("rstd", c), "bg_bc"], xo2k)
                outs.append(S.dma("sp", out_d[t * 128:(t + 1) * 128, :], xo2, reads=xo2k, writes=[("out", t)]))

            for t in range(NT_L):
                fin_A(t)
                fin_B(t)
                if t >= 1:
                    fin_C(t - 1)
            fin_C(NT_L - 1)
        S.add("sp", None, extra=outs)
        with nc.Block() as block:
            S.emit(block, st)
        print("stats", S.stats, flush=True)
    return nc

def _prep_shared(inp):
    f = np.float32
    g = lambda k: np.asarray(inp[k], dtype=f)
    sh = {}
    sh["w_ada"] = np.ascontiguousarray(g("w_ada"))
    b_ada = g("b_ada")
    sh["b_ada"] = np.ascontiguousarray(b_ada)
    sh["b_ada_col"] = np.ascontiguousarray(b_ada[:, :2048].reshape(2, 16, 128).transpose(2, 0, 1))
    sh["norm_g_col"] = np.ascontiguousarray(g("norm_g").reshape(2, 8, 128).transpose(2, 0, 1))
    sh["final_g"] = np.ascontiguousarray(g("final_g").reshape(1, 1024))
    w = g("w_in0")[0]
    aperm = np.concatenate([np.concatenate([np.arange(j * 64, j * 64 + 64), np.arange((j + 4) * 64, (j + 4) * 64 + 64)]) for j in range(4)])
    d = np.arange(64)
    src = np.where((d % 32) < 16, d + 16, d - 16)
    rot_a = (aperm // 64) * 64 + src[aperm % 64]
    kcols = 512 + np.arange(128)
    krot = 512 + (np.arange(128) // 64) * 64 + src[np.arange(128) % 64]
    cols = np.concatenate([aperm, rot_a, kcols, krot, 1792 + aperm, 640 + np.arange(128), 768 + np.arange(512), 1280 + np.arange(512), 1792 + 512 + np.arange(512)])
    sh["W0"] = np.ascontiguousarray(w[:, cols])
    wo = g("w_out0")[0]
    sh["wout0"] = np.ascontiguousarray(wo[np.concatenate([aperm, 512 + np.arange(512)]), :])
    sink = g("sink0")[0]
    sh["sink_rep"] = np.ascontiguousarray(np.repeat(sink.reshape(2, 4, 1), 128, axis=2).reshape(1, 2, 512))
    sh["ln_g"] = np.ascontiguousarray(g("gm_ln_g").reshape(1, 512))
    sh["ln_b"] = np.ascontiguousarray(g("gm_ln_b").reshape(1, 512))
    sh["WsT"] = np.ascontiguousarray(g("gm_ws")[0].transpose(2, 0, 1))
    sh["bs"] = np.ascontiguousarray(g("gm_bs")[0])
    bo = np.zeros((8, 512), f)
    for gi in range(8):
        bo[gi, gi * 64:(gi + 1) * 64] = 1.0
    sh["blockones"] = bo
    jj = np.arange(128)[:, None]
    ii = np.arange(128)[None, :]
    NEG = -30000.0
    m = np.stack([np.tile(np.where(jj >= ii, 0.0, NEG).astype(f), (1, 4)), np.tile(np.where(jj <= ii, 0.0, NEG).astype(f), (1, 4))], axis=1)
    sh["masks"] = np.ascontiguousarray(m)
    t = np.arange(2048)
    rowp = (t // 64).astype(np.float64)
    colp = (t % 64).astype(np.float64)
    p = np.arange(128)
    dd = p % 64
    inv = 10000.0 ** (-(dd % 16).astype(np.float64) / 16)
    pos = np.where((dd < 32)[:, None], rowp[None, :], colp[None, :])
    ang = (pos.astype(f) * inv.astype(f)[:, None]).astype(f)
    sgn = np.where((dd % 32) < 16, -1.0, 1.0)[:, None]
    sh["cos0"] = np.cos(ang).astype(f)
    sh["sin0"] = (np.sin(ang) * sgn).astype(f)
    w1 = g("w_in1")[0]
    d2 = np.arange(32)
    src2 = np.where((d2 % 16) < 8, d2 + 8, d2 - 8)
    z64 = np.zeros((1024, 64), f)
    sh["W1"] = np.ascontiguousarray(np.concatenate([w1[:, 0:384], z64, w1[:, 384:416], z64, w1[:, 384 + src2], w1[:, 416:1440]], axis=1))
    sh["qn_col"] = np.ascontiguousarray(g("q_norm")[0].reshape(2, 128).T)
    sh["kvn_col"] = np.ascontiguousarray(g("kv_norm")[0].reshape(1, 128).T)
    wq = g("w_qb")[0].reshape(256, 16, 96)
    z = np.zeros((256, 16, 64), f)
    sh["Wqb"] = np.ascontiguousarray(np.concatenate([wq, z, wq[:, :, 64 + src2]], axis=2).reshape(256, 3072))
    wk = g("w_kvb")[0].reshape(128, 16, 128)
    sh["Wkvb"] = np.ascontiguousarray(np.concatenate([wk[:, :, :64].reshape(128, 1024), wk[:, :, 64:].reshape(128, 1024)], axis=1))
    sh["wout1"] = np.ascontiguousarray(g("w_out1")[0])
    c1 = np.zeros((128, 2048), f)
    s1 = np.zeros((128, 2048), f)
    inv2 = 10000.0 ** (-(d2 % 8).astype(np.float64) / 8)
    pos2 = np.where((d2 < 16)[:, None], rowp[None, :], colp[None, :])
    ang2 = (pos2.astype(f) * inv2.astype(f)[:, None]).astype(f)
    sgn2 = np.where((d2 % 16) < 8, -1.0, 1.0)[:, None]
    c1[64:96] = np.cos(ang2)
    s1[64:96] = np.sin(ang2) * sgn2
    sh["cos1"] = c1
    sh["sin1"] = s1
    return sh


def _prep_core(inp, b):
    f = np.float32
    d = {}
    d["x"] = np.ascontiguousarray(np.asarray(inp["x"][b], dtype=f))
    d["ctx"] = np.ascontiguousarray(np.asarray(inp["ctx"][b], dtype=f))
    c = np.asarray(inp["c"][b], dtype=f).reshape(8, 128).T
    cx = np.asarray(inp["c_ctx"], dtype=f).reshape(8, 128).T
    d["cc"] = np.ascontiguousarray(np.stack([c, cx], axis=2))
    return d


_NC_CACHE = {}


def kernel(**inputs):
    sh = _prep_shared(inputs)
    if 2 not in _NC_CACHE:
        _NC_CACHE[2] = build(2)
    nc = _NC_CACHE[2]
    in_maps = []
    for b in range(8):
        m = dict(sh)
        m.update(_prep_core(inputs, b))
        in_maps.append(m)
    res = run_bass_kernel_spmd(nc, in_maps, core_ids=list(range(8)))
    return np.stack([np.asarray(r["out"], dtype=np.float32) for r in res.results], axis=0)
```

```python
import numpy as np
import concourse.bass as bass
import concourse.mybir as mybir
from concourse.bass_utils import run_bass_kernel_spmd

F32 = mybir.dt.float32
BF16 = mybir.dt.bfloat16
AF = mybir.ActivationFunctionType
ALU = mybir.AluOpType
AX = mybir.AxisListType

ENGS = ("pe", "act", "dve", "pool", "sp")
SAME_ENGINE_SYNC = True
N_DMA_SEMS = 6
SEM_K = 240
DMA_USES = 12
NOSYNC = ("pe",)


class Op:
    __slots__ = ("eng", "fn", "deps", "idx", "signaled", "is_dma", "dsem", "dval", "count", "prev_same_sem")

    def __init__(self, eng, fn, is_dma=False):
        self.eng = eng
        self.fn = fn
        self.deps = []
        self.idx = -1
        self.signaled = False
        self.is_dma = is_dma
        self.dsem = None
        self.dval = 0
        self.count = 0
        self.prev_same_sem = None


class Sched:
    def __init__(self, nc):
        self.nc = nc
        self.q = {e: [] for e in ENGS}
        self.last_w = {}
        self.readers = {}
        self.dma_count = {e: 0 for e in ENGS}
        self.dma_last = {}

    def _track(self, op, reads, writes):
        deps = []
        for r in reads:
            w = self.last_w.get(r)
            if w is not None:
                deps.append(w)
        for wkey in writes:
            w = self.last_w.get(wkey)
            if w is not None:
                deps.append(w)
            deps.extend(self.readers.get(wkey, ()))
        for r in reads:
            lst = self.readers.setdefault(r, [])
            if not op.is_dma:
                lst[:] = [o for o in lst if o.is_dma or o.eng != op.eng]
            lst.append(op)
        for wkey in writes:
            self.last_w[wkey] = op
            self.readers[wkey] = []
        best = {}
        keep = []
        seen = set()
        for d in deps:
            if d is op or id(d) in seen:
                continue
            seen.add(id(d))
            if d.is_dma:
                keep.append(d)
            else:
                b = best.get(d.eng)
                if b is None or d.idx > b.idx:
                    best[d.eng] = d
        op.deps.extend(keep)
        op.deps.extend(best.values())

    def add(self, eng, fn, reads=(), writes=(), extra=()):
        op = Op(eng, fn)
        op.idx = len(self.q[eng])
        self.q[eng].append(op)
        self._track(op, reads, writes)
        for d in extra:
            if d is not None and d not in op.deps:
                op.deps.append(d)
        return op

    def dma(self, eng, out, in_, reads=(), writes=(), extra=(), **kw):
        op = Op(eng, lambda e: e.dma_start(out=out, in_=in_, **kw), is_dma=True)
        op.idx = len(self.q[eng])
        self.q[eng].append(op)
        self._track(op, reads, writes)
        for d in extra:
            if d is not None and d not in op.deps:
                op.deps.append(d)
        j = self.dma_count[eng]
        self.dma_count[eng] += 1
        slot = j % N_DMA_SEMS
        ep = j // (N_DMA_SEMS * DMA_USES)
        op.dsem = (eng, ep, slot)
        op.dval = 16 * ((j % (N_DMA_SEMS * DMA_USES)) // N_DMA_SEMS + 1)
        op.prev_same_sem = self.dma_last.get((eng, slot))
        self.dma_last[(eng, slot)] = op
        return op

    def barrier(self):
        lasts = [self.q[e][-1] for e in ENGS if self.q[e]] + list(self.dma_last.values())
        for e in ENGS:
            self.add(e, None, extra=lasts)
        self.last_w = {}
        self.readers = {}

    def emit(self, block, stack):
        nc = self.nc
        csem = {}
        dsem = {}
        for e in ENGS:
            neps = (self.dma_count[e] + N_DMA_SEMS * DMA_USES - 1) // (N_DMA_SEMS * DMA_USES)
            for ep in range(neps):
                for s in range(N_DMA_SEMS):
                    dsem[(e, ep, s)] = stack.enter_context(nc.semaphore(f"d_{e}{ep}_{s}"))
        for e in ENGS:
            for op in self.q[e]:
                for d in op.deps:
                    if not d.is_dma:
                        if d.eng == op.eng and (not SAME_ENGINE_SYNC or d.eng in NOSYNC):
                            continue
                        d.signaled = True
        for e in ENGS:
            c = 0
            for op in self.q[e]:
                if op.signaled and not op.is_dma:
                    c += 1
                op.count = c
            for ep in range((c + SEM_K - 1) // SEM_K):
                csem[(e, ep)] = stack.enter_context(nc.semaphore(f"c_{e}{ep}"))
        self.nsems = len(csem) + len(dsem)
        stats = {e: [0, 0] for e in ENGS}

        def body_for(e):
            def body(eng):
                waited = {}
                for op in self.q[e]:
                    waits = {}
                    for d in op.deps:
                        if d.is_dma:
                            key = ("d",) + d.dsem
                            sem = dsem[d.dsem]
                            val = d.dval
                        else:
                            if d.eng == e and (not SAME_ENGINE_SYNC or e in NOSYNC):
                                continue
                            if d.count == 0:
                                continue
                            dep_ = (d.count - 1) // SEM_K
                            key = ("c", d.eng, dep_)
                            sem = csem[(d.eng, dep_)]
                            val = (d.count - 1) % SEM_K + 1
                        if waits.get(key, (None, 0))[1] < val:
                            waits[key] = (sem, val)
                    if op.is_dma and op.prev_same_sem is not None:
                        p = op.prev_same_sem
                        key = ("d",) + p.dsem
                        if waits.get(key, (None, 0))[1] < p.dval:
                            waits[key] = (dsem[p.dsem], p.dval)
                    for key, (sem, val) in waits.items():
                        if waited.get(key, 0) >= val:
                            continue
                        waited[key] = val
                        eng.wait_ge(sem, val)
                        stats[e][1] += 1
                    mysem = csem[(e, (op.count - 1) // SEM_K)] if (op.signaled and not op.is_dma) else None
                    if op.fn is None:
                        if op.signaled:
                            eng.nop(nofuse=True).then_inc(mysem, 1)
                        continue
                    ins = op.fn(eng)
                    stats[e][0] += 1
                    if op.is_dma:
                        ins.then_inc(dsem[op.dsem], 16)
                    elif op.signaled:
                        ins.then_inc(mysem, 1)
            return body

        block.tensor(body_for("pe"))
        block.scalar(body_for("act"))
        block.vector(body_for("dve"))
        block.gpsimd(body_for("pool"))
        block.sync(body_for("sp"))
        self.stats = stats

import contextlib

EPS = 1e-6
NT_L = 16
NT = 18
W0C = 3456
W1C = 1600
NDUMMY = 0


def build(nlayers=2):
    nc = bass.Bass("TRN2", target_bir_lowering=False)

    def din(name, shape):
        return nc.dram_tensor(name, list(shape), F32, kind="ExternalInput").ap()

    x_d = din("x", [2048, 1024]); ctx_d = din("ctx", [256, 1024]); cc_d = din("cc", [128, 8, 2])
    wada_d = din("w_ada", [2, 1024, 3072]); bada_d = din("b_ada", [2, 3072]); badac_d = din("b_ada_col", [128, 2, 16])
    ng_d = din("norm_g_col", [128, 2, 8]); fg_d = din("final_g", [1, 1024])
    w0_d = din("W0", [1024, W0C]); wout0_d = din("wout0", [1024, 1024])
    sink_d = din("sink_rep", [1, 2, 512]); lng_d = din("ln_g", [1, 512]); lnb_d = din("ln_b", [1, 512])
    wst_d = din("WsT", [128, 8, 128]); bs_d = din("bs", [8, 128]); bones_d = din("blockones", [8, 512])
    mask_d = din("masks", [128, 2, 512]); cos0_d = din("cos0", [128, 2048]); sin0_d = din("sin0", [128, 2048])
    w1_d = din("W1", [1024, W1C]); qn_d = din("qn_col", [128, 2]); kvn_d = din("kvn_col", [128, 1])
    wqb_d = din("Wqb", [256, 3072]); wkvb_d = din("Wkvb", [128, 2048]); wout1_d = din("wout1", [1024, 1024])
    cos1_d = din("cos1", [128, 2048]); sin1_d = din("sin1", [128, 2048])
    if nlayers == 2:
        out_d = nc.dram_tensor("out", [2048, 1024], F32, kind="ExternalOutput").ap()
        xs_d = nc.dram_tensor("xs", [2048, 1024], F32, kind="Internal").ap()
        xcs_d = nc.dram_tensor("xcs", [256, 1024], F32, kind="Internal").ap()
    else:
        xs_d = nc.dram_tensor("xs", [2048, 1024], F32, kind="ExternalOutput").ap()
        xcs_d = nc.dram_tensor("xcs", [256, 1024], F32, kind="ExternalOutput").ap()

    S = Sched(nc)
    with contextlib.ExitStack() as st:
        def sb(name, shape, dt):
            return st.enter_context(nc.sbuf_tensor("s_" + name, list(shape), dt))

        def ps(name, shape, dt):
            return st.enter_context(nc.psum_tensor("p_" + name, list(shape), dt))

        Tb = [ps("T0", [128, 1024], BF16), ps("T1", [128, 1024], BF16)]
        PP = ps("PP", [128, 1024], F32)
        Pb = [PP[:, 0:512], PP[:, 512:1024]]
        Ab = ps("A", [128, 1024], F32)
        Ob = ps("O", [128, 1024], F32)
        pctr = [0]

        pring = [(Pb[0], ("P", 0)), (Pb[1], ("P", 1))]

        def pnext():
            i = pctr[0] % len(pring)
            pctr[0] += 1
            return pring[i]


        def MM(out, lhsT, rhs, start, stop, reads, writes):
            return S.add("pe", lambda e: e.matmul(out, lhsT=lhsT, rhs=rhs, start=start, stop=stop), reads, writes)

        def TR(out, in_, reads, writes):
            return S.add("pe", lambda e: e.transpose(out, in_, ident[:]), list(reads) + ["ident"], writes)

        def ACT(out, in_, func, reads, writes, **kw):
            return S.add("act", lambda e: e.activation(out=out, in_=in_, func=func, **kw), reads, writes)

        def TT(eng, out, in0, in1, op, reads, writes):
            return S.add(eng, lambda e: e.tensor_tensor(out=out, in0=in0, in1=in1, op=op), reads, writes)

        def TS(eng, out, in0, s1, s2, op0, op1, reads, writes):
            if s2 is None:
                return S.add(eng, lambda e: e.tensor_scalar(out=out, in0=in0, scalar1=s1, scalar2=None, op0=op0), reads, writes)
            return S.add(eng, lambda e: e.tensor_scalar(out=out, in0=in0, scalar1=s1, scalar2=s2, op0=op0, op1=op1), reads, writes)

        def STT(eng, out, in0, scalar, in1, op0, op1, reads, writes):
            return S.add(eng, lambda e: e.scalar_tensor_tensor(out=out, in0=in0, scalar=scalar, in1=in1, op0=op0, op1=op1), reads, writes)

        def CP(eng, out, in_, reads, writes):
            return S.add(eng, lambda e: e.tensor_copy(out=out, in_=in_), reads, writes)

        def RCP(out, in_, reads, writes):
            return S.add("dve", lambda e: e.reciprocal(out=out, in_=in_), reads, writes)

        def MSET(ap, val, reads, writes):
            return S.add("pool", lambda e: e.memset(ap, val), reads, writes)

        ident = sb("ident", [128, 128], BF16)
        ones_bf = sb("ones_bf", [128, 128], BF16)
        negh = sb("negh", [128, 1], F32)
        cc = sb("cc", [128, 8, 2], F32)
        cond = sb("cond", [128, 8, 2], BF16)
        cond_bc = sb("cond_bc", [128, 2, 8, 128], BF16)
        badac = sb("badac", [128, 2, 16], F32)
        ngc = sb("ngc", [128, 2, 8], F32)
        modc = sb("modc", [128, 16, 2], F32)
        gmodL = [sb("gmod0", [128, 8, 2], F32), sb("gmod1", [128, 8, 2], F32)]
        shiftL = [sb("shiftc0", [128, 8, 2], F32), sb("shiftc1", [128, 8, 2], F32)]
        gate_bc = sb("gate_bc", [128, 2, 1024], F32)
        bg_bc = sb("bg_bc", [128, 1024], F32)
        wada = [sb("wada0", [128, 8, 512], BF16)] * 2
        NXR = 3
        xring = [sb(f"xr{i}", [128, 1024], F32) for i in range(NXR)]
        xctr = [0]
        junk = sb("junk", [128, 1024], BF16)
        ss = sb("ss", [128, 4], F32)
        std = sb("std", [128, 4], F32)
        rstd = sb("rstd", [128, 4], F32)
        xn = sb("xn", [128, 2, 1024], BF16)
        hT = [sb("hT0", [128, 8, 256], BF16)] * 2
        tmpA = sb("tmpA", [128, 1024], F32)
        tmpB = sb("tmpB", [128, 1024], F32)
        xnew = [sb("xnew0", [128, 1024], F32)] * 2
        recb = sb("recb", [128, 2, 512], F32)
        rec2 = recb
        pTw = [sb(f"pTw{i}", [128, 1024], BF16) for i in range(2)]
        pT = [pTw[0][:, 0:512], pTw[0][:, 512:1024], pTw[1][:, 0:512], pTw[1][:, 512:1024]]
        pTc = [0]
        cosT = sb("cosT", [128, 2048], BF16)
        sinT = sb("sinT", [128, 2048], BF16)

        MSET(ident[:], 1.0, [], ["ident"])
        S.add("pool", lambda e: e.affine_select(out=ident[:], in_=ident[:], pattern=[[-1, 128]],
                                                compare_op=ALU.is_equal, fill=0.0, base=0, channel_multiplier=1),
              reads=["ident"], writes=["ident"])
        MSET(ones_bf[:], 1.0, [], ["ones_bf"])
        MSET(negh[:], -0.5, [], ["negh"])
        S.dma("sp", cc[:], cc_d, writes=["cc"])
        S.dma("sp", badac[:], badac_d, writes=["badac"])
        S.dma("sp", ngc[:], ng_d, writes=["ngc"])
        ACT(cond[:], cc[:], AF.Silu, ["cc"], ["cond"])
        for j in range(2):
            for kc in range(8):
                CP("dve", cond_bc[:, j, kc, :], cond[:, kc, j:j + 1].to_broadcast([128, 128]), ["cond"], [("cond_bc", j)])

        def adaln_pieces(L, need_gate_c, gate_slot):
            wv = wada_d[L].rearrange("(kc p) n -> p kc n", p=128)
            wt = wada[0]
            wk = ("wada", 0)
            gm, sh = gmodL[L], shiftL[L]
            pieces = []
            for blk in range(6):
                def dma_piece(blk=blk):
                    S.dma("pool", wt[:], wv[:, :, blk * 512:(blk + 1) * 512], writes=[wk])
                    if blk == 4:
                        S.dma("sp", bg_bc[:], bada_d[L:L + 1, 2048:3072].partition_broadcast(128), writes=["bg_bc"])

                def mm_piece(blk=blk):
                    if blk < 4:
                        pt, pk = pnext()
                        for mm_ in range(4):
                            for kc in range(8):
                                MM(pt[:, mm_ * 2:mm_ * 2 + 2], wt[:, kc, mm_ * 128:(mm_ + 1) * 128], cond[:, kc, :], kc == 0, kc == 7,
                                   [wk, "cond"], [pk])
                        pv = pt[:, 0:8].rearrange("p (m j) -> p m j", j=2)
                        for j in range(2):
                            TT("dve", modc[:, blk * 4:blk * 4 + 4, j], pv[:, :, j], badac[:, L, blk * 4:blk * 4 + 4], ALU.add,
                               [pk, "badac"], [("modc", blk, j)])
                        if blk == 3:
                            for j in range(2):
                                mk = [("modc", b_, j) for b_ in range(4)]
                                TS("dve", gm[:, :, j], modc[:, 8:16, j], 1.0, None, ALU.add, None, mk, [("gmod", L, j)])
                                TT("dve", gm[:, :, j], gm[:, :, j], ngc[:, L, :], ALU.mult, [("gmod", L, j), "ngc"], [("gmod", L, j)])
                                CP("dve", sh[:, :, j], modc[:, 0:8, j], mk, [("shiftc", L, j)])
                    else:
                        n = blk - 4
                        for j in range(2 if need_gate_c else 1):
                            pt, pk = pnext()
                            for kc in range(8):
                                MM(pt[:, :], cond_bc[:, j, kc, :], wt[:, kc, :], kc == 0, kc == 7, [wk, ("cond_bc", j)], [pk])
                            TT("dve", gate_bc[:, gate_slot[j], n * 512:(n + 1) * 512], pt[:, :], bg_bc[:, n * 512:(n + 1) * 512], ALU.add,
                               [pk, "bg_bc"], [("gate_bc", gate_slot[j], n)])
                pieces.append((dma_piece, mm_piece))
            return pieces

        normed = set()

        preloaded = {}

        def preload_x(tag, srcs, src_keys, queue="act"):
            if tag in preloaded or tag in normed:
                return
            xts = []
            for src, sk in zip(srcs, src_keys):
                xi = xctr[0] % NXR
                xctr[0] += 1
                xt = xring[xi]
                xk = ("xr", xi)
                xts.append((xt, xk))
                S.dma(queue, xt[:], src, reads=[sk], writes=[xk])
            preloaded[tag] = xts

        def ensure_norm(tag, srcs, src_keys):
            if tag in normed:
                return
            normed.add(tag)
            stage1_norm(srcs, src_keys, preloaded.pop(tag, None))

        def stage1(srcs, src_keys, j, hTt, hkey, L=0, tag=None):
            ensure_norm(tag if tag is not None else ("anon", len(normed)), srcs, src_keys)
            stage1_tr(len(srcs), j, hTt, hkey, L)

        def stage1_norm(srcs, src_keys, pre=None):
            nt = len(srcs)
            xts = []
            for i, (src, sk) in enumerate(zip(srcs, src_keys)):
                if pre is not None:
                    xt, xk = pre[i]
                else:
                    xi = xctr[0] % NXR
                    xctr[0] += 1
                    xt = xring[xi]
                    xk = ("xr", xi)
                    S.dma("sp", xt[:], src, reads=[sk], writes=[xk])
                xts.append((xt, xk))
                ACT(junk[:], xt[:], AF.Square, [xk], ["junk", ("ss", i)], accum_out=ss[:, i:i + 1])
            ACT(std[:, 0:nt], ss[:, 0:nt], AF.Sqrt, [("ss", i) for i in range(nt)], ["std"], scale=1.0 / 1024, bias=EPS)
            RCP(rstd[:, 0:nt], std[:, 0:nt], ["std"], ["rstd"])
            for i, (xt, xk) in enumerate(xts):
                TS("dve", xn[:, i, :], xt[:], rstd[:, i:i + 1], None, ALU.mult, None, [xk, "rstd"], [("xn", i)])

        def stage1_tr(nt, j, hTt, hkey, L):
            for pr in range(4):
                tb = Tb[pr % 2]
                tk = ("T", pr % 2)
                for kc in (2 * pr, 2 * pr + 1):
                    off = (kc % 2) * 512
                    for i in range(nt):
                        TR(tb[:, off + i * 128: off + (i + 1) * 128], xn[:, i, kc * 128:(kc + 1) * 128], [("xn", i)], [tk])
                for kc in (2 * pr, 2 * pr + 1):
                    off = (kc % 2) * 512
                    TS("dve", hTt[:, kc, 0:nt * 128], tb[:, off:off + nt * 128], gmodL[L][:, kc, j:j + 1], shiftL[L][:, kc, j:j + 1], ALU.mult, ALU.add,
                       [tk, ("gmod", L, j), ("shiftc", L, j)], [hkey])

        def proj_fm(hTt, hkey, tok0, ntok, wt, wkey, col0, M, nkc=8):
            pt, pk = pnext()
            for kc in range(nkc):
                MM(pt[0:M, 0:ntok], wt[:, kc, col0:col0 + M], hTt[:, kc, tok0:tok0 + ntok], kc == 0, kc == nkc - 1, flat([hkey, wkey]), [pk])
            return pt, pk

        def flat(l):
            o = []
            for a in l:
                if isinstance(a, list):
                    o.extend(a)
                else:
                    o.append(a)
            return o

        octr = [0]

        def outproj_residual(yT, ykey, tok0, wo, wokey, xsrc, xsrc_key, j, dst, dst_key, final=False, alt=None):
            for n in range(2):
                for kc in range(8):
                    MM(Ob[:, n * 512:(n + 1) * 512], yT[:, kc, tok0:tok0 + 128], wo[:, kc, n * 512:(n + 1) * 512], kc == 0, kc == 7,
                       list(ykey) + [wokey], [("O", n)])
            xi = xctr[0] % NXR
            xctr[0] += 1
            xt = xring[xi]
            xk = ("xr", xi)
            S.dma("sp", xt[:], xsrc, reads=[xsrc_key], writes=[xk])
            if alt is None or alt[5] is None:
                tA, tAk = tmpA[:], ["tmpA", ("tmpo", 0), ("tmpo", 1)]
            else:
                tA, tAk = alt[5], alt[6]
            TT("dve", tA, Ob[:], gate_bc[:, j, :], ALU.mult, [("O", 0), ("O", 1), ("gate_bc", j, 0), ("gate_bc", j, 1)], tAk)
            xo = xnew[octr[0] % 2]
            xok = ("xnew", 0)
            octr[0] += 1
            if not final:
                TT("pool", xo[:], tA, xt[:], ALU.add, tAk + [xk], [xok])
                return S.dma("sp", dst, xo[:], reads=[xok], writes=[dst_key])
            if alt is None:
                tB, tBk, xo2, xo2k, c = tmpB[:], ["tmpB", "r1", "r2"], xo[:], [xok], 0
            else:
                tB, tBk, xo2, xo2k, c = alt[0:5]
            TT("pool", tB, tA, xt[:], ALU.add, tAk + [xk], tBk)
            ACT(junk[:], tB, AF.Square, tBk, ["junk", ("ss", c)], accum_out=ss[:, c:c + 1])
            ACT(std[:, c:c + 1], ss[:, c:c + 1], AF.Sqrt, [("ss", c)], [("std", c)], scale=1.0 / 1024, bias=EPS)
            RCP(rstd[:, c:c + 1], std[:, c:c + 1], [("std", c)], [("rstd", c)])
            STT("dve", xo2, tB, rstd[:, c:c + 1], bg_bc[:], ALU.mult, ALU.mult, tBk + [("rstd", c), "bg_bc"], xo2k)
            return S.dma("sp", dst, xo2, reads=xo2k, writes=[dst_key])

        actr = [0]
        pend_e2 = []

        def flush_e2():
            while pend_e2:
                pend_e2.pop(0)()

        def attn_core(q_ap, qkey, kts, A_idx_unused, ones_first, sink_mm, scale, o_ap, g_ap, t_view, gkey, ykey):
            A_idx = actr[0] % 2
            actr[0] += 1
            Aacc = Ab[:, A_idx * 512:(A_idx + 1) * 512]
            ak = ("A", A_idx)
            n = len(kts)

            def score(idx):
                kap, kkey, mask = kts[idx][0], kts[idx][1], kts[idx][4]
                pt, pk = pnext()
                MM(pt[:, :], kap, q_ap, True, mask is None, flat([kkey, qkey]), [pk])
                if mask is not None:
                    MM(pt[:, :], ident[:, :], mask, False, True, ["ident", "masks"], [pk])
                return pt, pk
            nxt = score(0)
            for idx, (kap, kkey, vap, vkey, mask) in enumerate(kts):
                pt, pk = nxt
                if idx + 1 < n:
                    nxt = score(idx + 1)
                pi = pTc[0] % 4
                pTc[0] += 1
                pt_sb = pT[pi]
                ptk = ("pT", pi)
                ACT(pt_sb, pt[:, :], AF.Exp, [pk], [ptk], scale=scale)
                last = (idx == n - 1) and sink_mm is None
                MM(Aacc, vap, pt_sb, idx == 0, last, flat([vkey, ptk]), [ak])
            if sink_mm is not None:
                MM(Aacc, sink_mm[0], sink_mm[1], False, True, ["sel", "esink"], [ak])
            oh = slice(64, 128) if ones_first else slice(0, 64)
            dh = slice(0, 64) if ones_first else slice(64, 128)
            flush_e2()
            RCP(recb[dh, A_idx, :], Aacc[dh, :], [ak], [("recb", A_idx)])
            S.dma("sp", rec2[oh, A_idx, :], recb[dh, A_idx, :], reads=[("recb", A_idx)], writes=[("rec2", A_idx)])

            def e2():
                TT("dve", tmpA[oh, A_idx * 512:(A_idx + 1) * 512], Aacc[oh, :], rec2[oh, A_idx, :], ALU.mult, [ak, ("rec2", A_idx)], [("tmpo", A_idx)])
                tv = tmpA[oh, A_idx * 512:(A_idx + 1) * 512]
                if t_view is not None:
                    tv = tv.rearrange("p (c t) -> p c t", c=4)
                TT("pool", o_ap, tv, g_ap, ALU.mult, flat([("tmpo", A_idx), gkey]), [ykey])
            pend_e2.append(e2)

        pair_banks = [(PP, [("P", 0), ("P", 1)]), (Ob, [("O", 0), ("O", 1)])]

        def attn_dual(chains, scale, acc=None):
            n = len(chains[0]["kts"])
            st_ = []
            for c, ch in enumerate(chains):
                A_idx = c
                a_ap, a_key = (Ab[:, A_idx * 512:(A_idx + 1) * 512], ("A", A_idx)) if acc is None else acc[c]
                st_.append({"A": a_ap, "ak": a_key, "A_idx": A_idx, "banks": pair_banks[c], "nxt": None})

            def score(c, idx):
                ch = chains[c]
                kap, kkey, mask = ch["kts"][idx][0], ch["kts"][idx][1], ch["kts"][idx][4]
                bt, bk = st_[c]["banks"]
                h_ = idx % 2
                pt, pk = bt[:, h_ * 512:(h_ + 1) * 512], bk[h_]
                MM(pt, kap, ch["q_ap"], True, mask is None, flat([kkey, ch["qkey"]]), [pk])
                if mask is not None:
                    MM(pt, ident[:, :], mask, False, True, ["ident", "masks"], [pk])
                return pt, pk
            for c in range(2):
                st_[c]["nxt"] = score(c, 0)
            for idx in range(n):
                cur = [st_[c]["nxt"] for c in range(2)]
                if idx + 1 < n:
                    for c in range(2):
                        st_[c]["nxt"] = score(c, idx + 1)
                for c in range(2):
                    ch = chains[c]
                    pt, pk = cur[c]
                    vap, vkey = ch["kts"][idx][2], ch["kts"][idx][3]
                    pi = pTc[0] % 4
                    pTc[0] += 1
                    pt_sb = pT[pi]
                    ptk = ("pT", pi)
                    ACT(pt_sb, pt, AF.Exp, [pk], [ptk], scale=scale)
                    MM(st_[c]["A"], vap, pt_sb, idx == 0, False, flat([vkey, ptk]), [st_[c]["ak"]])
            for c in range(2):
                ch = chains[c]
                MM(st_[c]["A"], ch["sink_mm"][0], ch["sink_mm"][1], False, True, ["sel", "esink"], [st_[c]["ak"]])
            flush_e2()
            for c in range(2):
                ch = chains[c]
                A_idx, Aacc, ak = st_[c]["A_idx"], st_[c]["A"], st_[c]["ak"]
                oh = slice(64, 128) if ch["ones_first"] else slice(0, 64)
                dh = slice(0, 64) if ch["ones_first"] else slice(64, 128)
                RCP(recb[dh, A_idx, :], Aacc[dh, :], [ak], [("recb", A_idx)])
                S.dma("sp", rec2[oh, A_idx, :], recb[dh, A_idx, :], reads=[("recb", A_idx)], writes=[("rec2", A_idx)])

                def e2(A_idx=A_idx, Aacc=Aacc, ak=ak, oh=oh, ch=ch):
                    TT("dve", tmpA[oh, A_idx * 512:(A_idx + 1) * 512], Aacc[oh, :], rec2[oh, A_idx, :], ALU.mult, [ak, ("rec2", A_idx)], [("tmpo", A_idx)])
                    tv = tmpA[oh, A_idx * 512:(A_idx + 1) * 512].rearrange("p (c t) -> p c t", c=4)
                    TT("pool", ch["o_ap"], tv, ch["g_ap"], ALU.mult, flat([("tmpo", A_idx), ch["gkey"]]), [ch["ykey"]])
                pend_e2.append(e2)

        def score_pair_of(kts, q_ap, qkey, j):
            bt, bk = pair_banks[pairc[0] % 2]
            pairc[0] += 1
            for t in range(2):
                kap, kkey = kts[2 * j + t][0], kts[2 * j + t][1]
                MM(bt[:, t * 512:(t + 1) * 512], kap, q_ap, True, True, flat([kkey, qkey]), [bk[t]])
            return bt, bk

        def attn_pairs(q_ap, qkey, kts, ones_first, scale, o_ap, g_ap, gkey, ykey, side=(), warm=None, pre=None, next_first=None):
            A_idx = actr[0] % 2
            actr[0] += 1
            Aacc = Ab[:, A_idx * 512:(A_idx + 1) * 512]
            ak = ("A", A_idx)
            n = len(kts)
            assert n % 2 == 0
            npair = n // 2
            def score_pair(j):
                return score_pair_of(kts, q_ap, qkey, j)
            nxt = pre if pre is not None else score_pair(0)
            handed = None
            for j in range(npair):
                bt, bk = nxt
                if j + 1 < npair:
                    nxt = score_pair(j + 1)
                elif next_first is not None:
                    handed = next_first()
                pi = pwc[0] % 2
                pwc[0] += 1
                pw = pTw[pi]
                pwk = [("pT", 2 * pi), ("pT", 2 * pi + 1)]
                ACT(pw[:], bt[:], AF.Exp, bk, pwk, scale=scale)
                for t in range(2):
                    idx = 2 * j + t
                    vap, vkey = kts[idx][2], kts[idx][3]
                    MM(Aacc, vap, pw[:, t * 512:(t + 1) * 512], idx == 0, idx == n - 1, flat([vkey, pwk[t]]), [ak])
                if j == 6:
                    flush_e2()
                ns = len(side)
                per = -(-ns // max(1, npair - 4))
                if j >= 2:
                    for pc_ in side[(j - 2) * per:(j - 1) * per]:
                        pc_()
                if warm is not None:
                    warm()
            oh = slice(64, 128) if ones_first else slice(0, 64)
            dh = slice(0, 64) if ones_first else slice(64, 128)
            RCP(recb[dh, A_idx, :], Aacc[dh, :], [ak], [("recb", A_idx)])
            S.dma("sp", rec2[oh, A_idx, :], recb[dh, A_idx, :], reads=[("recb", A_idx)], writes=[("rec2", A_idx)])

            def e2():
                TT("dve", tmpA[oh, A_idx * 512:(A_idx + 1) * 512], Aacc[oh, :], rec2[oh, A_idx, :], ALU.mult, [ak, ("rec2", A_idx)], [("tmpo", A_idx)])
                TT("pool", o_ap, tmpA[oh, A_idx * 512:(A_idx + 1) * 512], g_ap, ALU.mult, flat([("tmpo", A_idx), gkey]), [ykey])
            pend_e2.append(e2)
            return handed

        pairc = [0]
        pwc = [0]
        outs = []
        with contextlib.ExitStack() as st0:
            def sb0(name, shape, dt):
                return st0.enter_context(nc.sbuf_tensor("s0_" + name, list(shape), dt))
            w0 = sb0("w0", [128, 8, W0C], BF16)
            wo0 = sb0("wo0", [128, 8, 1024], BF16)
            wst = sb0("wst", [128, 8, 128], BF16)
            bs_t = sb0("bs_t", [8, 128], BF16)
            bones = sb0("bones", [8, 512], BF16)
            masks = sb0("masks", [128, 2, 512], BF16)
            lng = sb0("lng", [128, 512], F32)
            lnb = sb0("lnb", [128, 512], F32)
            sink_t = sb0("sink_t", [1, 2, 512], F32)
            esink = sb0("esink", [1, 2, 512], BF16)
            sel = sb0("sel", [1, 2, 128], BF16)
            kT = sb0("kT", [128, NT * 128], BF16)
            vaug = sb0("vaug", [128, NT, 2, 128], BF16)
            qT = [sb0("qT0", [128, 4, 256], BF16), sb0("qT1", [128, 4, 256], BF16)]
            sga = [sb0("sga0", [128, 4, 256], BF16), sb0("sga1", [128, 4, 256], BF16)]
            yT = [sb0("yT0", [128, 8, 256], BF16), sb0("yT1", [128, 8, 256], BF16)]
            gu2 = [sb0("gu", [128, 512], BF16), sb0("gu_b", [128, 512], BF16)]
            sgb2 = [sb0("sgb", [128, 512], BF16), sb0("sgb_b", [128, 512], BF16)]
            ugb2 = [sb0("ugb", [128, 512], BF16), sb0("ugb_b", [128, 512], BF16)]
            gv2 = [sb0("gv", [128, 512], F32), sb0("gv_b", [128, 512], F32)]
            vn2 = [sb0("vn", [128, 512], BF16), sb0("vn_b", [128, 512], BF16)]
            bgt = sb0("bgt", [128, 512], BF16)
            st42 = [sb0("st4", [128, 8], F32), sb0("st4_b", [128, 8], F32)]
            r1 = sb0("r1", [128, 256], F32)
            r2 = sb0("r2", [128, 256], F32)

            w0v = w0_d.rearrange("(kc p) n -> p kc n", p=128)
            for (c0, c1) in [(0, 1280), (1280, 2432), (2432, W0C)]:
                S.dma("pool", w0[:, :, c0:c1], w0v[:, :, c0:c1], writes=[("w0", c0)])
            S.add("pool", None, reads=[("w0", 0), ("w0", 1280), ("w0", 2432)], writes=["w0"])
            for dp_, mp_ in adaln_pieces(0, True, (0, 1)):
                dp_()
                mp_()
            S.dma("pool", wo0[:], wout0_d.rearrange("(kc p) n -> p kc n", p=128), writes=["wo0"])
            S.dma("pool", wst[:], wst_d, writes=["wst"])
            S.dma("pool", bs_t[:], bs_d, writes=["bs_t"])
            S.dma("pool", bones[:], bones_d, writes=["bones"])
            S.dma("pool", masks[:], mask_d, writes=["masks"])
            S.dma("sp", lng[:], lng_d.partition_broadcast(128), writes=["lng"])
            S.dma("sp", lnb[:], lnb_d.partition_broadcast(128), writes=["lnb"])
            S.dma("sp", sink_t[:], sink_d, writes=["sink_t"])
            S.dma("pool", cosT[:], cos0_d, writes=["cosT"])
            S.dma("pool", sinT[:], sin0_d, writes=["sinT"])
            ACT(esink[:], sink_t[:], AF.Exp, ["sink_t"], ["esink"])
            MSET(sel[:], 0.0, [], ["sel"])
            MSET(sel[:, 0, 64:128], 1.0, ["sel"], ["sel"])
            MSET(sel[:, 1, 0:64], 1.0, ["sel"], ["sel"])
            MSET(vaug[:, :, 0, 64:128], 1.0, [], ["vaug_ones"])
            MSET(vaug[:, :, 1, 0:64], 1.0, [], ["vaug_ones"])

            import os
            STOP = int(os.environ.get("KSTOP", "0"))

            def l0_stage(k, tiles, is_ctx, part="ab"):
                par = k % 2
                nt = len(tiles)
                ntok = nt * 128
                hTt = hT[par]
                hkey = ("hT", 0)

                def tmg(i, col0, n):
                    pt, pk = pnext()
                    for kc in range(8):
                        MM(pt[:, 0:n], hTt[:, kc, i * 128:(i + 1) * 128], w0[:, kc, col0:col0 + n], kc == 0, kc == 7, [hkey, "w0"], [pk])
                    return pt, pk
                if "a" in part:
                    l0_stage_a(k, tiles, is_ctx, par, nt, ntok, hTt, hkey, tmg)
                if "b" in part:
                    sub = "".join(ch for ch in part if ch in "12") or "12"
                    l0_stage_b(k, tiles, is_ctx, par, nt, ntok, hTt, hkey, tmg, sub)

            def l0_stage_a(k, tiles, is_ctx, par, nt, ntok, hTt, hkey, tmg):
                if is_ctx:
                    srcs = [ctx_d[i * 128:(i + 1) * 128, :] for i in range(2)]
                else:
                    srcs = [x_d[t * 128:(t + 1) * 128, :] for t in tiles]
                stage1(srcs, [("xin", t) for t in tiles], 1 if is_ctx else 0, hTt, hkey, tag=("L0", k))
                if STOP in (12, 122):
                    return
                tok0 = 0 if is_ctx else tiles[0] * 128
                ktok0 = tiles[0] * 128
                for c in range(5):
                    col = c * 128 if c < 4 else 1024
                    rcol = 512 + c * 128 if c < 4 else 1152
                    dst = qT[par][:, c, 0:ntok] if c < 4 else kT[:, ktok0:ktok0 + ntok]
                    dkey = ("qT", par) if c < 4 else ("kT", k)
                    pt, pk = proj_fm(hTt, hkey, 0, ntok, w0, "w0", col, 128)
                    if is_ctx:
                        ACT(dst, pt[:, 0:ntok], AF.Copy, [pk], [dkey])
                    else:
                        TT("dve", r1[:, 0:ntok], pt[:, 0:ntok], cosT[:, tok0:tok0 + ntok], ALU.mult, [pk, "cosT"], ["r1"])
                        pt2, pk2 = proj_fm(hTt, hkey, 0, ntok, w0, "w0", rcol, 128)
                        TT("dve", r2[:, 0:ntok], pt2[:, 0:ntok], sinT[:, tok0:tok0 + ntok], ALU.mult, [pk2, "sinT"], ["r2"])
                        TT("dve", dst, r1[:, 0:ntok], r2[:, 0:ntok], ALU.add, ["r1", "r2"], [dkey])
                for c in range(4):
                    pt, pk = proj_fm(hTt, hkey, 0, ntok, w0, "w0", 1280 + c * 128, 128)
                    ACT(sga[par][:, c, 0:ntok], pt[:, 0:ntok], AF.Silu, [pk], [("sga", par)])
                for i, t in enumerate(tiles):
                    pt, pk = tmg(i, 1792, 128)
                    ACT(vaug[:, t, 0, 0:64], pt[:, 0:64], AF.Copy, [pk], [("v", t, 0)])
                    ACT(vaug[:, t, 1, 64:128], pt[:, 64:128], AF.Copy, [pk], [("v", t, 1)])

            def l0_stage_b(k, tiles, is_ctx, par, nt, ntok, hTt, hkey, tmg, sub="12"):
                if "1" in sub:
                    l0_stage_b1(k, tiles, is_ctx, par, nt, ntok, hTt, hkey, tmg)
                if "2" in sub:
                    l0_stage_b2(k, tiles, is_ctx, par, nt, ntok, hTt, hkey, tmg)

            def l0_stage_b1(k, tiles, is_ctx, par, nt, ntok, hTt, hkey, tmg):
                for i, t in enumerate(tiles):
                    gu, sgb, ugb, gv, st4 = gu2[i], sgb2[i], ugb2[i], gv2[i], st42[i]
                    pt, pk = tmg(i, 2944, 512)
                    ACT(sgb[:], pt[:, :], AF.Silu, [pk], [("sgb", i)])
                    pt, pk = tmg(i, 1920, 512)
                    ACT(gu[:], pt[:, :], AF.Gelu, [pk], [("gu", i)])
                    TT("pool", ugb[:], gu[:], sgb[:], ALU.mult, [("gu", i), ("sgb", i)], [("ugb", i)])
                    pt, pk = tmg(i, 2432, 512)
                    ACT(gv[:], pt[:, :], AF.Gelu, [pk], [("gv", i), ("st4a", i)], accum_out=st4[:, 0:1])
                for i, t in enumerate(tiles):
                    gu, sgb, ugb, gv, st4 = gu2[i], sgb2[i], ugb2[i], gv2[i], st42[i]
                    ACT(junk[:, 0:512], gv[:], AF.Square, [("gv", i)], ["junk", ("st4b", i)], accum_out=st4[:, 1:2])
                    TS("dve", st4[:, 2:4], st4[:, 0:2], 1.0 / 512, None, ALU.mult, None, [("st4a", i), ("st4b", i)], [("st4c", i)])
                    TT("dve", st4[:, 4:5], st4[:, 2:3], st4[:, 2:3], ALU.mult, [("st4c", i)], [("st4d", i)])
                    TT("dve", st4[:, 5:6], st4[:, 3:4], st4[:, 4:5], ALU.subtract, [("st4c", i), ("st4d", i)], [("st4e", i)])
                    ACT(st4[:, 6:7], st4[:, 5:6], AF.Sqrt, [("st4e", i)], [("st4f", i)], scale=1.0, bias=EPS)
                    RCP(st4[:, 7:8], st4[:, 6:7], [("st4f", i)], [("st4g", i)])
                    STT("dve", gv[:], gv[:], st4[:, 2:3], lng[:], ALU.subtract, ALU.mult, [("gv", i), ("st4c", i), "lng"], [("gv", i)])
                    STT("dve", vn2[i][:], gv[:], st4[:, 7:8], lnb[:], ALU.mult, ALU.add, [("gv", i), ("st4g", i), "lnb"], [("vn", i)])

            def l0_stage_b2(k, tiles, is_ctx, par, nt, ntok, hTt, hkey, tmg):
                for i, t in enumerate(tiles):
                    ugb, vn = ugb2[i], vn2[i]
                    pt, pk = pnext()
                    MM(pt[:, :], bs_t[:, :], bones[:, :], True, False, ["bs_t", "bones"], [pk])
                    for g in range(8):
                        MM(pt[:, g * 64:(g + 1) * 64], wst[:, g, :], vn[:, g * 64:(g + 1) * 64], False, g == 7, ["wst", ("vn", i)], [pk])
                    TT("dve", bgt[:], pt[:, :], ugb[:], ALU.mult, [pk, ("ugb", i)], ["bgt"])
                    tb = Tb[i % 2]
                    tk = ("T", i % 2)
                    for jj in range(4):
                        TR(tb[:, jj * 128:(jj + 1) * 128], bgt[:, jj * 128:(jj + 1) * 128], ["bgt"], [tk])
                    ACT(yT[par][:, 4:8, i * 128:(i + 1) * 128], tb[:, 0:512].rearrange("p (c t) -> p c t", c=4), AF.Copy, [tk], [("yTb", par, i)])

            def l0_attn(k, tiles, is_ctx, part="ao"):
                par = k % 2
                if "a" in part:
                    l0_attn_a(k, tiles, is_ctx, par)
                if "o" in part:
                    l0_attn_o(k, tiles, is_ctx, par)

            def l0_attn_o(k, tiles, is_ctx, par):
                flush_e2()
                for i, t in enumerate(tiles):
                    if is_ctx:
                        src, sk, dst, dk, j = ctx_d[i * 128:(i + 1) * 128, :], ("xin", t), xcs_d[i * 128:(i + 1) * 128, :], ("xs", t), 1
                    else:
                        src, sk, dst, dk, j = x_d[t * 128:(t + 1) * 128, :], ("xin", t), xs_d[t * 128:(t + 1) * 128, :], ("xs", t), 0
                    outs.append(outproj_residual(yT[par], [("yTa", par, i, 0), ("yTa", par, i, 1), ("yTb", par, i)], i * 128, wo0, "wo0", src, sk, j, dst, dk))

            def l0_attn_a(k, tiles, is_ctx, par):
                for i, t in enumerate(tiles):
                    if is_ctx:
                        kt_ids = [(16, None), (17, None)]
                    else:
                        kt_ids = [(16, None), (17, None)]
                        if t > 0:
                            kt_ids.append((t - 1, 0))
                        kt_ids.append((t, None))
                        if t < NT_L - 1:
                            kt_ids.append((t + 1, 1))
                    chains = []
                    for hk in range(2):
                        hs = slice(hk * 64, (hk + 1) * 64)
                        kts = []
                        for (kt, m) in kt_ids:
                            kslab = 0 if kt >= 16 else 1 + kt // 2
                            kts.append((kT[hs, kt * 128:(kt + 1) * 128], ("kT", kslab), vaug[:, kt, hk, :], [("v", kt, hk), "vaug_ones"],
                                        None if m is None else masks[:, m, :]))
                        oh = slice(64, 128) if hk == 1 else slice(0, 64)
                        chains.append(dict(q_ap=qT[par][hs, 0:4, i * 128:(i + 1) * 128], qkey=("qT", par), kts=kts, ones_first=(hk == 1),
                                           sink_mm=(sel[0:1, hk, :], esink[0:1, hk, :]),
                                           o_ap=yT[par][oh, 0:4, i * 128:(i + 1) * 128], g_ap=sga[par][oh, 0:4, i * 128:(i + 1) * 128],
                                           gkey=("sga", par), ykey=("yTa", par, i, hk)))
                    acc = None if i % 2 == 0 else [(Tb[0][:, :].bitcast(F32), ("T", 0)), (Tb[1][:, :].bitcast(F32), ("T", 1))]
                    attn_dual(chains, 0.125, acc)

            seq = [(0, [16, 17], True)] + [(1 + s, [2 * s, 2 * s + 1], False) for s in range(8)]
            def fin():
                S.barrier()
                with nc.Block() as block:
                    S.emit(block, st)
                print("stats", S.stats, flush=True)
                return nc
            if STOP == 1:
                return fin()
            l0_stage(*seq[0])
            if STOP in (2, 12, 13, 122):
                return fin()
            l0_attn(*seq[0])
            if STOP == 3:
                return fin()
            l0_stage(*seq[1])
            ada1 = adaln_pieces(1, False, (1, 1))

            def prefetch_norm0(k):
                if k > 8:
                    return
                tiles_ = seq[k][1]
                ensure_norm(("L0", k), [x_d[t * 128:(t + 1) * 128, :] for t in tiles_], [("xin", t) for t in tiles_])
            prefetch_norm0(2)
            for s in range(1, 9):
                if 2 <= s <= 7:
                    ada1[s - 2][0]()
                if s + 1 <= 8:
                    l0_stage(*seq[s + 1], part="a")
                if s + 2 <= 8:
                    tiles_p = seq[s + 2][1]
                    preload_x(("L0", s + 2), [x_d[t * 128:(t + 1) * 128, :] for t in tiles_p], [("xin", t) for t in tiles_p])
                l0_attn(*seq[s], part="a")
                prefetch_norm0(s + 2)
                flush_e2()
                if s + 1 <= 8:
                    l0_stage(*seq[s + 1], part="b1")
                l0_attn(*seq[s], part="o")
                if s + 1 <= 8:
                    l0_stage(*seq[s + 1], part="b2")
                if 2 <= s <= 7:
                    ada1[s - 2][1]()
            S.barrier()

        if nlayers == 1:
            S.add("sp", None, extra=outs)
            with nc.Block() as block:
                S.emit(block, st)
            print("stats", S.stats, flush=True)
            return nc

        with contextlib.ExitStack() as st1:
            def sb1(name, shape, dt):
                return st1.enter_context(nc.sbuf_tensor("s1_" + name, list(shape), dt))
            w1 = sb1("w1", [128, 8, W1C], BF16)
            wqb = sb1("wqb", [128, 2, 3072], BF16)
            wkvb = sb1("wkvb", [128, 2048], BF16)
            wo1 = sb1("wo1", [128, 8, 1024], BF16)
            qnc = sb1("qnc", [128, 2], F32)
            kvnc = sb1("kvnc", [128, 1], F32)
            sg = sb1("sg", [128, 8, 2048], BF16)
            qan = sb1("qan", [128, 2, 2048], BF16)
            kvn = sb1("kvn", [128, NT * 128], BF16)
            kpe = sb1("kpe", [96, NT * 128], BF16)
            kTh = [sb1("kTh0", [96, NT * 128], BF16), sb1("kTh1", [96, NT * 128], BF16)]
            vah = [sb1("vah0", [128, NT, 128], BF16), sb1("vah1", [128, NT, 128], BF16)]
            qTh = [sb1("qTh0", [96, 512], BF16), sb1("qTh1", [96, 512], BF16)]
            sq = sb1("sq", [128, 2, 256], BF16)
            qa32 = sb1("qa32", [128, 2, 256], F32)
            sdb = sb1("sdb", [128, 256], F32)
            r1 = tmpB[:, 0:512]
            r2 = tmpB[:, 512:1024]

            S.dma("pool", w1[:], w1_d.rearrange("(kc p) n -> p kc n", p=128), writes=["w1"])
            S.dma("pool", wqb[:], wqb_d.rearrange("(kc p) n -> p kc n", p=128), writes=["wqb"])
            S.dma("pool", wkvb[:], wkvb_d, writes=["wkvb"])
            S.dma("pool", wo1[:], wout1_d.rearrange("(kc p) n -> p kc n", p=128), writes=["wo1"])
            S.dma("sp", qnc[:], qn_d, writes=["qnc"])
            S.dma("sp", kvnc[:], kvn_d, writes=["kvnc"])
            S.dma("pool", cosT[:], cos1_d, writes=["cosT"])
            S.dma("pool", sinT[:], sin1_d, writes=["sinT"])
            MSET(vah[0][:, :, 64:128], 1.0, [], ["vah_ones"])
            MSET(vah[1][:, :, 0:64], 1.0, [], ["vah_ones"])

            sqk = sb1("sqk", [128, 1, 256], BF16)
            qa32k = sb1("qa32k", [128, 1, 256], F32)

            def rms_evac(pts, ntok, sqt, q32t, tag):
                for c, (pt, pk) in enumerate(pts):
                    CP("dve", q32t[:, c, 0:ntok], pt[:, 0:ntok], [pk], [("qa32", tag, c)])
                    ACT(sqt[:, c, 0:ntok], q32t[:, c, 0:ntok], AF.Square, [("qa32", tag, c)], [("sq", tag, c)])

            def rms_finish(n, ntok, sqt, q32t, tag, gcol, gkey, dsts, dkey, nfeat):
                pt2, pk2 = pnext()
                for c in range(n):
                    MM(pt2[:, 0:ntok], ones_bf[:, :], sqt[:, c, 0:ntok], c == 0, c == n - 1, ["ones_bf", ("sq", tag, c)], [pk2])
                ACT(sdb[:, 0:ntok], pt2[:, 0:ntok], AF.Sqrt, [pk2], ["sdb"], scale=1.0 / nfeat, bias=EPS)
                RCP(sdb[:, 0:ntok], sdb[:, 0:ntok], ["sdb"], ["sdb"])
                for c in range(n):
                    STT("dve", dsts[c], q32t[:, c, 0:ntok], gcol[:, c:c + 1], sdb[:, 0:ntok], ALU.mult, ALU.mult,
                        [("qa32", tag, c), "sdb", gkey], [(dkey, c)])

            def l1_stage(k, tiles, is_ctx):
                par = k % 2
                nt = len(tiles)
                ntok = nt * 128
                hTt = hT[par]
                hkey = ("hT", 0)
                if is_ctx:
                    srcs = [xcs_d[i * 128:(i + 1) * 128, :] for i in range(2)]
                else:
                    srcs = [xs_d[t * 128:(t + 1) * 128, :] for t in tiles]
                stage1(srcs, [("xs", t) for t in tiles], 1 if is_ctx else 0, hTt, hkey, L=1, tag=("L1", k))
                if k + 1 <= 8:
                    tiles_n = seq[k + 1][1]
                    ensure_norm(("L1", k + 1), [xs_d[t * 128:(t + 1) * 128, :] for t in tiles_n], [("xs", t) for t in tiles_n])
                ktok0 = tiles[0] * 128
                if not is_ctx:
                    ptsq = [proj_fm(hTt, hkey, 0, ntok, w1, "w1", c * 128, 128) for c in range(2)]
                    rms_evac(ptsq, ntok, sq, qa32, "q")
                pkv = [proj_fm(hTt, hkey, 0, ntok, w1, "w1", 256, 128)]
                rms_evac(pkv, ntok, sqk, qa32k, "k")
                pt, pk = proj_fm(hTt, hkey, 0, ntok, w1, "w1", 384, 96)
                if is_ctx:
                    ACT(kpe[64:96, ktok0:ktok0 + ntok], pt[64:96, 0:ntok], AF.Copy, [pk], [("kpe", k)])
                else:
                    TT("dve", r1[64:96, 0:ntok], pt[64:96, 0:ntok], cosT[64:96, ktok0:ktok0 + ntok], ALU.mult, [pk, "cosT"], ["r1"])
                    pt2, pk2 = proj_fm(hTt, hkey, 0, ntok, w1, "w1", 480, 96)
                    TT("dve", r2[64:96, 0:ntok], pt2[64:96, 0:ntok], sinT[64:96, ktok0:ktok0 + ntok], ALU.mult, [pk2, "sinT"], ["r2"])
                    TT("pool", kpe[64:96, ktok0:ktok0 + ntok], r1[64:96, 0:ntok], r2[64:96, 0:ntok], ALU.add, ["r1", "r2"], [("kpe", k)])

                def gates(c0, c1):
                    for c in range(c0, c1):
                        pt, pk = proj_fm(hTt, hkey, 0, ntok, w1, "w1", 576 + c * 128, 128)
                        ACT(sg[:, c, ktok0:ktok0 + ntok], pt[:, 0:ntok], AF.Silu, [pk], [("sg", c, k)])
                if not is_ctx:
                    gates(0, 3)
                    rms_finish(2, ntok, sq, qa32, "q", qnc, "qnc", [qan[:, c, ktok0:ktok0 + ntok] for c in range(2)], ("qan", k), 256)
                    gates(3, 6)
                rms_finish(1, ntok, sqk, qa32k, "k", kvnc, "kvnc", [kvn[:, ktok0:ktok0 + ntok]], ("kvn", k), 128)
                if not is_ctx:
                    gates(6, 8)

            def fin1():
                S.barrier()
                S.add("sp", None, extra=outs)
                with nc.Block() as block:
                    S.emit(block, st)
                print("stats", S.stats, flush=True)
                return nc
            if STOP == 21:
                return fin1()
            seq = [(0, [16, 17], True)] + [(1 + s, [2 * s, 2 * s + 1], False) for s in range(8)]
            pring.extend([(Ob[:, 0:512], ("O", 0)), (Ob[:, 512:1024], ("O", 1))])
            for sq_ in seq:
                l1_stage(*sq_)
                if STOP == 22:
                    return fin1()
            del pring[2:]
            if STOP == 23:
                return fin1()
            allk = [(("kvn", k), 0) for k in range(9)]
            allkpe = [("kpe", k) for k in range(9)]
            slabs = [(16 * 128, 256)] + [(s * 512, 512) for s in range(4)]
            L1SCALE = 96 ** -0.5
            Xb = [Tb[0][:, :].bitcast(F32), Tb[1][:, :].bitcast(F32)]
            xpc = [0]

            def xnext():
                i = xpc[0] % 2
                xpc[0] += 1
                return Xb[i], ("T", i)

            def pe_keepwarm():
                for _ in range(NDUMMY):
                    S.add("pe", lambda e: e.matmul(Xb[1][:, :], lhsT=ident[:, :], rhs=cosT[:, 0:512], start=True, stop=True), ["ident", "cosT"], [("T", 1)])

            def expand_pieces(h):
                hp = h % 2
                kt_t = kTh[hp]
                va_t = vah[hp]
                pieces = []
                for (t0, n) in slabs:
                    def pc(t0=t0, n=n):
                        pt, pk = xnext()
                        MM(pt[0:64, 0:n], wkvb[:, h * 64:(h + 1) * 64], kvn[:, t0:t0 + n], True, True, allk + ["wkvb"], [pk])
                        CP("dve", kt_t[0:64, t0:t0 + n], pt[0:64, 0:n], [pk], [("kTh", hp)])
                    pieces.append(pc)
                pieces.append(lambda: S.dma("sp", kt_t[64:96, :], kpe[64:96, :], reads=allkpe, writes=[("kThp", hp)]))
                vs = slice(0, 64) if hp == 0 else slice(64, 128)
                for g0 in range(0, NT, 8):
                    def pv(g0=g0):
                        ng = min(8, NT - g0)
                        pt, pk = xnext()
                        for u in range(ng):
                            MM(pt[:, u * 64:(u + 1) * 64], kvn[:, (g0 + u) * 128:(g0 + u + 1) * 128], wkvb[:, 1024 + h * 64:1024 + (h + 1) * 64],
                               True, True, allk + ["wkvb"], [pk])
                        CP("dve", va_t[:, g0:g0 + ng, vs], pt[:, 0:ng * 64].rearrange("p (u d) -> p u d", d=64), [pk], [("vah", hp)])
                    pieces.append(pv)
                return pieces

            def qproj_pieces(h, s):
                q0 = s * 512
                qt = qTh[s % 2]
                qk = ("qTh", s % 2)
                qank = [(("qan", 1 + 2 * s + d_), c_) for d_ in range(2) for c_ in range(2)]

                def p1():
                    pt, pk = xnext()
                    for kc in range(2):
                        MM(pt[0:96, :], wqb[:, kc, h * 192:h * 192 + 96], qan[:, kc, q0:q0 + 512], kc == 0, kc == 1, qank + ["wqb"], [pk])
                    TT("dve", r1[64:96, :], pt[64:96, :], cosT[64:96, q0:q0 + 512], ALU.mult, [pk, "cosT"], ["r1"])
                    CP("dve", qt[0:64, :], pt[0:64, :], [pk, "r1"], [(qk, "n")])

                def p2():
                    pt2, pk2 = xnext()
                    for kc in range(2):
                        MM(pt2[0:96, :], wqb[:, kc, h * 192 + 96:h * 192 + 192], qan[:, kc, q0:q0 + 512], kc == 0, kc == 1, qank + ["wqb"], [pk2])
                    TT("dve", r2[64:96, :], pt2[64:96, :], sinT[64:96, q0:q0 + 512], ALU.mult, [pk2, "sinT"], ["r2"])
                    TT("pool", qt[64:96, :], r1[64:96, :], r2[64:96, :], ALU.add, ["r1", "r2"], [(qk, "p")])
                return [p1, p2]

            def block_args(h, s):
                hp = h % 2
                qt = qTh[s % 2]
                qk = ("qTh", s % 2)
                kt_t = kTh[hp]
                va_t = vah[hp]
                kts = [(kt_t[0:96, kt * 128:(kt + 1) * 128], [("kTh", hp), ("kThp", hp)], va_t[:, kt, :], [("vah", hp), "vah_ones"], None) for kt in range(NT)]
                return kts, qt[0:96, :], [(qk, "n"), (qk, "p")]

            def attn_block(h, s, side, pre, nxt_hs):
                hp = h % 2
                q0 = s * 512
                kts, q_ap, qkey = block_args(h, s)
                oh = slice(64, 128) if hp == 1 else slice(0, 64)
                gk = [("sg", h // 2, 1 + 2 * s), ("sg", h // 2, 2 + 2 * s)]
                nf = None
                if nxt_hs is not None:
                    nkts, nq_ap, nqkey = block_args(*nxt_hs)
                    nf = lambda: score_pair_of(nkts, nq_ap, nqkey, 0)
                return attn_pairs(q_ap, qkey, kts, hp == 1, L1SCALE,
                                  sg[oh, h // 2, q0:q0 + 512], sg[oh, h // 2, q0:q0 + 512], gk, ("og", h // 2, s, hp), side=side, warm=pe_keepwarm,
                                  pre=pre, next_first=nf)

            for pc_ in expand_pieces(0) + qproj_pieces(0, 0):
                pc_()
            handed_pair = None
            for h in range(16):
                nxt_exp = expand_pieces(h + 1) if h < 15 else []
                cuts = [0, 3, 5, 7, 9]
                for s in range(4):
                    side = list(nxt_exp[cuts[s]:cuts[s + 1]])
                    if s < 3:
                        side = qproj_pieces(h, s + 1) + side
                    elif h < 15:
                        side = side + qproj_pieces(h + 1, 0)
                    nxt_hs = (h, s + 1) if s < 3 else ((h + 1, 0) if h < 15 else None)
                    handed_pair = attn_block(h, s, side, handed_pair, nxt_hs)
            if STOP == 27:
                return fin1()
            flush_e2()
            S.dma("sp", bg_bc[:], fg_d.partition_broadcast(128), writes=["bg_bc"])
            altA = vah[0][:, :, :].rearrange("p a b -> p (a b)").bitcast(F32)[:, 0:1024]
            altB = vah[1][:, :, :].rearrange("p a b -> p (a b)").bitcast(F32)[:, 0:1024]
            altX = qan[:, :, :].rearrange("p a b -> p (a b)").bitcast(F32)[:, 0:1024]
            bufA = [(tmpA[:], ["tmpA", ("tmpo", 0), ("tmpo", 1)]), (altA, ["falA"])]
            bufB = [(tmpB[:], ["tmpB", "r1", "r2"]), (altB, ["falB"])]
            bufX = [(xnew[0][:], [("xnew", 0)]), (altX, ["falX"])]
            xts = {}

            def fin_A(t):
                s_ = t // 4
                ykey = [("og", c, s_, hp_) for c in range(8) for hp_ in range(2)]
                ot, okeys = pair_banks[t % 2]
                for n in range(2):
                    for kc in range(8):
                        MM(ot[:, n * 512:(n + 1) * 512], sg[:, kc, t * 128:(t + 1) * 128], wo1[:, kc, n * 512:(n + 1) * 512], kc == 0, kc == 7,
                           ykey + ["wo1"], [okeys[n]])
                xi = xctr[0] % NXR
                xctr[0] += 1
                xt = xring[xi]
                xk = ("xr", xi)
                xts[t] = (xt, xk)
                S.dma("sp", xt[:], xs_d[t * 128:(t + 1) * 128, :], reads=[("xs", t)], writes=[xk])
                tA, tAk = bufA[t % 2]
                TT("dve", tA, ot[:], gate_bc[:, 1, :], ALU.mult, okeys + [("gate_bc", 1, 0), ("gate_bc", 1, 1)], tAk)

            def fin_B(t):
                tA, tAk = bufA[t % 2]
                tB, tBk = bufB[t % 2]
                xt, xk = xts[t]
                c = t % 2
                TT("pool", tB, tA, xt[:], ALU.add, tAk + [xk], tBk)
                ACT(junk[:], tB, AF.Square, tBk, ["junk", ("ss", c)], accum_out=ss[:, c:c + 1])
                ACT(std[:, c:c + 1], ss[:, c:c + 1], AF.Sqrt, [("ss", c)], [("std", c)], scale=1.0 / 1024, bias=EPS)

            def fin_C(t):
                tB, tBk = bufB[t % 2]
                xo2, xo2k = bufX[t % 2]
                c = t % 2
                RCP(rstd[:, c:c + 1], std[:, c:c + 1], [("std", c)], [("rstd", c)])
                STT("dve", xo2, tB, rstd[:, c:c + 1], bg_bc[:], ALU.mult, ALU.mult, tBk + [("rstd", c), "bg_bc"], xo2k)
                outs.append(S.dma("sp", out_d[t * 128:(t + 1) * 128, :], xo2, reads=xo2k, writes=[("out", t)]))

            for t in range(NT_L):
                fin_A(t)
                fin_B(t)
                if t >= 1:
                    fin_C(t - 1)
            fin_C(NT_L - 1)
        S.add("sp", None, extra=outs)
        with nc.Block() as block:
            S.emit(block, st)
        print("stats", S.stats, flush=True)
    return nc

def _prep_shared(inp):
    f = np.float32
    g = lambda k: np.asarray(inp[k], dtype=f)
    sh = {}
    sh["w_ada"] = np.ascontiguousarray(g("w_ada"))
    b_ada = g("b_ada")
    sh["b_ada"] = np.ascontiguousarray(b_ada)
    sh["b_ada_col"] = np.ascontiguousarray(b_ada[:, :2048].reshape(2, 16, 128).transpose(2, 0, 1))
    sh["norm_g_col"] = np.ascontiguousarray(g("norm_g").reshape(2, 8, 128).transpose(2, 0, 1))
    sh["final_g"] = np.ascontiguousarray(g("final_g").reshape(1, 1024))
    w = g("w_in0")[0]
    aperm = np.concatenate([np.concatenate([np.arange(j * 64, j * 64 + 64), np.arange((j + 4) * 64, (j + 4) * 64 + 64)]) for j in range(4)])
    d = np.arange(64)
    src = np.where((d % 32) < 16, d + 16, d - 16)
    rot_a = (aperm // 64) * 64 + src[aperm % 64]
    kcols = 512 + np.arange(128)
    krot = 512 + (np.arange(128) // 64) * 64 + src[np.arange(128) % 64]
    cols = np.concatenate([aperm, rot_a, kcols, krot, 1792 + aperm, 640 + np.arange(128), 768 + np.arange(512), 1280 + np.arange(512), 1792 + 512 + np.arange(512)])
    sh["W0"] = np.ascontiguousarray(w[:, cols])
    wo = g("w_out0")[0]
    sh["wout0"] = np.ascontiguousarray(wo[np.concatenate([aperm, 512 + np.arange(512)]), :])
    sink = g("sink0")[0]
    sh["sink_rep"] = np.ascontiguousarray(np.repeat(sink.reshape(2, 4, 1), 128, axis=2).reshape(1, 2, 512))
    sh["ln_g"] = np.ascontiguousarray(g("gm_ln_g").reshape(1, 512))
    sh["ln_b"] = np.ascontiguousarray(g("gm_ln_b").reshape(1, 512))
    sh["WsT"] = np.ascontiguousarray(g("gm_ws")[0].transpose(2, 0, 1))
    sh["bs"] = np.ascontiguousarray(g("gm_bs")[0])
    bo = np.zeros((8, 512), f)
    for gi in range(8):
        bo[gi, gi * 64:(gi + 1) * 64] = 1.0
    sh["blockones"] = bo
    jj = np.arange(128)[:, None]
    ii = np.arange(128)[None, :]
    NEG = -30000.0
    m = np.stack([np.tile(np.where(jj >= ii, 0.0, NEG).astype(f), (1, 4)), np.tile(np.where(jj <= ii, 0.0, NEG).astype(f), (1, 4))], axis=1)
    sh["masks"] = np.ascontiguousarray(m)
    t = np.arange(2048)
    rowp = (t // 64).astype(np.float64)
    colp = (t % 64).astype(np.float64)
    p = np.arange(128)
    dd = p % 64
    inv = 10000.0 ** (-(dd % 16).astype(np.float64) / 16)
    pos = np.where((dd < 32)[:, None], rowp[None, :], colp[None, :])
    ang = (pos.astype(f) * inv.astype(f)[:, None]).astype(f)
    sgn = np.where((dd % 32) < 16, -1.0, 1.0)[:, None]
    sh["cos0"] = np.cos(ang).astype(f)
    sh["sin0"] = (np.sin(ang) * sgn).astype(f)
    w1 = g("w_in1")[0]
    d2 = np.arange(32)
    src2 = np.where((d2 % 16) < 8, d2 + 8, d2 - 8)
    z64 = np.zeros((1024, 64), f)
    sh["W1"] = np.ascontiguousarray(np.concatenate([w1[:, 0:384], z64, w1[:, 384:416], z64, w1[:, 384 + src2], w1[:, 416:1440]], axis=1))
    sh["qn_col"] = np.ascontiguousarray(g("q_norm")[0].reshape(2, 128).T)
    sh["kvn_col"] = np.ascontiguousarray(g("kv_norm")[0].reshape(1, 128).T)
    wq = g("w_qb")[0].reshape(256, 16, 96)
    z = np.zeros((256, 16, 64), f)
    sh["Wqb"] = np.ascontiguousarray(np.concatenate([wq, z, wq[:, :, 64 + src2]], axis=2).reshape(256, 3072))
    wk = g("w_kvb")[0].reshape(128, 16, 128)
    sh["Wkvb"] = np.ascontiguousarray(np.concatenate([wk[:, :, :64].reshape(128, 1024), wk[:, :, 64:].reshape(128, 1024)], axis=1))
    sh["wout1"] = np.ascontiguousarray(g("w_out1")[0])
    c1 = np.zeros((128, 2048), f)
    s1 = np.zeros((128, 2048), f)
    inv2 = 10000.0 ** (-(d2 % 8).astype(np.float64) / 8)
    pos2 = np.where((d2 < 16)[:, None], rowp[None, :], colp[None, :])
    ang2 = (pos2.astype(f) * inv2.astype(f)[:, None]).astype(f)
    sgn2 = np.where((d2 % 16) < 8, -1.0, 1.0)[:, None]
    c1[64:96] = np.cos(ang2)
    s1[64:96] = np.sin(ang2) * sgn2
    sh["cos1"] = c1
    sh["sin1"] = s1
    return sh


def _prep_core(inp, b):
    f = np.float32
    d = {}
    d["x"] = np.ascontiguousarray(np.asarray(inp["x"][b], dtype=f))
    d["ctx"] = np.ascontiguousarray(np.asarray(inp["ctx"][b], dtype=f))
    c = np.asarray(inp["c"][b], dtype=f).reshape(8, 128).T
    cx = np.asarray(inp["c_ctx"], dtype=f).reshape(8, 128).T
    d["cc"] = np.ascontiguousarray(np.stack([c, cx], axis=2))
    return d


_NC_CACHE = {}


def kernel(**inputs):
    sh = _prep_shared(inputs)
    if 2 not in _NC_CACHE:
        _NC_CACHE[2] = build(2)
    nc = _NC_CACHE[2]
    in_maps = []
    for b in range(8):
        m = dict(sh)
        m.update(_prep_core(inputs, b))
        in_maps.append(m)
    res = run_bass_kernel_spmd(nc, in_maps, core_ids=list(range(8)))
    return np.stack([np.asarray(r["out"], dtype=np.float32) for r in res.results], axis=0)
```

```python
import numpy as np
import concourse.bass as bass
import concourse.mybir as mybir
from concourse.bass_utils import run_bass_kernel_spmd

F32 = mybir.dt.float32
BF16 = mybir.dt.bfloat16
AF = mybir.ActivationFunctionType
ALU = mybir.AluOpType
AX = mybir.AxisListType

ENGS = ("pe", "act", "dve", "pool", "sp")
SAME_ENGINE_SYNC = True
N_DMA_SEMS = 6
SEM_K = 240
DMA_USES = 12
NOSYNC = ("pe",)
EMBED_WAIT = True


class Op:
    __slots__ = ("eng", "fn", "deps", "idx", "signaled", "is_dma", "dsem", "dval", "count", "prev_same_sem")

    def __init__(self, eng, fn, is_dma=False):
        self.eng = eng
        self.fn = fn
        self.deps = []
        self.idx = -1
        self.signaled = False
        self.is_dma = is_dma
        self.dsem = None
        self.dval = 0
        self.count = 0
        self.prev_same_sem = None


class Sched:
    def __init__(self, nc):
        self.nc = nc
        self.q = {e: [] for e in ENGS}
        self.last_w = {}
        self.readers = {}
        self.dma_count = {e: 0 for e in ENGS}
        self.dma_last = {}

    def _track(self, op, reads, writes):
        deps = []
        for r in reads:
            w = self.last_w.get(r)
            if w is not None:
                deps.append(w)
        for wkey in writes:
            w = self.last_w.get(wkey)
            if w is not None:
                deps.append(w)
            deps.extend(self.readers.get(wkey, ()))
        for r in reads:
            lst = self.readers.setdefault(r, [])
            if not op.is_dma:
                lst[:] = [o for o in lst if o.is_dma or o.eng != op.eng]
            lst.append(op)
        for wkey in writes:
            self.last_w[wkey] = op
            self.readers[wkey] = []
        best = {}
        keep = []
        seen = set()
        for d in deps:
            if d is op or id(d) in seen:
                continue
            seen.add(id(d))
            if d.is_dma:
                keep.append(d)
            else:
                b = best.get(d.eng)
                if b is None or d.idx > b.idx:
                    best[d.eng] = d
        op.deps.extend(keep)
        op.deps.extend(best.values())

    def add(self, eng, fn, reads=(), writes=(), extra=()):
        op = Op(eng, fn)
        op.idx = len(self.q[eng])
        self.q[eng].append(op)
        self._track(op, reads, writes)
        for d in extra:
            if d is not None and d not in op.deps:
                op.deps.append(d)
        return op

    def dma(self, eng, out, in_, reads=(), writes=(), extra=(), **kw):
        op = Op(eng, lambda e: e.dma_start(out=out, in_=in_, **kw), is_dma=True)
        op.idx = len(self.q[eng])
        self.q[eng].append(op)
        self._track(op, reads, writes)
        for d in extra:
            if d is not None and d not in op.deps:
                op.deps.append(d)
        j = self.dma_count[eng]
        self.dma_count[eng] += 1
        slot = j % N_DMA_SEMS
        ep = j // (N_DMA_SEMS * DMA_USES)
        op.dsem = (eng, ep, slot)
        op.dval = 16 * ((j % (N_DMA_SEMS * DMA_USES)) // N_DMA_SEMS + 1)
        op.prev_same_sem = self.dma_last.get((eng, slot))
        self.dma_last[(eng, slot)] = op
        return op

    def barrier(self):
        lasts = [self.q[e][-1] for e in ENGS if self.q[e]] + list(self.dma_last.values())
        for e in ENGS:
            self.add(e, None, extra=lasts)
        self.last_w = {}
        self.readers = {}

    def emit(self, block, stack):
        nc = self.nc
        csem = {}
        dsem = {}
        for e in ENGS:
            neps = (self.dma_count[e] + N_DMA_SEMS * DMA_USES - 1) // (N_DMA_SEMS * DMA_USES)
            for ep in range(neps):
                for s in range(N_DMA_SEMS):
                    dsem[(e, ep, s)] = stack.enter_context(nc.semaphore(f"d_{e}{ep}_{s}"))
        for e in ENGS:
            for op in self.q[e]:
                for d in op.deps:
                    if not d.is_dma:
                        if d.eng == op.eng and (not SAME_ENGINE_SYNC or d.eng in NOSYNC):
                            continue
                        d.signaled = True
        for e in ENGS:
            c = 0
            for op in self.q[e]:
                if op.signaled and not op.is_dma:
                    c += 1
                op.count = c
            for ep in range((c + SEM_K - 1) // SEM_K):
                csem[(e, ep)] = stack.enter_context(nc.semaphore(f"c_{e}{ep}"))
        self.nsems = len(csem) + len(dsem)
        stats = {e: [0, 0] for e in ENGS}

        def body_for(e):
            def body(eng):
                waited = {}
                for op in self.q[e]:
                    waits = {}
                    for d in op.deps:
                        if d.is_dma:
                            key = ("d",) + d.dsem
                            sem = dsem[d.dsem]
                            val = d.dval
                        else:
                            if d.eng == e and (not SAME_ENGINE_SYNC or e in NOSYNC):
                                continue
                            if d.count == 0:
                                continue
                            dep_ = (d.count - 1) // SEM_K
                            key = ("c", d.eng, dep_)
                            sem = csem[(d.eng, dep_)]
                            val = (d.count - 1) % SEM_K + 1
                        if waits.get(key, (None, 0))[1] < val:
                            waits[key] = (sem, val)
                    if op.is_dma and op.prev_same_sem is not None:
                        p = op.prev_same_sem
                        key = ("d",) + p.dsem
                        if waits.get(key, (None, 0))[1] < p.dval:
                            waits[key] = (dsem[p.dsem], p.dval)
                    need = []
                    for key, (sem, val) in waits.items():
                        if waited.get(key, 0) >= val:
                            continue
                        waited[key] = val
                        need.append((sem, val))
                    embed = None
                    if need and op.fn is not None and not op.is_dma and EMBED_WAIT:
                        embed = need.pop()
                    for sem, val in need:
                        eng.wait_ge(sem, val)
                        stats[e][1] += 1
                    mysem = csem[(e, (op.count - 1) // SEM_K)] if (op.signaled and not op.is_dma) else None
                    if op.fn is None:
                        if op.signaled:
                            eng.nop(nofuse=True).then_inc(mysem, 1)
                        continue
                    ins = op.fn(eng)
                    if embed is not None:
                        ins._wait_ge(embed[0], embed[1])
                    stats[e][0] += 1
                    if op.is_dma:
                        ins.then_inc(dsem[op.dsem], 16)
                    elif op.signaled:
                        ins.then_inc(mysem, 1)
            return body

        block.tensor(body_for("pe"))
        block.scalar(body_for("act"))
        block.vector(body_for("dve"))
        block.gpsimd(body_for("pool"))
        block.sync(body_for("sp"))
        self.stats = stats

import contextlib

EPS = 1e-6
NT_L = 16
NT = 18
W0C = 3456
W1C = 1600
NDUMMY = 0


def build(nlayers=2):
    nc = bass.Bass("TRN2", target_bir_lowering=False)

    def din(name, shape):
        return nc.dram_tensor(name, list(shape), F32, kind="ExternalInput").ap()

    x_d = din("x", [2048, 1024]); ctx_d = din("ctx", [256, 1024]); cc_d = din("cc", [128, 8, 2])
    wada_d = din("w_ada", [2, 1024, 3072]); bada_d = din("b_ada", [2, 3072]); badac_d = din("b_ada_col", [128, 2, 16])
    ng_d = din("norm_g_col", [128, 2, 8]); fg_d = din("final_g", [1, 1024])
    w0_d = din("W0", [1024, W0C]); wout0_d = din("wout0", [1024, 1024])
    sink_d = din("sink_rep", [1, 2, 512]); lng_d = din("ln_g", [1, 512]); lnb_d = din("ln_b", [1, 512])
    wst_d = din("WsT", [128, 8, 128]); bs_d = din("bs", [8, 128]); bones_d = din("blockones", [8, 512])
    mask_d = din("masks", [128, 2, 512]); cos0_d = din("cos0", [128, 2048]); sin0_d = din("sin0", [128, 2048])
    w1_d = din("W1", [1024, W1C]); qn_d = din("qn_col", [128, 2]); kvn_d = din("kvn_col", [128, 1])
    wqb_d = din("Wqb", [256, 3072]); wkvb_d = din("Wkvb", [128, 2048]); wout1_d = din("wout1", [1024, 1024])
    cos1_d = din("cos1", [128, 2048]); sin1_d = din("sin1", [128, 2048])
    if nlayers == 2:
        out_d = nc.dram_tensor("out", [2048, 1024], F32, kind="ExternalOutput").ap()
        xs_d = nc.dram_tensor("xs", [2048, 1024], F32, kind="Internal").ap()
        xcs_d = nc.dram_tensor("xcs", [256, 1024], F32, kind="Internal").ap()
    else:
        xs_d = nc.dram_tensor("xs", [2048, 1024], F32, kind="ExternalOutput").ap()
        xcs_d = nc.dram_tensor("xcs", [256, 1024], F32, kind="ExternalOutput").ap()

    S = Sched(nc)
    with contextlib.ExitStack() as st:
        def sb(name, shape, dt):
            return st.enter_context(nc.sbuf_tensor("s_" + name, list(shape), dt))

        def ps(name, shape, dt):
            return st.enter_context(nc.psum_tensor("p_" + name, list(shape), dt))

        Tb = [ps("T0", [128, 1024], BF16), ps("T1", [128, 1024], BF16)]
        PP = ps("PP", [128, 1024], F32)
        Pb = [PP[:, 0:512], PP[:, 512:1024]]
        Ab = ps("A", [128, 1024], F32)
        Ob = ps("O", [128, 1024], F32)
        pctr = [0]

        def pnext():
            i = pctr[0] % 2
            pctr[0] += 1
            return Pb[i], ("P", i)


        def MM(out, lhsT, rhs, start, stop, reads, writes):
            return S.add("pe", lambda e: e.matmul(out, lhsT=lhsT, rhs=rhs, start=start, stop=stop), reads, writes)

        def TR(out, in_, reads, writes):
            return S.add("pe", lambda e: e.transpose(out, in_, ident[:]), list(reads) + ["ident"], writes)

        def ACT(out, in_, func, reads, writes, **kw):
            return S.add("act", lambda e: e.activation(out=out, in_=in_, func=func, **kw), reads, writes)

        def TT(eng, out, in0, in1, op, reads, writes):
            return S.add(eng, lambda e: e.tensor_tensor(out=out, in0=in0, in1=in1, op=op), reads, writes)

        def TS(eng, out, in0, s1, s2, op0, op1, reads, writes):
            if s2 is None:
                return S.add(eng, lambda e: e.tensor_scalar(out=out, in0=in0, scalar1=s1, scalar2=None, op0=op0), reads, writes)
            return S.add(eng, lambda e: e.tensor_scalar(out=out, in0=in0, scalar1=s1, scalar2=s2, op0=op0, op1=op1), reads, writes)

        def STT(eng, out, in0, scalar, in1, op0, op1, reads, writes):
            return S.add(eng, lambda e: e.scalar_tensor_tensor(out=out, in0=in0, scalar=scalar, in1=in1, op0=op0, op1=op1), reads, writes)

        def CP(eng, out, in_, reads, writes):
            return S.add(eng, lambda e: e.tensor_copy(out=out, in_=in_), reads, writes)

        def RCP(out, in_, reads, writes):
            return S.add("dve", lambda e: e.reciprocal(out=out, in_=in_), reads, writes)

        def MSET(ap, val, reads, writes):
            return S.add("pool", lambda e: e.memset(ap, val), reads, writes)

        ident = sb("ident", [128, 128], BF16)
        ones_bf = sb("ones_bf", [128, 128], BF16)
        negh = sb("negh", [128, 1], F32)
        cc = sb("cc", [128, 8, 2], F32)
        cond = sb("cond", [128, 8, 2], BF16)
        cond_bc = sb("cond_bc", [128, 2, 8, 128], BF16)
        badac = sb("badac", [128, 2, 16], F32)
        ngc = sb("ngc", [128, 2, 8], F32)
        modc = sb("modc", [128, 16, 2], F32)
        gmodL = [sb("gmod0", [128, 8, 2], F32), sb("gmod1", [128, 8, 2], F32)]
        shiftL = [sb("shiftc0", [128, 8, 2], F32), sb("shiftc1", [128, 8, 2], F32)]
        gate_bc = sb("gate_bc", [128, 2, 1024], F32)
        bg_bc = sb("bg_bc", [128, 1024], F32)
        wada = [sb("wada0", [128, 8, 512], BF16)] * 2
        NXR = 3
        xring = [sb(f"xr{i}", [128, 1024], F32) for i in range(NXR)]
        xctr = [0]
        junk = sb("junk", [128, 1024], BF16)
        ss = sb("ss", [128, 4], F32)
        std = sb("std", [128, 4], F32)
        rstd = sb("rstd", [128, 4], F32)
        xn = sb("xn", [128, 2, 1024], BF16)
        hT = [sb("hT0", [128, 8, 256], BF16)] * 2
        tmpA = sb("tmpA", [128, 1024], F32)
        tmpB = sb("tmpB", [128, 1024], F32)
        xnew = [sb("xnew0", [128, 1024], F32)] * 2
        recb = sb("recb", [128, 2, 512], F32)
        rec2 = recb
        pTw = [sb(f"pTw{i}", [128, 1024], BF16) for i in range(2)]
        pT = [pTw[0][:, 0:512], pTw[0][:, 512:1024], pTw[1][:, 0:512], pTw[1][:, 512:1024]]
        pTc = [0]
        cosT = sb("cosT", [128, 2048], BF16)
        sinT = sb("sinT", [128, 2048], BF16)

        MSET(ident[:], 1.0, [], ["ident"])
        S.add("pool", lambda e: e.affine_select(out=ident[:], in_=ident[:], pattern=[[-1, 128]],
                                                compare_op=ALU.is_equal, fill=0.0, base=0, channel_multiplier=1),
              reads=["ident"], writes=["ident"])
        MSET(ones_bf[:], 1.0, [], ["ones_bf"])
        MSET(negh[:], -0.5, [], ["negh"])
        S.dma("sp", cc[:], cc_d, writes=["cc"])
        S.dma("sp", badac[:], badac_d, writes=["badac"])
        S.dma("sp", ngc[:], ng_d, writes=["ngc"])
        ACT(cond[:], cc[:], AF.Silu, ["cc"], ["cond"])
        for j in range(2):
            for kc in range(8):
                CP("dve", cond_bc[:, j, kc, :], cond[:, kc, j:j + 1].to_broadcast([128, 128]), ["cond"], [("cond_bc", j)])

        def adaln_pieces(L, need_gate_c, gate_slot):
            wv = wada_d[L].rearrange("(kc p) n -> p kc n", p=128)
            wt = wada[0]
            wk = ("wada", 0)
            gm, sh = gmodL[L], shiftL[L]
            pieces = []
            for blk in range(6):
                def dma_piece(blk=blk):
                    S.dma("pool", wt[:], wv[:, :, blk * 512:(blk + 1) * 512], writes=[wk])
                    if blk == 4:
                        S.dma("sp", bg_bc[:], bada_d[L:L + 1, 2048:3072].partition_broadcast(128), writes=["bg_bc"])

                def mm_piece(blk=blk):
                    if blk < 4:
                        pt, pk = pnext()
                        for mm_ in range(4):
                            for kc in range(8):
                                MM(pt[:, mm_ * 2:mm_ * 2 + 2], wt[:, kc, mm_ * 128:(mm_ + 1) * 128], cond[:, kc, :], kc == 0, kc == 7,
                                   [wk, "cond"], [pk])
                        pv = pt[:, 0:8].rearrange("p (m j) -> p m j", j=2)
                        for j in range(2):
                            TT("dve", modc[:, blk * 4:blk * 4 + 4, j], pv[:, :, j], badac[:, L, blk * 4:blk * 4 + 4], ALU.add,
                               [pk, "badac"], [("modc", blk, j)])
                        if blk == 3:
                            for j in range(2):
                                mk = [("modc", b_, j) for b_ in range(4)]
                                TS("dve", gm[:, :, j], modc[:, 8:16, j], 1.0, None, ALU.add, None, mk, [("gmod", L, j)])
                                TT("dve", gm[:, :, j], gm[:, :, j], ngc[:, L, :], ALU.mult, [("gmod", L, j), "ngc"], [("gmod", L, j)])
                                CP("dve", sh[:, :, j], modc[:, 0:8, j], mk, [("shiftc", L, j)])
                    else:
                        n = blk - 4
                        for j in range(2 if need_gate_c else 1):
                            pt, pk = pnext()
                            for kc in range(8):
                                MM(pt[:, :], cond_bc[:, j, kc, :], wt[:, kc, :], kc == 0, kc == 7, [wk, ("cond_bc", j)], [pk])
                            TT("dve", gate_bc[:, gate_slot[j], n * 512:(n + 1) * 512], pt[:, :], bg_bc[:, n * 512:(n + 1) * 512], ALU.add,
                               [pk, "bg_bc"], [("gate_bc", gate_slot[j], n)])
                pieces.append((dma_piece, mm_piece))
            return pieces

        normed = set()

        preloaded = {}

        def preload_x(tag, srcs, src_keys, queue="act"):
            if tag in preloaded or tag in normed:
                return
            xts = []
            for src, sk in zip(srcs, src_keys):
                xi = xctr[0] % NXR
                xctr[0] += 1
                xt = xring[xi]
                xk = ("xr", xi)
                xts.append((xt, xk))
                S.dma(queue, xt[:], src, reads=[sk], writes=[xk])
            preloaded[tag] = xts

        def ensure_norm(tag, srcs, src_keys):
            if tag in normed:
                return
            normed.add(tag)
            stage1_norm(srcs, src_keys, preloaded.pop(tag, None))

        def stage1(srcs, src_keys, j, hTt, hkey, L=0, tag=None):
            ensure_norm(tag if tag is not None else ("anon", len(normed)), srcs, src_keys)
            stage1_tr(len(srcs), j, hTt, hkey, L)

        def stage1_norm(srcs, src_keys, pre=None):
            nt = len(srcs)
            xts = []
            for i, (src, sk) in enumerate(zip(srcs, src_keys)):
                if pre is not None:
                    xt, xk = pre[i]
                else:
                    xi = xctr[0] % NXR
                    xctr[0] += 1
                    xt = xring[xi]
                    xk = ("xr", xi)
                    S.dma("sp", xt[:], src, reads=[sk], writes=[xk])
                xts.append((xt, xk))
                ACT(junk[:], xt[:], AF.Square, [xk], ["junk", ("ss", i)], accum_out=ss[:, i:i + 1])
            ACT(std[:, 0:nt], ss[:, 0:nt], AF.Sqrt, [("ss", i) for i in range(nt)], ["std"], scale=1.0 / 1024, bias=EPS)
            RCP(rstd[:, 0:nt], std[:, 0:nt], ["std"], ["rstd"])
            for i, (xt, xk) in enumerate(xts):
                TS("dve", xn[:, i, :], xt[:], rstd[:, i:i + 1], None, ALU.mult, None, [xk, "rstd"], [("xn", i)])

        def stage1_tr(nt, j, hTt, hkey, L):
            for pr in range(4):
                tb = Tb[pr % 2]
                tk = ("T", pr % 2)
                for kc in (2 * pr, 2 * pr + 1):
                    off = (kc % 2) * 512
                    for i in range(nt):
                        TR(tb[:, off + i * 128: off + (i + 1) * 128], xn[:, i, kc * 128:(kc + 1) * 128], [("xn", i)], [tk])
                for kc in (2 * pr, 2 * pr + 1):
                    off = (kc % 2) * 512
                    TS("dve", hTt[:, kc, 0:nt * 128], tb[:, off:off + nt * 128], gmodL[L][:, kc, j:j + 1], shiftL[L][:, kc, j:j + 1], ALU.mult, ALU.add,
                       [tk, ("gmod", L, j), ("shiftc", L, j)], [hkey])

        def proj_fm(hTt, hkey, tok0, ntok, wt, wkey, col0, M, nkc=8):
            pt, pk = pnext()
            for kc in range(nkc):
                MM(pt[0:M, 0:ntok], wt[:, kc, col0:col0 + M], hTt[:, kc, tok0:tok0 + ntok], kc == 0, kc == nkc - 1, flat([hkey, wkey]), [pk])
            return pt, pk

        def flat(l):
            o = []
            for a in l:
                if isinstance(a, list):
                    o.extend(a)
                else:
                    o.append(a)
            return o

        octr = [0]

        def outproj_residual(yT, ykey, tok0, wo, wokey, xsrc, xsrc_key, j, dst, dst_key, final=False, alt=None):
            for n in range(2):
                for kc in range(8):
                    MM(Ob[:, n * 512:(n + 1) * 512], yT[:, kc, tok0:tok0 + 128], wo[:, kc, n * 512:(n + 1) * 512], kc == 0, kc == 7,
                       list(ykey) + [wokey], [("O", n)])
            xi = xctr[0] % NXR
            xctr[0] += 1
            xt = xring[xi]
            xk = ("xr", xi)
            S.dma("sp", xt[:], xsrc, reads=[xsrc_key], writes=[xk])
            if alt is None or alt[5] is None:
                tA, tAk = tmpA[:], ["tmpA", ("tmpo", 0), ("tmpo", 1)]
            else:
                tA, tAk = alt[5], alt[6]
            TT("dve", tA, Ob[:], gate_bc[:, j, :], ALU.mult, [("O", 0), ("O", 1), ("gate_bc", j, 0), ("gate_bc", j, 1)], tAk)
            xo = xnew[octr[0] % 2]
            xok = ("xnew", 0)
            octr[0] += 1
            if not final:
                TT("pool", xo[:], tA, xt[:], ALU.add, tAk + [xk], [xok])
                return S.dma("sp", dst, xo[:], reads=[xok], writes=[dst_key])
            if alt is None:
                tB, tBk, xo2, xo2k, c = tmpB[:], ["tmpB", "r1", "r2"], xo[:], [xok], 0
            else:
                tB, tBk, xo2, xo2k, c = alt[0:5]
            TT("pool", tB, tA, xt[:], ALU.add, tAk + [xk], tBk)
            ACT(junk[:], tB, AF.Square, tBk, ["junk", ("ss", c)], accum_out=ss[:, c:c + 1])
            ACT(std[:, c:c + 1], ss[:, c:c + 1], AF.Sqrt, [("ss", c)], [("std", c)], scale=1.0 / 1024, bias=EPS)
            RCP(rstd[:, c:c + 1], std[:, c:c + 1], [("std", c)], [("rstd", c)])
            STT("dve", xo2, tB, rstd[:, c:c + 1], bg_bc[:], ALU.mult, ALU.mult, tBk + [("rstd", c), "bg_bc"], xo2k)
            return S.dma("sp", dst, xo2, reads=xo2k, writes=[dst_key])

        actr = [0]
        pend_e2 = []

        def flush_e2():
            while pend_e2:
                pend_e2.pop(0)()

        def attn_core(q_ap, qkey, kts, A_idx_unused, ones_first, sink_mm, scale, o_ap, g_ap, t_view, gkey, ykey):
            A_idx = actr[0] % 2
            actr[0] += 1
            Aacc = Ab[:, A_idx * 512:(A_idx + 1) * 512]
            ak = ("A", A_idx)
            n = len(kts)

            def score(idx):
                kap, kkey, mask = kts[idx][0], kts[idx][1], kts[idx][4]
                pt, pk = pnext()
                MM(pt[:, :], kap, q_ap, True, mask is None, flat([kkey, qkey]), [pk])
                if mask is not None:
                    MM(pt[:, :], ident[:, :], mask, False, True, ["ident", "masks"], [pk])
                return pt, pk
            nxt = score(0)
            for idx, (kap, kkey, vap, vkey, mask) in enumerate(kts):
                pt, pk = nxt
                if idx + 1 < n:
                    nxt = score(idx + 1)
                pi = pTc[0] % 4
                pTc[0] += 1
                pt_sb = pT[pi]
                ptk = ("pT", pi)
                ACT(pt_sb, pt[:, :], AF.Exp, [pk], [ptk], scale=scale)
                last = (idx == n - 1) and sink_mm is None
                MM(Aacc, vap, pt_sb, idx == 0, last, flat([vkey, ptk]), [ak])
            if sink_mm is not None:
                MM(Aacc, sink_mm[0], sink_mm[1], False, True, ["sel", "esink"], [ak])
            oh = slice(64, 128) if ones_first else slice(0, 64)
            dh = slice(0, 64) if ones_first else slice(64, 128)
            flush_e2()
            RCP(recb[dh, A_idx, :], Aacc[dh, :], [ak], [("recb", A_idx)])
            S.dma("sp", rec2[oh, A_idx, :], recb[dh, A_idx, :], reads=[("recb", A_idx)], writes=[("rec2", A_idx)])

            def e2():
                TT("dve", tmpA[oh, A_idx * 512:(A_idx + 1) * 512], Aacc[oh, :], rec2[oh, A_idx, :], ALU.mult, [ak, ("rec2", A_idx)], [("tmpo", A_idx)])
                tv = tmpA[oh, A_idx * 512:(A_idx + 1) * 512]
                if t_view is not None:
                    tv = tv.rearrange("p (c t) -> p c t", c=4)
                TT("pool", o_ap, tv, g_ap, ALU.mult, flat([("tmpo", A_idx), gkey]), [ykey])
            pend_e2.append(e2)

        pair_banks = [(PP, [("P", 0), ("P", 1)]), (Ob, [("O", 0), ("O", 1)])]

        def attn_dual(chains, scale, acc=None):
            n = len(chains[0]["kts"])
            st_ = []
            for c, ch in enumerate(chains):
                A_idx = c
                a_ap, a_key = (Ab[:, A_idx * 512:(A_idx + 1) * 512], ("A", A_idx)) if acc is None else acc[c]
                st_.append({"A": a_ap, "ak": a_key, "A_idx": A_idx, "banks": pair_banks[c], "nxt": None})

            def score(c, idx):
                ch = chains[c]
                kap, kkey, mask = ch["kts"][idx][0], ch["kts"][idx][1], ch["kts"][idx][4]
                bt, bk = st_[c]["banks"]
                h_ = idx % 2
                pt, pk = bt[:, h_ * 512:(h_ + 1) * 512], bk[h_]
                MM(pt, kap, ch["q_ap"], True, mask is None, flat([kkey, ch["qkey"]]), [pk])
                if mask is not None:
                    MM(pt, ident[:, :], mask, False, True, ["ident", "masks"], [pk])
                return pt, pk
            for c in range(2):
                st_[c]["nxt"] = score(c, 0)
            for idx in range(n):
                cur = [st_[c]["nxt"] for c in range(2)]
                if idx + 1 < n:
                    for c in range(2):
                        st_[c]["nxt"] = score(c, idx + 1)
                for c in range(2):
                    ch = chains[c]
                    pt, pk = cur[c]
                    vap, vkey = ch["kts"][idx][2], ch["kts"][idx][3]
                    pi = pTc[0] % 4
                    pTc[0] += 1
                    pt_sb = pT[pi]
                    ptk = ("pT", pi)
                    ACT(pt_sb, pt, AF.Exp, [pk], [ptk], scale=scale)
                    MM(st_[c]["A"], vap, pt_sb, idx == 0, False, flat([vkey, ptk]), [st_[c]["ak"]])
            for c in range(2):
                ch = chains[c]
                MM(st_[c]["A"], ch["sink_mm"][0], ch["sink_mm"][1], False, True, ["sel", "esink"], [st_[c]["ak"]])
            flush_e2()
            for c in range(2):
                ch = chains[c]
                A_idx, Aacc, ak = st_[c]["A_idx"], st_[c]["A"], st_[c]["ak"]
                oh = slice(64, 128) if ch["ones_first"] else slice(0, 64)
                dh = slice(0, 64) if ch["ones_first"] else slice(64, 128)
                RCP(recb[dh, A_idx, :], Aacc[dh, :], [ak], [("recb", A_idx)])
                S.dma("sp", rec2[oh, A_idx, :], recb[dh, A_idx, :], reads=[("recb", A_idx)], writes=[("rec2", A_idx)])

                def e2(A_idx=A_idx, Aacc=Aacc, ak=ak, oh=oh, ch=ch):
                    TT("dve", tmpA[oh, A_idx * 512:(A_idx + 1) * 512], Aacc[oh, :], rec2[oh, A_idx, :], ALU.mult, [ak, ("rec2", A_idx)], [("tmpo", A_idx)])
                    tv = tmpA[oh, A_idx * 512:(A_idx + 1) * 512].rearrange("p (c t) -> p c t", c=4)
                    TT("pool", ch["o_ap"], tv, ch["g_ap"], ALU.mult, flat([("tmpo", A_idx), ch["gkey"]]), [ch["ykey"]])
                pend_e2.append(e2)

        def score_pair_of(kts, q_ap, qkey, j):
            bt, bk = pair_banks[pairc[0] % 2]
            pairc[0] += 1
            for t in range(2):
                kap, kkey = kts[2 * j + t][0], kts[2 * j + t][1]
                MM(bt[:, t * 512:(t + 1) * 512], kap, q_ap, True, True, flat([kkey, qkey]), [bk[t]])
            return bt, bk

        def attn_pairs(q_ap, qkey, kts, ones_first, scale, o_ap, g_ap, gkey, ykey, side=(), warm=None, pre=None, next_first=None):
            A_idx = actr[0] % 2
            actr[0] += 1
            Aacc = Ab[:, A_idx * 512:(A_idx + 1) * 512]
            ak = ("A", A_idx)
            n = len(kts)
            assert n % 2 == 0
            npair = n // 2
            def score_pair(j):
                return score_pair_of(kts, q_ap, qkey, j)
            nxt = pre if pre is not None else score_pair(0)
            handed = None
            for j in range(npair):
                bt, bk = nxt
                if j + 1 < npair:
                    nxt = score_pair(j + 1)
                elif next_first is not None:
                    handed = next_first()
                pi = pwc[0] % 2
                pwc[0] += 1
                pw = pTw[pi]
                pwk = [("pT", 2 * pi), ("pT", 2 * pi + 1)]
                ACT(pw[:], bt[:], AF.Exp, bk, pwk, scale=scale)
                for t in range(2):
                    idx = 2 * j + t
                    vap, vkey = kts[idx][2], kts[idx][3]
                    MM(Aacc, vap, pw[:, t * 512:(t + 1) * 512], idx == 0, idx == n - 1, flat([vkey, pwk[t]]), [ak])
                if j == 6:
                    flush_e2()
                ns = len(side)
                per = -(-ns // max(1, npair - 4))
                if j >= 2:
                    for pc_ in side[(j - 2) * per:(j - 1) * per]:
                        pc_()
                if warm is not None:
                    warm()
            oh = slice(64, 128) if ones_first else slice(0, 64)
            dh = slice(0, 64) if ones_first else slice(64, 128)
            RCP(recb[dh, A_idx, :], Aacc[dh, :], [ak], [("recb", A_idx)])
            S.dma("sp", rec2[oh, A_idx, :], recb[dh, A_idx, :], reads=[("recb", A_idx)], writes=[("rec2", A_idx)])

            def e2():
                TT("dve", tmpA[oh, A_idx * 512:(A_idx + 1) * 512], Aacc[oh, :], rec2[oh, A_idx, :], ALU.mult, [ak, ("rec2", A_idx)], [("tmpo", A_idx)])
                TT("pool", o_ap, tmpA[oh, A_idx * 512:(A_idx + 1) * 512], g_ap, ALU.mult, flat([("tmpo", A_idx), gkey]), [ykey])
            pend_e2.append(e2)
            return handed

        pairc = [0]
        pwc = [0]
        outs = []
        with contextlib.ExitStack() as st0:
            def sb0(name, shape, dt):
                return st0.enter_context(nc.sbuf_tensor("s0_" + name, list(shape), dt))
            w0 = sb0("w0", [128, 8, W0C], BF16)
            wo0 = sb0("wo0", [128, 8, 1024], BF16)
            wst = sb0("wst", [128, 8, 128], BF16)
            bs_t = sb0("bs_t", [8, 128], BF16)
            bones = sb0("bones", [8, 512], BF16)
            masks = sb0("masks", [128, 2, 512], BF16)
            lng = sb0("lng", [128, 512], F32)
            lnb = sb0("lnb", [128, 512], F32)
            sink_t = sb0("sink_t", [1, 2, 512], F32)
            esink = sb0("esink", [1, 2, 512], BF16)
            sel = sb0("sel", [1, 2, 128], BF16)
            kT = sb0("kT", [128, NT * 128], BF16)
            vaug = sb0("vaug", [128, NT, 2, 128], BF16)
            qT = [sb0("qT0", [128, 4, 256], BF16), sb0("qT1", [128, 4, 256], BF16)]
            sga = [sb0("sga0", [128, 4, 256], BF16), sb0("sga1", [128, 4, 256], BF16)]
            yT = [sb0("yT0", [128, 8, 256], BF16), sb0("yT1", [128, 8, 256], BF16)]
            gu2 = [sb0("gu", [128, 512], BF16), sb0("gu_b", [128, 512], BF16)]
            sgb2 = [sb0("sgb", [128, 512], BF16), sb0("sgb_b", [128, 512], BF16)]
            ugb2 = [sb0("ugb", [128, 512], BF16), sb0("ugb_b", [128, 512], BF16)]
            gv2 = [sb0("gv", [128, 512], F32), sb0("gv_b", [128, 512], F32)]
            vn2 = [sb0("vn", [128, 512], BF16), sb0("vn_b", [128, 512], BF16)]
            bgt = sb0("bgt", [128, 512], BF16)
            st42 = [sb0("st4", [128, 8], F32), sb0("st4_b", [128, 8], F32)]
            r1 = sb0("r1", [128, 256], F32)
            r2 = sb0("r2", [128, 256], F32)

            w0v = w0_d.rearrange("(kc p) n -> p kc n", p=128)
            for (c0, c1) in [(0, 1280), (1280, 2432), (2432, W0C)]:
                S.dma("pool", w0[:, :, c0:c1], w0v[:, :, c0:c1], writes=[("w0", c0)])
            S.add("pool", None, reads=[("w0", 0), ("w0", 1280), ("w0", 2432)], writes=["w0"])
            for dp_, mp_ in adaln_pieces(0, True, (0, 1)):
                dp_()
                mp_()
            S.dma("pool", wo0[:], wout0_d.rearrange("(kc p) n -> p kc n", p=128), writes=["wo0"])
            S.dma("pool", wst[:], wst_d, writes=["wst"])
            S.dma("pool", bs_t[:], bs_d, writes=["bs_t"])
            S.dma("pool", bones[:], bones_d, writes=["bones"])
            S.dma("pool", masks[:], mask_d, writes=["masks"])
            S.dma("sp", lng[:], lng_d.partition_broadcast(128), writes=["lng"])
            S.dma("sp", lnb[:], lnb_d.partition_broadcast(128), writes=["lnb"])
            S.dma("sp", sink_t[:], sink_d, writes=["sink_t"])
            S.dma("pool", cosT[:], cos0_d, writes=["cosT"])
            S.dma("pool", sinT[:], sin0_d, writes=["sinT"])
            ACT(esink[:], sink_t[:], AF.Exp, ["sink_t"], ["esink"])
            MSET(sel[:], 0.0, [], ["sel"])
            MSET(sel[:, 0, 64:128], 1.0, ["sel"], ["sel"])
            MSET(sel[:, 1, 0:64], 1.0, ["sel"], ["sel"])
            MSET(vaug[:, :, 0, 64:128], 1.0, [], ["vaug_ones"])
            MSET(vaug[:, :, 1, 0:64], 1.0, [], ["vaug_ones"])

            import os
            STOP = int(os.environ.get("KSTOP", "0"))

            def l0_stage(k, tiles, is_ctx, part="ab"):
                par = k % 2
                nt = len(tiles)
                ntok = nt * 128
                hTt = hT[par]
                hkey = ("hT", 0)

                def tmg(i, col0, n):
                    pt, pk = pnext()
                    for kc in range(8):
                        MM(pt[:, 0:n], hTt[:, kc, i * 128:(i + 1) * 128], w0[:, kc, col0:col0 + n], kc == 0, kc == 7, [hkey, "w0"], [pk])
                    return pt, pk
                if "a" in part:
                    l0_stage_a(k, tiles, is_ctx, par, nt, ntok, hTt, hkey, tmg)
                if "b" in part:
                    sub = "".join(ch for ch in part if ch in "12") or "12"
                    l0_stage_b(k, tiles, is_ctx, par, nt, ntok, hTt, hkey, tmg, sub)

            def l0_stage_a(k, tiles, is_ctx, par, nt, ntok, hTt, hkey, tmg):
                if is_ctx:
                    srcs = [ctx_d[i * 128:(i + 1) * 128, :] for i in range(2)]
                else:
                    srcs = [x_d[t * 128:(t + 1) * 128, :] for t in tiles]
                stage1(srcs, [("xin", t) for t in tiles], 1 if is_ctx else 0, hTt, hkey, tag=("L0", k))
                if STOP in (12, 122):
                    return
                tok0 = 0 if is_ctx else tiles[0] * 128
                ktok0 = tiles[0] * 128
                for c in range(5):
                    col = c * 128 if c < 4 else 1024
                    rcol = 512 + c * 128 if c < 4 else 1152
                    dst = qT[par][:, c, 0:ntok] if c < 4 else kT[:, ktok0:ktok0 + ntok]
                    dkey = ("qT", par) if c < 4 else ("kT", k)
                    pt, pk = proj_fm(hTt, hkey, 0, ntok, w0, "w0", col, 128)
                    if is_ctx:
                        ACT(dst, pt[:, 0:ntok], AF.Copy, [pk], [dkey])
                    else:
                        TT("dve", r1[:, 0:ntok], pt[:, 0:ntok], cosT[:, tok0:tok0 + ntok], ALU.mult, [pk, "cosT"], ["r1"])
                        pt2, pk2 = proj_fm(hTt, hkey, 0, ntok, w0, "w0", rcol, 128)
                        TT("dve", r2[:, 0:ntok], pt2[:, 0:ntok], sinT[:, tok0:tok0 + ntok], ALU.mult, [pk2, "sinT"], ["r2"])
                        TT("dve", dst, r1[:, 0:ntok], r2[:, 0:ntok], ALU.add, ["r1", "r2"], [dkey])
                for c in range(4):
                    pt, pk = proj_fm(hTt, hkey, 0, ntok, w0, "w0", 1280 + c * 128, 128)
                    ACT(sga[par][:, c, 0:ntok], pt[:, 0:ntok], AF.Silu, [pk], [("sga", par)])
                for i, t in enumerate(tiles):
                    pt, pk = tmg(i, 1792, 128)
                    ACT(vaug[:, t, 0, 0:64], pt[:, 0:64], AF.Copy, [pk], [("v", t, 0)])
                    ACT(vaug[:, t, 1, 64:128], pt[:, 64:128], AF.Copy, [pk], [("v", t, 1)])

            def l0_stage_b(k, tiles, is_ctx, par, nt, ntok, hTt, hkey, tmg, sub="12"):
                if "1" in sub:
                    l0_stage_b1(k, tiles, is_ctx, par, nt, ntok, hTt, hkey, tmg)
                if "2" in sub:
                    l0_stage_b2(k, tiles, is_ctx, par, nt, ntok, hTt, hkey, tmg)

            def l0_stage_b1(k, tiles, is_ctx, par, nt, ntok, hTt, hkey, tmg):
                for i, t in enumerate(tiles):
                    gu, sgb, ugb, gv, st4 = gu2[i], sgb2[i], ugb2[i], gv2[i], st42[i]
                    pt, pk = tmg(i, 2944, 512)
                    ACT(sgb[:], pt[:, :], AF.Silu, [pk], [("sgb", i)])
                    pt, pk = tmg(i, 1920, 512)
                    ACT(gu[:], pt[:, :], AF.Gelu, [pk], [("gu", i)])
                    TT("pool", ugb[:], gu[:], sgb[:], ALU.mult, [("gu", i), ("sgb", i)], [("ugb", i)])
                    pt, pk = tmg(i, 2432, 512)
                    ACT(gv[:], pt[:, :], AF.Gelu, [pk], [("gv", i), ("st4a", i)], accum_out=st4[:, 0:1])
                for i, t in enumerate(tiles):
                    gu, sgb, ugb, gv, st4 = gu2[i], sgb2[i], ugb2[i], gv2[i], st42[i]
                    ACT(junk[:, 0:512], gv[:], AF.Square, [("gv", i)], ["junk", ("st4b", i)], accum_out=st4[:, 1:2])
                    TS("dve", st4[:, 2:4], st4[:, 0:2], 1.0 / 512, None, ALU.mult, None, [("st4a", i), ("st4b", i)], [("st4c", i)])
                    TT("dve", st4[:, 4:5], st4[:, 2:3], st4[:, 2:3], ALU.mult, [("st4c", i)], [("st4d", i)])
                    TT("dve", st4[:, 5:6], st4[:, 3:4], st4[:, 4:5], ALU.subtract, [("st4c", i), ("st4d", i)], [("st4e", i)])
                    ACT(st4[:, 6:7], st4[:, 5:6], AF.Sqrt, [("st4e", i)], [("st4f", i)], scale=1.0, bias=EPS)
                    RCP(st4[:, 7:8], st4[:, 6:7], [("st4f", i)], [("st4g", i)])
                    STT("dve", gv[:], gv[:], st4[:, 2:3], lng[:], ALU.subtract, ALU.mult, [("gv", i), ("st4c", i), "lng"], [("gv", i)])
                    STT("dve", vn2[i][:], gv[:], st4[:, 7:8], lnb[:], ALU.mult, ALU.add, [("gv", i), ("st4g", i), "lnb"], [("vn", i)])

            def l0_stage_b2(k, tiles, is_ctx, par, nt, ntok, hTt, hkey, tmg):
                for i, t in enumerate(tiles):
                    ugb, vn = ugb2[i], vn2[i]
                    pt, pk = pnext()
                    MM(pt[:, :], bs_t[:, :], bones[:, :], True, False, ["bs_t", "bones"], [pk])
                    for g in range(8):
                        MM(pt[:, g * 64:(g + 1) * 64], wst[:, g, :], vn[:, g * 64:(g + 1) * 64], False, g == 7, ["wst", ("vn", i)], [pk])
                    TT("dve", bgt[:], pt[:, :], ugb[:], ALU.mult, [pk, ("ugb", i)], ["bgt"])
                    tb = Tb[i % 2]
                    tk = ("T", i % 2)
                    for jj in range(4):
                        TR(tb[:, jj * 128:(jj + 1) * 128], bgt[:, jj * 128:(jj + 1) * 128], ["bgt"], [tk])
                    ACT(yT[par][:, 4:8, i * 128:(i + 1) * 128], tb[:, 0:512].rearrange("p (c t) -> p c t", c=4), AF.Copy, [tk], [("yTb", par, i)])

            def l0_attn(k, tiles, is_ctx, part="ao"):
                par = k % 2
                if "a" in part:
                    l0_attn_a(k, tiles, is_ctx, par)
                if "o" in part:
                    l0_attn_o(k, tiles, is_ctx, par)

            def l0_attn_o(k, tiles, is_ctx, par):
                flush_e2()
                for i, t in enumerate(tiles):
                    if is_ctx:
                        src, sk, dst, dk, j = ctx_d[i * 128:(i + 1) * 128, :], ("xin", t), xcs_d[i * 128:(i + 1) * 128, :], ("xs", t), 1
                    else:
                        src, sk, dst, dk, j = x_d[t * 128:(t + 1) * 128, :], ("xin", t), xs_d[t * 128:(t + 1) * 128, :], ("xs", t), 0
                    outs.append(outproj_residual(yT[par], [("yTa", par, i, 0), ("yTa", par, i, 1), ("yTb", par, i)], i * 128, wo0, "wo0", src, sk, j, dst, dk))

            def l0_attn_a(k, tiles, is_ctx, par):
                for i, t in enumerate(tiles):
                    if is_ctx:
                        kt_ids = [(16, None), (17, None)]
                    else:
                        kt_ids = [(16, None), (17, None)]
                        if t > 0:
                            kt_ids.append((t - 1, 0))
                        kt_ids.append((t, None))
                        if t < NT_L - 1:
                            kt_ids.append((t + 1, 1))
                    chains = []
                    for hk in range(2):
                        hs = slice(hk * 64, (hk + 1) * 64)
                        kts = []
                        for (kt, m) in kt_ids:
                            kslab = 0 if kt >= 16 else 1 + kt // 2
                            kts.append((kT[hs, kt * 128:(kt + 1) * 128], ("kT", kslab), vaug[:, kt, hk, :], [("v", kt, hk), "vaug_ones"],
                                        None if m is None else masks[:, m, :]))
                        oh = slice(64, 128) if hk == 1 else slice(0, 64)
                        chains.append(dict(q_ap=qT[par][hs, 0:4, i * 128:(i + 1) * 128], qkey=("qT", par), kts=kts, ones_first=(hk == 1),
                                           sink_mm=(sel[0:1, hk, :], esink[0:1, hk, :]),
                                           o_ap=yT[par][oh, 0:4, i * 128:(i + 1) * 128], g_ap=sga[par][oh, 0:4, i * 128:(i + 1) * 128],
                                           gkey=("sga", par), ykey=("yTa", par, i, hk)))
                    acc = None if i % 2 == 0 else [(Tb[0][:, :].bitcast(F32), ("T", 0)), (Tb[1][:, :].bitcast(F32), ("T", 1))]
                    attn_dual(chains, 0.125, acc)

            seq = [(0, [16, 17], True)] + [(1 + s, [2 * s, 2 * s + 1], False) for s in range(8)]
            def fin():
                S.barrier()
                with nc.Block() as block:
                    S.emit(block, st)
                print("stats", S.stats, flush=True)
                return nc
            if STOP == 1:
                return fin()
            l0_stage(*seq[0])
            if STOP in (2, 12, 13, 122):
                return fin()
            l0_attn(*seq[0])
            if STOP == 3:
                return fin()
            l0_stage(*seq[1])
            ada1 = adaln_pieces(1, False, (1, 1))

            def prefetch_norm0(k):
                if k > 8:
                    return
                tiles_ = seq[k][1]
                ensure_norm(("L0", k), [x_d[t * 128:(t + 1) * 128, :] for t in tiles_], [("xin", t) for t in tiles_])
            prefetch_norm0(2)
            for s in range(1, 9):
                if 2 <= s <= 7:
                    ada1[s - 2][0]()
                if s + 1 <= 8:
                    l0_stage(*seq[s + 1], part="a")
                if s + 2 <= 8:
                    tiles_p = seq[s + 2][1]
                    preload_x(("L0", s + 2), [x_d[t * 128:(t + 1) * 128, :] for t in tiles_p], [("xin", t) for t in tiles_p])
                l0_attn(*seq[s], part="a")
                prefetch_norm0(s + 2)
                flush_e2()
                if s + 1 <= 8:
                    l0_stage(*seq[s + 1], part="b1")
                l0_attn(*seq[s], part="o")
                if s + 1 <= 8:
                    l0_stage(*seq[s + 1], part="b2")
                if 2 <= s <= 7:
                    ada1[s - 2][1]()
            S.barrier()

        if nlayers == 1:
            S.add("sp", None, extra=outs)
            with nc.Block() as block:
                S.emit(block, st)
            print("stats", S.stats, flush=True)
            return nc

        with contextlib.ExitStack() as st1:
            def sb1(name, shape, dt):
                return st1.enter_context(nc.sbuf_tensor("s1_" + name, list(shape), dt))
            w1 = sb1("w1", [128, 8, W1C], BF16)
            wqb = sb1("wqb", [128, 2, 3072], BF16)
            wkvb = sb1("wkvb", [128, 2048], BF16)
            wo1 = sb1("wo1", [128, 8, 1024], BF16)
            qnc = sb1("qnc", [128, 2], F32)
            kvnc = sb1("kvnc", [128, 1], F32)
            sg = sb1("sg", [128, 8, 2048], BF16)
            qan = sb1("qan", [128, 2, 2048], BF16)
            kvn = sb1("kvn", [128, NT * 128], BF16)
            kpe = sb1("kpe", [96, NT * 128], BF16)
            kTh = [sb1("kTh0", [96, NT * 128], BF16), sb1("kTh1", [96, NT * 128], BF16)]
            vah = [sb1("vah0", [128, NT, 128], BF16), sb1("vah1", [128, NT, 128], BF16)]
            qTh = [sb1("qTh0", [96, 512], BF16), sb1("qTh1", [96, 512], BF16)]
            sq = sb1("sq", [128, 2, 256], BF16)
            qa32 = sb1("qa32", [128, 2, 256], F32)
            sdb = sb1("sdb", [128, 256], F32)
            r1 = tmpB[:, 0:512]
            r2 = tmpB[:, 512:1024]

            S.dma("pool", w1[:], w1_d.rearrange("(kc p) n -> p kc n", p=128), writes=["w1"])
            S.dma("pool", wqb[:], wqb_d.rearrange("(kc p) n -> p kc n", p=128), writes=["wqb"])
            S.dma("pool", wkvb[:], wkvb_d, writes=["wkvb"])
            S.dma("pool", wo1[:], wout1_d.rearrange("(kc p) n -> p kc n", p=128), writes=["wo1"])
            S.dma("sp", qnc[:], qn_d, writes=["qnc"])
            S.dma("sp", kvnc[:], kvn_d, writes=["kvnc"])
            S.dma("pool", cosT[:], cos1_d, writes=["cosT"])
            S.dma("pool", sinT[:], sin1_d, writes=["sinT"])
            MSET(vah[0][:, :, 64:128], 1.0, [], ["vah_ones"])
            MSET(vah[1][:, :, 0:64], 1.0, [], ["vah_ones"])

            sqk = sb1("sqk", [128, 1, 256], BF16)
            qa32k = sb1("qa32k", [128, 1, 256], F32)

            def rms_evac(pts, ntok, sqt, q32t, tag):
                for c, (pt, pk) in enumerate(pts):
                    CP("dve", q32t[:, c, 0:ntok], pt[:, 0:ntok], [pk], [("qa32", tag, c)])
                    ACT(sqt[:, c, 0:ntok], q32t[:, c, 0:ntok], AF.Square, [("qa32", tag, c)], [("sq", tag, c)])

            def rms_finish(n, ntok, sqt, q32t, tag, gcol, gkey, dsts, dkey, nfeat):
                pt2, pk2 = pnext()
                for c in range(n):
                    MM(pt2[:, 0:ntok], ones_bf[:, :], sqt[:, c, 0:ntok], c == 0, c == n - 1, ["ones_bf", ("sq", tag, c)], [pk2])
                ACT(sdb[:, 0:ntok], pt2[:, 0:ntok], AF.Sqrt, [pk2], ["sdb"], scale=1.0 / nfeat, bias=EPS)
                RCP(sdb[:, 0:ntok], sdb[:, 0:ntok], ["sdb"], ["sdb"])
                for c in range(n):
                    STT("dve", dsts[c], q32t[:, c, 0:ntok], gcol[:, c:c + 1], sdb[:, 0:ntok], ALU.mult, ALU.mult,
                        [("qa32", tag, c), "sdb", gkey], [(dkey, c)])

            def l1_stage(k, tiles, is_ctx):
                par = k % 2
                nt = len(tiles)
                ntok = nt * 128
                hTt = hT[par]
                hkey = ("hT", 0)
                if is_ctx:
                    srcs = [xcs_d[i * 128:(i + 1) * 128, :] for i in range(2)]
                else:
                    srcs = [xs_d[t * 128:(t + 1) * 128, :] for t in tiles]
                stage1(srcs, [("xs", t) for t in tiles], 1 if is_ctx else 0, hTt, hkey, L=1, tag=("L1", k))
                if k + 1 <= 8:
                    tiles_n = seq[k + 1][1]
                    ensure_norm(("L1", k + 1), [xs_d[t * 128:(t + 1) * 128, :] for t in tiles_n], [("xs", t) for t in tiles_n])
                ktok0 = tiles[0] * 128
                if not is_ctx:
                    ptsq = [proj_fm(hTt, hkey, 0, ntok, w1, "w1", c * 128, 128) for c in range(2)]
                    rms_evac(ptsq, ntok, sq, qa32, "q")
                pkv = [proj_fm(hTt, hkey, 0, ntok, w1, "w1", 256, 128)]
                rms_evac(pkv, ntok, sqk, qa32k, "k")
                pt, pk = proj_fm(hTt, hkey, 0, ntok, w1, "w1", 384, 96)
                if is_ctx:
                    ACT(kpe[64:96, ktok0:ktok0 + ntok], pt[64:96, 0:ntok], AF.Copy, [pk], [("kpe", k)])
                else:
                    TT("dve", r1[64:96, 0:ntok], pt[64:96, 0:ntok], cosT[64:96, ktok0:ktok0 + ntok], ALU.mult, [pk, "cosT"], ["r1"])
                    pt2, pk2 = proj_fm(hTt, hkey, 0, ntok, w1, "w1", 480, 96)
                    TT("dve", r2[64:96, 0:ntok], pt2[64:96, 0:ntok], sinT[64:96, ktok0:ktok0 + ntok], ALU.mult, [pk2, "sinT"], ["r2"])
                    TT("pool", kpe[64:96, ktok0:ktok0 + ntok], r1[64:96, 0:ntok], r2[64:96, 0:ntok], ALU.add, ["r1", "r2"], [("kpe", k)])

                def gates(c0, c1):
                    for c in range(c0, c1):
                        pt, pk = proj_fm(hTt, hkey, 0, ntok, w1, "w1", 576 + c * 128, 128)
                        ACT(sg[:, c, ktok0:ktok0 + ntok], pt[:, 0:ntok], AF.Silu, [pk], [("sg", c, k)])
                if not is_ctx:
                    gates(0, 3)
                    rms_finish(2, ntok, sq, qa32, "q", qnc, "qnc", [qan[:, c, ktok0:ktok0 + ntok] for c in range(2)], ("qan", k), 256)
                    gates(3, 6)
                rms_finish(1, ntok, sqk, qa32k, "k", kvnc, "kvnc", [kvn[:, ktok0:ktok0 + ntok]], ("kvn", k), 128)
                if not is_ctx:
                    gates(6, 8)

            def fin1():
                S.barrier()
                S.add("sp", None, extra=outs)
                with nc.Block() as block:
                    S.emit(block, st)
                print("stats", S.stats, flush=True)
                return nc
            if STOP == 21:
                return fin1()
            seq = [(0, [16, 17], True)] + [(1 + s, [2 * s, 2 * s + 1], False) for s in range(8)]
            for sq_ in seq:
                l1_stage(*sq_)
                if STOP == 22:
                    return fin1()
            if STOP == 23:
                return fin1()
            allk = [(("kvn", k), 0) for k in range(9)]
            allkpe = [("kpe", k) for k in range(9)]
            slabs = [(16 * 128, 256)] + [(s * 512, 512) for s in range(4)]
            L1SCALE = 96 ** -0.5
            Xb = [Tb[0][:, :].bitcast(F32), Tb[1][:, :].bitcast(F32)]
            xpc = [0]

            def xnext():
                i = xpc[0] % 2
                xpc[0] += 1
                return Xb[i], ("T", i)

            def pe_keepwarm():
                for _ in range(NDUMMY):
                    S.add("pe", lambda e: e.matmul(Xb[1][:, :], lhsT=ident[:, :], rhs=cosT[:, 0:512], start=True, stop=True), ["ident", "cosT"], [("T", 1)])

            def expand_pieces(h):
                hp = h % 2
                kt_t = kTh[hp]
                va_t = vah[hp]
                pieces = []
                for (t0, n) in slabs:
                    def pc(t0=t0, n=n):
                        pt, pk = xnext()
                        MM(pt[0:64, 0:n], wkvb[:, h * 64:(h + 1) * 64], kvn[:, t0:t0 + n], True, True, allk + ["wkvb"], [pk])
                        CP("dve", kt_t[0:64, t0:t0 + n], pt[0:64, 0:n], [pk], [("kTh", hp)])
                    pieces.append(pc)
                pieces.append(lambda: S.dma("sp", kt_t[64:96, :], kpe[64:96, :], reads=allkpe, writes=[("kThp", hp)]))
                vs = slice(0, 64) if hp == 0 else slice(64, 128)
                for g0 in range(0, NT, 8):
                    def pv(g0=g0):
                        ng = min(8, NT - g0)
                        pt, pk = xnext()
                        for u in range(ng):
                            MM(pt[:, u * 64:(u + 1) * 64], kvn[:, (g0 + u) * 128:(g0 + u + 1) * 128], wkvb[:, 1024 + h * 64:1024 + (h + 1) * 64],
                               True, True, allk + ["wkvb"], [pk])
                        CP("dve", va_t[:, g0:g0 + ng, vs], pt[:, 0:ng * 64].rearrange("p (u d) -> p u d", d=64), [pk], [("vah", hp)])
                    pieces.append(pv)
                return pieces

            def qproj_pieces(h, s):
                q0 = s * 512
                qt = qTh[s % 2]
                qk = ("qTh", s % 2)
                qank = [(("qan", 1 + 2 * s + d_), c_) for d_ in range(2) for c_ in range(2)]

                def p1():
                    pt, pk = xnext()
                    for kc in range(2):
                        MM(pt[0:96, :], wqb[:, kc, h * 192:h * 192 + 96], qan[:, kc, q0:q0 + 512], kc == 0, kc == 1, qank + ["wqb"], [pk])
                    TT("dve", r1[64:96, :], pt[64:96, :], cosT[64:96, q0:q0 + 512], ALU.mult, [pk, "cosT"], ["r1"])
                    CP("dve", qt[0:64, :], pt[0:64, :], [pk, "r1"], [(qk, "n")])

                def p2():
                    pt2, pk2 = xnext()
                    for kc in range(2):
                        MM(pt2[0:96, :], wqb[:, kc, h * 192 + 96:h * 192 + 192], qan[:, kc, q0:q0 + 512], kc == 0, kc == 1, qank + ["wqb"], [pk2])
                    TT("dve", r2[64:96, :], pt2[64:96, :], sinT[64:96, q0:q0 + 512], ALU.mult, [pk2, "sinT"], ["r2"])
                    TT("pool", qt[64:96, :], r1[64:96, :], r2[64:96, :], ALU.add, ["r1", "r2"], [(qk, "p")])
                return [p1, p2]

            def block_args(h, s):
                hp = h % 2
                qt = qTh[s % 2]
                qk = ("qTh", s % 2)
                kt_t = kTh[hp]
                va_t = vah[hp]
                kts = [(kt_t[0:96, kt * 128:(kt + 1) * 128], [("kTh", hp), ("kThp", hp)], va_t[:, kt, :], [("vah", hp), "vah_ones"], None) for kt in range(NT)]
                return kts, qt[0:96, :], [(qk, "n"), (qk, "p")]

            def attn_block(h, s, side, pre, nxt_hs):
                hp = h % 2
                q0 = s * 512
                kts, q_ap, qkey = block_args(h, s)
                oh = slice(64, 128) if hp == 1 else slice(0, 64)
                gk = [("sg", h // 2, 1 + 2 * s), ("sg", h // 2, 2 + 2 * s)]
                nf = None
                if nxt_hs is not None:
                    nkts, nq_ap, nqkey = block_args(*nxt_hs)
                    nf = lambda: score_pair_of(nkts, nq_ap, nqkey, 0)
                return attn_pairs(q_ap, qkey, kts, hp == 1, L1SCALE,
                                  sg[oh, h // 2, q0:q0 + 512], sg[oh, h // 2, q0:q0 + 512], gk, ("og", h // 2, s, hp), side=side, warm=pe_keepwarm,
                                  pre=pre, next_first=nf)

            for pc_ in expand_pieces(0) + qproj_pieces(0, 0):
                pc_()
            handed_pair = None
            for h in range(16):
                nxt_exp = expand_pieces(h + 1) if h < 15 else []
                cuts = [0, 3, 5, 7, 9]
                for s in range(4):
                    side = list(nxt_exp[cuts[s]:cuts[s + 1]])
                    if s < 3:
                        side = qproj_pieces(h, s + 1) + side
                    elif h < 15:
                        side = side + qproj_pieces(h + 1, 0)
                    nxt_hs = (h, s + 1) if s < 3 else ((h + 1, 0) if h < 15 else None)
                    handed_pair = attn_block(h, s, side, handed_pair, nxt_hs)
            if STOP == 27:
                return fin1()
            flush_e2()
            S.dma("sp", bg_bc[:], fg_d.partition_broadcast(128), writes=["bg_bc"])
            altA = vah[0][:, :, :].rearrange("p a b -> p (a b)").bitcast(F32)[:, 0:1024]
            altB = vah[1][:, :, :].rearrange("p a b -> p (a b)").bitcast(F32)[:, 0:1024]
            altX = qan[:, :, :].rearrange("p a b -> p (a b)").bitcast(F32)[:, 0:1024]
            bufA = [(tmpA[:], ["tmpA", ("tmpo", 0), ("tmpo", 1)]), (altA, ["falA"])]
            bufB = [(tmpB[:], ["tmpB", "r1", "r2"]), (altB, ["falB"])]
            bufX = [(xnew[0][:], [("xnew", 0)]), (altX, ["falX"])]
            xts = {}

            def fin_A(t):
                s_ = t // 4
                ykey = [("og", c, s_, hp_) for c in range(8) for hp_ in range(2)]
                ot, okeys = pair_banks[t % 2]
                for n in range(2):
                    for kc in range(8):
                        MM(ot[:, n * 512:(n + 1) * 512], sg[:, kc, t * 128:(t + 1) * 128], wo1[:, kc, n * 512:(n + 1) * 512], kc == 0, kc == 7,
                           ykey + ["wo1"], [okeys[n]])
                xi = xctr[0] % NXR
                xctr[0] += 1
                xt = xring[xi]
                xk = ("xr", xi)
                xts[t] = (xt, xk)
                S.dma("sp", xt[:], xs_d[t * 128:(t + 1) * 128, :], reads=[("xs", t)], writes=[xk])
                tA, tAk = bufA[t % 2]
                TT("dve", tA, ot[:], gate_bc[:, 1, :], ALU.mult, okeys + [("gate_bc", 1, 0), ("gate_bc", 1, 1)], tAk)

            def fin_B(t):
                tA, tAk = bufA[t % 2]
                tB, tBk = bufB[t % 2]
                xt, xk = xts[t]
                c = t % 2
                TT("pool", tB, tA, xt[:], ALU.add, tAk + [xk], tBk)
                ACT(junk[:], tB, AF.Square, tBk, ["junk", ("ss", c)], accum_out=ss[:, c:c + 1])
                ACT(std[:, c:c + 1], ss[:, c:c + 1], AF.Sqrt, [("ss", c)], [("std", c)], scale=1.0 / 1024, bias=EPS)

            def fin_C(t):
                tB, tBk = bufB[t % 2]
                xo2, xo2k = bufX[t % 2]
                c = t % 2
                RCP(rstd[:, c:c + 1], std[:, c:c + 1], [("std", c)], [("rstd", c)])
                STT("dve", xo2, tB, rstd[:, c:c + 1], bg_bc[:], ALU.mult, ALU.mult, tBk + [("rstd", c), "bg_bc"], xo2k)
                outs.append(S.dma("sp", out_d[t * 128:(t + 1) * 128, :], xo2, reads=xo2k, writes=[("out", t)]))

            for t in range(NT_L):
                fin_A(t)
                fin_B(t)
                if t >= 1:
                    fin_C(t - 1)
            fin_C(NT_L - 1)
        S.add("sp", None, extra=outs)
        with nc.Block() as block:
            S.emit(block, st)
        print("stats", S.stats, flush=True)
    return nc

def _prep_shared(inp):
    f = np.float32
    g = lambda k: np.asarray(inp[k], dtype=f)
    sh = {}
    sh["w_ada"] = np.ascontiguousarray(g("w_ada"))
    b_ada = g("b_ada")
    sh["b_ada"] = np.ascontiguousarray(b_ada)
    sh["b_ada_col"] = np.ascontiguousarray(b_ada[:, :2048].reshape(2, 16, 128).transpose(2, 0, 1))
    sh["norm_g_col"] = np.ascontiguousarray(g("norm_g").reshape(2, 8, 128).transpose(2, 0, 1))
    sh["final_g"] = np.ascontiguousarray(g("final_g").reshape(1, 1024))
    w = g("w_in0")[0]
    aperm = np.concatenate([np.concatenate([np.arange(j * 64, j * 64 + 64), np.arange((j + 4) * 64, (j + 4) * 64 + 64)]) for j in range(4)])
    d = np.arange(64)
    src = np.where((d % 32) < 16, d + 16, d - 16)
    rot_a = (aperm // 64) * 64 + src[aperm % 64]
    kcols = 512 + np.arange(128)
    krot = 512 + (np.arange(128) // 64) * 64 + src[np.arange(128) % 64]
    cols = np.concatenate([aperm, rot_a, kcols, krot, 1792 + aperm, 640 + np.arange(128), 768 + np.arange(512), 1280 + np.arange(512), 1792 + 512 + np.arange(512)])
    sh["W0"] = np.ascontiguousarray(w[:, cols])
    wo = g("w_out0")[0]
    sh["wout0"] = np.ascontiguousarray(wo[np.concatenate([aperm, 512 + np.arange(512)]), :])
    sink = g("sink0")[0]
    sh["sink_rep"] = np.ascontiguousarray(np.repeat(sink.reshape(2, 4, 1), 128, axis=2).reshape(1, 2, 512))
    sh["ln_g"] = np.ascontiguousarray(g("gm_ln_g").reshape(1, 512))
    sh["ln_b"] = np.ascontiguousarray(g("gm_ln_b").reshape(1, 512))
    sh["WsT"] = np.ascontiguousarray(g("gm_ws")[0].transpose(2, 0, 1))
    sh["bs"] = np.ascontiguousarray(g("gm_bs")[0])
    bo = np.zeros((8, 512), f)
    for gi in range(8):
        bo[gi, gi * 64:(gi + 1) * 64] = 1.0
    sh["blockones"] = bo
    jj = np.arange(128)[:, None]
    ii = np.arange(128)[None, :]
    NEG = -30000.0
    m = np.stack([np.tile(np.where(jj >= ii, 0.0, NEG).astype(f), (1, 4)), np.tile(np.where(jj <= ii, 0.0, NEG).astype(f), (1, 4))], axis=1)
    sh["masks"] = np.ascontiguousarray(m)
    t = np.arange(2048)
    rowp = (t // 64).astype(np.float64)
    colp = (t % 64).astype(np.float64)
    p = np.arange(128)
    dd = p % 64
    inv = 10000.0 ** (-(dd % 16).astype(np.float64) / 16)
    pos = np.where((dd < 32)[:, None], rowp[None, :], colp[None, :])
    ang = (pos.astype(f) * inv.astype(f)[:, None]).astype(f)
    sgn = np.where((dd % 32) < 16, -1.0, 1.0)[:, None]
    sh["cos0"] = np.cos(ang).astype(f)
    sh["sin0"] = (np.sin(ang) * sgn).astype(f)
    w1 = g("w_in1")[0]
    d2 = np.arange(32)
    src2 = np.where((d2 % 16) < 8, d2 + 8, d2 - 8)
    z64 = np.zeros((1024, 64), f)
    sh["W1"] = np.ascontiguousarray(np.concatenate([w1[:, 0:384], z64, w1[:, 384:416], z64, w1[:, 384 + src2], w1[:, 416:1440]], axis=1))
    sh["qn_col"] = np.ascontiguousarray(g("q_norm")[0].reshape(2, 128).T)
    sh["kvn_col"] = np.ascontiguousarray(g("kv_norm")[0].reshape(1, 128).T)
    wq = g("w_qb")[0].reshape(256, 16, 96)
    z = np.zeros((256, 16, 64), f)
    sh["Wqb"] = np.ascontiguousarray(np.concatenate([wq, z, wq[:, :, 64 + src2]], axis=2).reshape(256, 3072))
    wk = g("w_kvb")[0].reshape(128, 16, 128)
    sh["Wkvb"] = np.ascontiguousarray(np.concatenate([wk[:, :, :64].reshape(128, 1024), wk[:, :, 64:].reshape(128, 1024)], axis=1))
    sh["wout1"] = np.ascontiguousarray(g("w_out1")[0])
    c1 = np.zeros((128, 2048), f)
    s1 = np.zeros((128, 2048), f)
    inv2 = 10000.0 ** (-(d2 % 8).astype(np.float64) / 8)
    pos2 = np.where((d2 < 16)[:, None], rowp[None, :], colp[None, :])
    ang2 = (pos2.astype(f) * inv2.astype(f)[:, None]).astype(f)
    sgn2 = np.where((d2 % 16) < 8, -1.0, 1.0)[:, None]
    c1[64:96] = np.cos(ang2)
    s1[64:96] = np.sin(ang2) * sgn2
    sh["cos1"] = c1
    sh["sin1"] = s1
    return sh


def _prep_core(inp, b):
    f = np.float32
    d = {}
    d["x"] = np.ascontiguousarray(np.asarray(inp["x"][b], dtype=f))
    d["ctx"] = np.ascontiguousarray(np.asarray(inp["ctx"][b], dtype=f))
    c = np.asarray(inp["c"][b], dtype=f).reshape(8, 128).T
    cx = np.asarray(inp["c_ctx"], dtype=f).reshape(8, 128).T
    d["cc"] = np.ascontiguousarray(np.stack([c, cx], axis=2))
    return d


_NC_CACHE = {}


def kernel(**inputs):
    sh = _prep_shared(inputs)
    if 2 not in _NC_CACHE:
        _NC_CACHE[2] = build(2)
    nc = _NC_CACHE[2]
    in_maps = []
    for b in range(8):
        m = dict(sh)
        m.update(_prep_core(inputs, b))
        in_maps.append(m)
    res = run_bass_kernel_spmd(nc, in_maps, core_ids=list(range(8)))
    return np.stack([np.asarray(r["out"], dtype=np.float32) for r in res.results], axis=0)
```
